# Optimizing a Trainium2 kernel written in Bass

```python
import math
import jax, jax.numpy as jnp
from jax import lax
import numpy as np

D_MODEL = 1024
BATCH = 4
SEQ = 4096
DEPTH = 2

N_HEADS_A = 4
HEAD_DIM_A = 64
W_A = N_HEADS_A * 2 * HEAD_DIM_A
N_HEADS_B = 4
HEAD_DIM_B = D_MODEL // 8
W_B = N_HEADS_B * HEAD_DIM_B
CONV_W = 4
Q_BLOCK = 128
MLSTM_CHUNK = 64
EPS = 1e-6
N_IN = 4 * W_A + 5 * W_B + 2 * N_HEADS_B + 2 * D_MODEL

kernel_name = "hybrid_diffattn_mlstm_gated_merge"


def _rmsnorm(x, g):
    xf = x.astype(jnp.float32)
    y = xf * lax.rsqrt(jnp.mean(xf * xf, axis=-1, keepdims=True) + EPS)
    return (y * g.astype(jnp.float32)).astype(x.dtype)


def _split(t, sizes):
    idx = np.cumsum(sizes)[:-1].tolist()
    return jnp.split(t, idx, axis=-1)


def _causal_conv(x, w):
    S = x.shape[1]
    xp = jnp.pad(x, ((0, 0), (CONV_W - 1, 0), (0, 0)))
    y = xp[:, 0:S] * w[0]
    for k in range(1, CONV_W):
        y = y + xp[:, k:k + S] * w[k]
    return y


def _alibi_slopes(n_heads):
    return 2.0 ** (-8.0 * jnp.arange(1, n_heads + 1, dtype=jnp.float32) / n_heads)


def _diff_attention(q, k, v, lam):
    B, S, _ = q.shape
    H, d = N_HEADS_A, HEAD_DIM_A
    qh = q.reshape(B, S, H, 2, d).transpose(0, 2, 3, 1, 4)
    kh = k.reshape(B, S, H, 2, d).transpose(0, 2, 3, 1, 4)
    vh = v.reshape(B, S, H, 2 * d).transpose(0, 2, 1, 3)
    nb = S // Q_BLOCK
    q_blocks = qh.reshape(B, H, 2, nb, Q_BLOCK, d).transpose(3, 0, 1, 2, 4, 5)
    slopes = _alibi_slopes(H)
    s_pos = jnp.arange(S)
    scale = d ** -0.5

    def block(args):
        q_blk, i = args
        t_pos = i * Q_BLOCK + jnp.arange(Q_BLOCK)
        dist = (t_pos[:, None] - s_pos[None, :]).astype(jnp.float32)
        bias = jnp.where(dist >= 0, -slopes[:, None, None] * dist, -jnp.inf)
        scores = jnp.einsum('bhcqd,bhcsd->bhcqs', q_blk, kh,
                            preferred_element_type=jnp.float32) * scale + bias[None, :, None]
        p = jax.nn.softmax(scores, axis=-1)
        p = p[:, :, 0] - lam * p[:, :, 1]
        return jnp.einsum('bhqs,bhse->bhqe', p.astype(vh.dtype), vh)

    out = lax.map(block, (q_blocks, jnp.arange(nb)))
    return out.transpose(1, 0, 3, 2, 4).reshape(B, S, H, 2 * d)


def _mlstm(q, k, v, ig, fg):
    B, S, _ = q.shape
    H, d, L = N_HEADS_B, HEAD_DIM_B, MLSTM_CHUNK
    nc = S // L

    def heads(t):
        return t.astype(jnp.float32).reshape(B, nc, L, H, d).transpose(1, 0, 3, 2, 4)

    def gates(t):
        return t.astype(jnp.float32).reshape(B, nc, L, H).transpose(1, 0, 3, 2)

    qc, kc, vc = heads(q), heads(k) * (d ** -0.5), heads(v)
    igc = gates(ig)
    lfc = jax.nn.log_sigmoid(gates(fg))
    causal = jnp.tril(jnp.ones((L, L), dtype=bool))

    def body(carry, xs):
        C, n, m = carry
        qj, kj, vj, ij, lj = xs
        b = jnp.cumsum(lj, axis=-1)
        logD = b[..., :, None] - b[..., None, :] + ij[..., None, :]
        logD = jnp.where(causal, logD, -jnp.inf)
        inter = b + m[..., None]
        m_row = jnp.maximum(inter, jnp.max(logD, axis=-1))
        Dm = jnp.exp(logD - m_row[..., None])
        w_inter = jnp.exp(inter - m_row)
        Sm = jnp.einsum('bhjd,bhsd->bhjs', qj, kj) * Dm
        num = jnp.einsum('bhjs,bhse->bhje', Sm, vj) + w_inter[..., None] * jnp.einsum('bhjd,bhde->bhje', qj, C)
        den = jnp.sum(Sm, axis=-1) + w_inter * jnp.einsum('bhjd,bhd->bhj', qj, n)
        h = num / jnp.maximum(jnp.abs(den), jnp.exp(-m_row))[..., None]
        bL = b[..., -1]
        a = bL[..., None] - b + ij
        m_new = jnp.maximum(bL + m, jnp.max(a, axis=-1))
        decay = jnp.exp(bL + m - m_new)
        wk = jnp.exp(a - m_new[..., None])
        C_new = decay[..., None, None] * C + jnp.einsum('bhs,bhsd,bhse->bhde', wk, kj, vj)
        n_new = decay[..., None] * n + jnp.einsum('bhs,bhsd->bhd', wk, kj)
        return (C_new, n_new, m_new), h

    init = (jnp.zeros((B, H, d, d), jnp.float32), jnp.zeros((B, H, d), jnp.float32),
            jnp.zeros((B, H), jnp.float32))
    _, hs = lax.scan(body, init, (qc, kc, vc, igc, lfc))
    return hs.transpose(1, 0, 3, 2, 4).reshape(B, S, H, d)


def setup_inputs(seed: int = 0) -> dict:
    key = jax.random.key(seed)
    ks = jax.random.split(key, 14)
    nrm = jax.random.normal
    x = nrm(ks[0], (BATCH, SEQ, D_MODEL), jnp.float32)
    norm_pre = 1.0 + 0.05 * nrm(ks[1], (DEPTH, D_MODEL), jnp.float32)
    norm_post = 1.0 + 0.05 * nrm(ks[2], (DEPTH, D_MODEL), jnp.float32)
    w_in = nrm(ks[3], (DEPTH, D_MODEL, N_IN), jnp.float32) * D_MODEL ** -0.5
    b_i = 0.1 * nrm(ks[4], (DEPTH, N_HEADS_B), jnp.float32)
    b_f = jnp.linspace(3.0, 6.0, N_HEADS_B, dtype=jnp.float32)[None, :] + 0.01 * nrm(ks[5], (DEPTH, N_HEADS_B), jnp.float32)
    b_if = jnp.concatenate([b_i, b_f], axis=-1)
    conv_qk = nrm(ks[6], (DEPTH, CONV_W, 2 * W_B), jnp.float32) * CONV_W ** -0.5
    lambda_qk = 0.1 * nrm(ks[7], (DEPTH, 4, HEAD_DIM_A), jnp.float32)
    norm_a = 1.0 + 0.05 * nrm(ks[8], (DEPTH, W_A), jnp.float32)
    norm_b = 1.0 + 0.05 * nrm(ks[9], (DEPTH, W_B), jnp.float32)
    w_a = nrm(ks[10], (DEPTH, W_A, D_MODEL), jnp.float32) * W_A ** -0.5
    w_b = nrm(ks[11], (DEPTH, W_B, D_MODEL), jnp.float32) * W_B ** -0.5
    w_out = nrm(ks[12], (DEPTH, D_MODEL, D_MODEL), jnp.float32) * D_MODEL ** -0.5
    return {"x": x, "norm_pre": norm_pre, "norm_post": norm_post, "w_in": w_in,
            "b_if": b_if, "conv_qk": conv_qk, "lambda_qk": lambda_qk,
            "norm_a": norm_a, "norm_b": norm_b, "w_a": w_a, "w_b": w_b, "w_out": w_out}


def reference(x, norm_pre, norm_post, w_in, b_if, conv_qk, lambda_qk, norm_a, norm_b, w_a, w_b, w_out):
    B, S, _ = x.shape
    H_A, H_B = N_HEADS_A, N_HEADS_B
    sizes = [W_A, W_A, W_A, W_A, W_B, W_B, W_B, W_B, W_B, H_B, H_B, D_MODEL, D_MODEL]
    for l in range(DEPTH):
        h = _rmsnorm(x, norm_pre[l])
        proj = jnp.einsum('bsd,dn->bsn', h, w_in[l])
        qa, ka, va, za, qb, kb, vb, ob, zb, igb, fgb, ga, gb = _split(proj, sizes)

        lam_init = 0.8 - 0.6 * math.exp(-0.3 * l)
        lq = lambda_qk[l].astype(jnp.float32)
        lam = jnp.exp(jnp.sum(lq[0] * lq[1])) - jnp.exp(jnp.sum(lq[2] * lq[3])) + lam_init
        oa = _diff_attention(qa, ka, va, lam)
        oa = _rmsnorm(oa, norm_a[l].reshape(H_A, 2 * HEAD_DIM_A)) * (1.0 - lam_init)
        ya = oa.reshape(B, S, W_A) * jax.nn.silu(za)

        qk = jax.nn.silu(_causal_conv(jnp.concatenate([qb, kb], axis=-1), conv_qk[l]))
        qb_c, kb_c = qk[..., :W_B], qk[..., W_B:]
        ig = igb + b_if[l, :H_B]
        fg = fgb + b_if[l, H_B:]
        hb = _mlstm(qb_c, kb_c, vb, ig, fg)
        hb = jax.nn.sigmoid(ob.astype(jnp.float32)).reshape(B, S, H_B, HEAD_DIM_B) * hb
        hb = _rmsnorm(hb, norm_b[l].reshape(H_B, HEAD_DIM_B)).astype(x.dtype)
        yb = hb.reshape(B, S, W_B) * jax.nn.silu(zb)

        merged = (jax.nn.sigmoid(ga) * jnp.einsum('bsw,wd->bsd', ya, w_a[l])
                  + jax.nn.sigmoid(gb) * jnp.einsum('bsw,wd->bsd', yb, w_b[l]))
        out = jnp.einsum('bsd,de->bse', merged, w_out[l])
        x = x + _rmsnorm(out, norm_post[l])
    return x
```

```python
import math
import contextlib
import ml_dtypes
import numpy as np
import concourse.bass as bass
import concourse.mybir as mybir
from concourse.bass_utils import run_bass_kernel_spmd

F32 = mybir.dt.float32
BF16 = mybir.dt.bfloat16
AF = mybir.ActivationFunctionType
ALU = mybir.AluOpType
AX = mybir.AxisListType


class Buf:
    __slots__ = ("name", "w", "r", "psum")

    def __init__(self, name="", psum=False):
        self.name = name
        self.w = []
        self.r = []
        self.psum = psum


class Op:
    __slots__ = ("eng", "fn", "deps", "is_dma", "need_inc", "cnt", "slot", "val", "pos", "is_cc")


ENGS = ("sync", "scalar", "vector", "gpsimd", "tensor")
NSLOT = 12


class Sched:
    def __init__(self, nc):
        self.nc = nc
        self.q = {e: [] for e in ENGS}
        self.ndma = {e: 0 for e in ENGS}

    def op(self, eng, fn, reads=(), writes=(), dma=False, cc=False):
        o = Op()
        o.is_cc = cc
        o.eng = eng
        o.fn = fn
        o.is_dma = dma
        o.need_inc = False
        o.cnt = 0
        o.deps = set()
        for b in reads:
            for w in b.w:
                o.deps.add(w)
            if b.psum:
                for r in b.r:
                    if r.eng != eng:
                        o.deps.add(r)
        for b in writes:
            append = dma and (not cc) and len(b.w) > 0 and (not b.r) and all(w.is_dma and not w.is_cc for w in b.w)
            if not append:
                for w in b.w:
                    o.deps.add(w)
            else:
                for w in b.w:
                    o.deps.update(d for d in w.deps)
            for r in b.r:
                o.deps.add(r)
        o.deps.discard(o)
        for b in reads:
            if b not in writes:
                b.r.append(o)
        for b in writes:
            append = dma and (not cc) and len(b.w) > 0 and (not b.r) and all(w.is_dma and not w.is_cc for w in b.w)
            if append:
                b.w.append(o)
            else:
                b.w = [o]
            b.r = []
        if cc:
            assert dma and eng == "gpsimd"
            self.ncc = getattr(self, "ncc", 0) + 1
            o.slot = -1
            o.val = self.ncc
        elif dma:
            n = self.ndma[eng]
            self.ndma[eng] = n + 1
            o.slot = n % NSLOT
            o.val = 16 * (n // NSLOT + 1)
        o.pos = len(self.q[eng])
        self.q[eng].append(o)
        return o

    def dma(self, eng, out, in_, reads=(), writes=(), **kw):
        return self.op(eng, lambda e: e.dma_start(out=out, in_=in_, **kw), reads, writes, dma=True)

    def emit(self):
        nc = self.nc
        for e in ENGS:
            for o in self.q[e]:
                for d in o.deps:
                    if d.is_dma:
                        continue
                    if d.eng == "tensor" and o.eng == "tensor":
                        continue
                    d.need_inc = True
        for e in ENGS:
            c = 0
            for o in self.q[e]:
                if (not o.is_dma) and o.need_inc:
                    c += 1
                    o.cnt = c
        import contextlib
        with contextlib.ExitStack() as st:
            esem = {e: st.enter_context(nc.semaphore("s_" + e)) for e in ENGS}
            dsem = {e: [st.enter_context(nc.semaphore("d_%s_%d" % (e, i))) for i in range(NSLOT)]
                    for e in ENGS if self.ndma[e] > 0}
            ccsem = st.enter_context(nc.semaphore("s_cc"))
            block = st.enter_context(nc.Block())

            def run(ename, eng):
                waited = {}

                def wait(sem, val):
                    k = sem.name
                    if waited.get(k, 0) >= val:
                        return
                    waited[k] = val
                    eng.wait_ge(sem, val)

                for o in self.q[ename]:
                    need = {}
                    for d in o.deps:
                        if d.is_cc:
                            sem, val = ccsem, d.val
                        elif d.is_dma:
                            sem, val = dsem[d.eng][d.slot], d.val
                        else:
                            if d.eng == "tensor" and ename == "tensor":
                                continue
                            sem, val = esem[d.eng], d.cnt
                        if sem.name not in need or need[sem.name][1] < val:
                            need[sem.name] = (sem, val)
                    for nm in sorted(need):
                        wait(*need[nm])
                    if o.is_cc:
                        o.fn(eng).then_inc(ccsem, 1)
                    elif o.is_dma:
                        if o.val > 16:
                            wait(dsem[ename][o.slot], o.val - 16)
                        o.fn(eng).then_inc(dsem[ename][o.slot], 16)
                    else:
                        ins = o.fn(eng)
                        if o.need_inc:
                            ins.then_inc(esem[ename], 1)
                n = self.ndma[ename]
                for s in range(min(n, NSLOT)):
                    last = ((n - 1 - s) // NSLOT) * NSLOT + s
                    wait(dsem[ename][s], 16 * (last // NSLOT + 1))

            @block.sync
            def _(eng):
                run("sync", eng)

            @block.scalar
            def _(eng):
                run("scalar", eng)

            @block.vector
            def _(eng):
                run("vector", eng)

            @block.gpsimd
            def _(eng):
                run("gpsimd", eng)

            @block.tensor
            def _(eng):
                run("tensor", eng)

    def barrier(self):
        last = []
        for e in ENGS:
            seen = set()
            got_c = False
            got_cc = False
            for o in reversed(self.q[e]):
                if o.is_cc:
                    continue
                elif o.is_dma:
                    if o.slot not in seen:
                        seen.add(o.slot)
                        last.append(o)
                elif not got_c:
                    last.append(o)
                    got_c = True
                if got_c and len(seen) >= NSLOT:
                    break
        for e in ENGS:
            o = self.op(e, lambda eng: eng.nop(), (), ())
            o.deps = set(last)


EPS = 1e-6
SEQ = 4096
DM = 1024
HALF = 2048


def _consts_f32():
    ident = np.eye(128, dtype=np.float32)
    tri = np.triu(np.ones((128, 128), dtype=np.float32))
    ones = np.ones((128, 128), dtype=np.float32)
    return np.concatenate([ident, tri, ones], axis=1)


def p2_weight_dmas(S, w2b, wabb, woutb, w2, wab, wout, wbufs):
    for kc in range(8):
        for c0 in range(0, 2048, 1024):
            S.dma("gpsimd", w2b[:, kc, c0:c0 + 1024], w2[kc * 128:(kc + 1) * 128, c0:c0 + 1024], writes=wbufs)
    for kc in range(8):
        S.dma("gpsimd", wabb[:, kc, :], wab[kc * 128:(kc + 1) * 128, :], writes=wbufs)
    for kc in range(8):
        S.dma("gpsimd", woutb[:, kc, :], wout[kc * 128:(kc + 1) * 128, :], writes=wbufs)


def emit_p2(ctx, T):
    nc, S, ps, bank, mem = ctx.nc, ctx.S, ctx.ps, ctx.bank, ctx.mem
    x, ysrc, sel, w2, wab, wout, gpre, gpost, xo = (T[k] for k in
                                                     ("x", "ysrc", "sel", "w2", "wab", "wout", "gpre", "gpost", "xo"))
    cst_t, idb = ctx.cst_t, ctx.idb
    mem.reset()
    w2b = mem.alloc([8, 2048], BF16)
    wabb = mem.alloc([8, 1024], BF16)
    woutb = mem.alloc([8, 1024], BF16)
    gpre_t = mem.alloc([DM], F32)
    gpost_t = mem.alloc([DM], F32)
    sel_t = mem.alloc([8], F32)
    xt = [[mem.alloc([DM], F32) for t in range(4)] for i in range(2)]
    junk = mem.alloc([DM], BF16)
    hb = [mem.alloc([DM], BF16) for i in range(2)]
    hT = [mem.alloc([8, 512], BF16) for i in range(2)]
    yt = [mem.alloc([8, 512], BF16) for i in range(2)]
    yh = [mem.alloc([8, 512], BF16) for i in range(2)]
    sg = mem.alloc([16, 512], BF16)
    t1 = [mem.alloc([512], F32) for i in range(2)]
    t2 = [mem.alloc([512], F32) for i in range(2)]
    mT = mem.alloc([8, 512], BF16)
    xn = [mem.alloc([DM], F32) for i in range(2)]
    ss = [mem.alloc([8], F32)[:, 0:1] for i in range(4)]
    rs = [mem.alloc([8], F32)[:, 0:1] for i in range(4)]
    B = {}

    def buf(name):
        if name in PERSIST:
            return ctx.pb(T.get(name + "_buf", name))
        if name not in B:
            B[name] = Buf(name)
        return B[name]

    S.dma("sync", gpre_t[:], gpre, writes=[buf("gpre")])
    S.dma("sync", gpost_t[:], gpost, writes=[buf("gpost")])
    S.dma("sync", sel_t[:, 0:2], sel, writes=[buf("sel")])

    if not T.get("w_prefetched"):
        p2_weight_dmas(S, w2b, wabb, woutb, w2, wab, wout, [buf("w2b"), buf("wabb"), buf("woutb")])

    ri = [0]

    def prologue(blk):
        xs = xt[blk % 2]
        hTb = hT[blk % 2]
        for t in range(4):
            xb = buf("xt%d_%d" % (blk % 2, t))
            xsrc_ap, xsrc_buf = x(blk * 4 + t)
            S.dma("sync", xs[t][:], xsrc_ap, reads=[buf(xsrc_buf)], writes=[xb])
            k = ri[0] % 4
            ri[0] += 1
            S.op("scalar", lambda e, t=t, k=k, xs=xs: e.activation(junk[:], xs[t][:], AF.Square, accum_out=ss[k]),
                 [xb], [buf("junk"), buf("ss%d" % k)])
            S.op("scalar", lambda e, k=k: e.activation(rs[k], ss[k], AF.Sqrt, scale=1.0 / DM, bias=EPS),
                 [buf("ss%d" % k)], [buf("rs%d" % k)])
            S.op("vector", lambda e, k=k: e.reciprocal(rs[k], rs[k]), [buf("rs%d" % k)], [buf("rs%d" % k)])
            hbi = t % 2
            S.op("vector", lambda e, t=t, k=k, xs=xs, hbi=hbi: e.scalar_tensor_tensor(
                hb[hbi][:], xs[t][:], rs[k], gpre_t[:], ALU.mult, ALU.mult),
                [xb, buf("rs%d" % k), buf("gpre")], [buf("hb%d" % hbi)])
            pb = 6 + (t % 2)
            pT = ps[:, pb, 0:512].bitcast(BF16)
            for c in range(8):
                S.op("tensor", lambda e, c=c, hbi=hbi, pT=pT: e.transpose(
                    pT[:, c * 128:(c + 1) * 128], hb[hbi][:, c * 128:(c + 1) * 128], idb[:]),
                    [buf("hb%d" % hbi), buf("idb")], [bank[pb]])
            S.op("scalar", lambda e, t=t, pT=pT, hTb=hTb: e.activation(
                hTb[:, :, t * 128:(t + 1) * 128], pT.rearrange("p (c t) -> p c t", c=8), AF.Copy),
                [bank[pb]], [buf("hT%d" % (blk % 2))])
        yb = buf("yt%d" % (blk % 2))
        ytb = yt[blk % 2]
        for hh in range(2):
            for ab in range(2):
                for r in range(2):
                    src = ysrc[ab][r * 256:(r + 1) * 256, hh * HALF + blk * 512:hh * HALF + (blk + 1) * 512].rearrange(
                        "(i p) t -> p i t", p=128)
                    c0_ = ab * 4 + r * 2
                    S.dma("sync", yh[hh][:, c0_:c0_ + 2, :], src, reads=[buf("ysrcA" if ab == 0 else "ysrcB")],
                          writes=[buf("yh%d" % hh)])
        S.op("vector", lambda e, ytb=ytb: e.tensor_scalar(ytb[:], yh[0][:], sel_t[:, 0:1], None, ALU.mult),
             [buf("yh0"), buf("sel")], [yb])
        S.op("vector", lambda e, ytb=ytb: e.scalar_tensor_tensor(ytb[:], yh[1][:], sel_t[:, 1:2], ytb[:], ALU.mult, ALU.add),
             [buf("yh1"), buf("sel"), yb], [yb])

    def gates(blk):
        hTb = hT[blk % 2]
        for fg in range(16):
            pb = fg % 2
            for kc in range(8):
                S.op("tensor", lambda e, fg=fg, kc=kc, pb=pb, hTb=hTb: e.matmul(
                    ps[:, pb, :], w2b[:, kc, fg * 128:(fg + 1) * 128], hTb[:, kc, :], start=(kc == 0), stop=(kc == 7)),
                    [buf("w2b"), buf("hT%d" % (blk % 2))], [bank[pb]])
            S.op("scalar", lambda e, fg=fg, pb=pb: e.activation(sg[:, fg, :], ps[:, pb, :], AF.Sigmoid),
                 [bank[pb]], [buf("sg")])

    def merge_out(blk):
        xs = xt[blk % 2]
        yb = buf("yt%d" % (blk % 2))
        ytb = yt[blk % 2]
        for dg in range(8):
            mb = 2 if dg % 2 == 0 else 0
            for ab in range(2):
                pb = mb + ab
                for kc in range(4):
                    S.op("tensor", lambda e, dg=dg, ab=ab, kc=kc, pb=pb, ytb=ytb: e.matmul(
                        ps[:, pb, :], wabb[:, ab * 4 + kc, dg * 128:(dg + 1) * 128], ytb[:, ab * 4 + kc, :],
                        start=(kc == 0), stop=(kc == 3)),
                        [buf("wabb"), yb], [bank[pb]])
            i = dg % 2
            S.op("vector", lambda e, dg=dg, i=i, mb=mb: e.tensor_tensor(t1[i][:], ps[:, mb, :], sg[:, dg, :], ALU.mult),
                 [bank[mb], buf("sg")], [buf("t1_%d" % i)])
            S.op("vector", lambda e, dg=dg, i=i, mb=mb: e.tensor_tensor(t2[i][:], ps[:, mb + 1, :], sg[:, 8 + dg, :], ALU.mult),
                 [bank[mb + 1], buf("sg")], [buf("t2_%d" % i)])
            S.op("gpsimd", lambda e, dg=dg, i=i: e.tensor_tensor(mT[:, dg, :], t1[i][:], t2[i][:], ALU.add),
                 [buf("t1_%d" % i), buf("t2_%d" % i)], [buf("mT")])
        for t in range(4):
            xb = buf("xt%d_%d" % (blk % 2, t))
            ob = 4 if t % 2 == 0 else 2
            for nb in range(2):
                for kc in range(8):
                    S.op("tensor", lambda e, t=t, nb=nb, kc=kc, ob=ob: e.matmul(
                        ps[:, ob + nb, :], mT[:, kc, t * 128:(t + 1) * 128], woutb[:, kc, nb * 512:(nb + 1) * 512],
                        start=(kc == 0), stop=(kc == 7)),
                        [buf("mT"), buf("woutb")], [bank[ob + nb]])
            k = ri[0] % 4
            ri[0] += 1
            po = ps[:, ob:ob + 2, :].rearrange("p a b -> p (a b)")
            S.op("scalar", lambda e, k=k, po=po: e.activation(junk[:], po, AF.Square, accum_out=ss[k]),
                 [bank[ob], bank[ob + 1]], [buf("junk"), buf("ss%d" % k)])
            S.op("scalar", lambda e, k=k: e.activation(rs[k], ss[k], AF.Sqrt, scale=1.0 / DM, bias=EPS),
                 [buf("ss%d" % k)], [buf("rs%d" % k)])
            S.op("vector", lambda e, k=k: e.reciprocal(rs[k], rs[k]), [buf("rs%d" % k)], [buf("rs%d" % k)])
            xi = t % 2
            S.op("vector", lambda e, k=k, xi=xi, po=po: e.scalar_tensor_tensor(
                xn[xi][:], po, rs[k], gpost_t[:], ALU.mult, ALU.mult),
                [bank[ob], bank[ob + 1], buf("rs%d" % k), buf("gpost")], [buf("xn%d" % xi)])
            S.op("gpsimd", lambda e, xi=xi, t=t, xs=xs: e.tensor_tensor(xn[xi][:], xn[xi][:], xs[t][:], ALU.add),
                 [buf("xn%d" % xi), xb], [buf("xn%d" % xi)])
            xdst_ap, xdst_buf = xo(blk * 4 + t)
            S.dma("gpsimd", xdst_ap, xn[xi][:], reads=[buf("xn%d" % xi)], writes=[buf(xdst_buf)])
        if T.get("after_blk") is not None:
            T["after_blk"](blk)

    prologue(0)
    for blk in range(4):
        gates(blk)
        if blk + 1 < 4:
            prologue(blk + 1)
        merge_out(blk)
    S.barrier()


class Arena:
    def __init__(self, t, nbytes):
        self.t = t
        self.n = nbytes
        self.off = 0

    def reset(self):
        self.off = 0

    def mark(self):
        return self.off

    def release(self, m):
        self.off = m

    def alloc(self, shape, dtype):
        esz = 4 if dtype == F32 else 2
        n = int(np.prod(shape)) * esz
        self.off = (self.off + 31) // 32 * 32
        a = self.off
        assert a + n <= self.n, ("arena overflow", a, n, self.n)
        self.off = a + n
        v = self.t[:, a // 2:(a + n) // 2]
        if dtype == F32:
            v = v.bitcast(F32)
        if len(shape) == 2:
            v = v.rearrange("p (a b) -> p a b", a=shape[0])
        elif len(shape) == 3:
            v = v.rearrange("p (a b c) -> p a b c", a=shape[0], b=shape[1])
        return v


SBUF_ARENA_BYTES = 210432


class Ctx:
    def __init__(self, nc, cst):
        self.nc = nc
        self.S = Sched(nc)
        sb = nc.alloc_sbuf_tensor
        self.cst_t = sb("cst_t", [128, 384], F32)
        self.idb = sb("idb", [128, 128], BF16)
        self.trib = sb("trib", [128, 128], BF16)
        self.mem = Arena(sb("arena", [128, SBUF_ARENA_BYTES // 2], BF16), SBUF_ARENA_BYTES)
        self.ps = nc.alloc_psum_tensor("ps", [128, 8, 512], F32)
        self.bank = [Buf("bank%d" % i, psum=True) for i in range(8)]
        self.pbuf = {}
        S = self.S
        S.dma("sync", self.cst_t[:], cst, writes=[self.pb("cst")])
        S.op("vector", lambda e: e.tensor_copy(self.idb[:], self.cst_t[:, 0:128]), [self.pb("cst")], [self.pb("idb")])
        S.op("vector", lambda e: e.tensor_copy(self.trib[:], self.cst_t[:, 128:256]), [self.pb("cst")], [self.pb("trib")])

    def pb(self, name):
        if name not in self.pbuf:
            self.pbuf[name] = Buf(name)
        return self.pbuf[name]


PERSIST = ("cst", "idb", "trib", "yA", "yB", "ysrcA", "ysrcB", "xin", "xo0", "xo1", "xo2", "xo3", "xin0", "xin1", "xin2", "xin3")


W1_COLS = 2308
ARENA_BYTES = 102 * 1024


def emit_p1(ctx, layer, T):
    lam_init = 0.8 - 0.6 * math.exp(-0.3 * layer)
    nc, S, ps, bank, mem = ctx.nc, ctx.S, ctx.ps, ctx.bank, ctx.mem
    x, w1, gpre, aug, abias, na, nb, convw, bif, lq, yTo = (T[k] for k in
                                                             ("x", "w1", "gpre", "aug", "abias", "na", "nb", "convw", "bif", "lq", "yTo"))
    cst_t, idb, trib = ctx.cst_t, ctx.idb, ctx.trib
    mem.reset()
    hT = mem.alloc([8, SEQ], BF16)
    wb = mem.alloc([8, 1284], BF16)
    gpre_t = mem.alloc([DM], F32)
    abias_t = mem.alloc([16], F32)
    na_t = mem.alloc([256], F32)
    nb_t = mem.alloc([256], F32)
    convw_t = mem.alloc([16], F32)
    bif_t = mem.alloc([8], F32)
    lq_t = mem.alloc([256], F32)
    sm = mem.alloc([64], F32)
    ss = [mem.alloc([8], F32) for i in range(4)]
    rs = [mem.alloc([8], F32) for i in range(4)]
    ar = mem
    stage_mark = mem.mark()
    B = {}
    tri_f = cst_t[:, 128:256]
    ones_f = cst_t[:, 256:384]
    xin_name = T.get("xin_name", "xin")

    def buf(name):
        if name in PERSIST:
            return ctx.pb(T.get(name + "_buf", name))
        if name not in B:
            B[name] = Buf(name)
        return B[name]

    def act(out, in_, func, reads, writes, **kw):
        return S.op("scalar", lambda e: e.activation(out, in_, func, **kw), reads, writes)

    def dve(method, *args, reads=(), writes=(), **kw):
        return S.op("vector", lambda e: getattr(e, method)(*args, **kw), reads, writes)

    def pool(method, *args, reads=(), writes=(), **kw):
        return S.op("gpsimd", lambda e: getattr(e, method)(*args, **kw), reads, writes)

    def mm(out, lhsT, rhs, reads, writes, **kw):
        return S.op("tensor", lambda e: e.matmul(out, lhsT, rhs, **kw), reads, writes)

    def tp(out, in_, reads, writes):
        return S.op("tensor", lambda e: e.transpose(out, in_, idb[:]), list(reads) + [buf("idb")], writes)

    for name, t, src in (("gpre", gpre_t, gpre), ("abias", abias_t, abias), ("na", na_t, na),
                         ("nb", nb_t, nb), ("convw", convw_t, convw), ("lq", lq_t, lq)):
        S.dma("sync", t[:], src, writes=[buf(name)])
    S.dma("sync", bif_t[:, 0:4], bif, writes=[buf("bif")])

    si = [0]

    def load_w(c0, ncols):
        for kc in range(8):
            S.dma("gpsimd", wb[:, kc, 0:ncols], w1[kc * 128:(kc + 1) * 128, c0:c0 + ncols], writes=[buf("wb")])

    load_w(0, 1024)

    ri = [0]

    def rstd_from(ssk, k, n):
        act(rs[k][:, 0:1], ssk, AF.Sqrt, [buf("ss%d" % k)], [buf("rs%d" % k)], scale=1.0 / n, bias=EPS)
        dve("reciprocal", rs[k][:, 0:1], rs[k][:, 0:1], reads=[buf("rs%d" % k)], writes=[buf("rs%d" % k)])

    mem.release(stage_mark)
    xt = [ar.alloc([DM], F32) for _ in range(3)]
    junk = ar.alloc([DM], BF16)
    hb = [ar.alloc([DM], BF16) for _ in range(2)]
    kk = {}

    pos = {}

    def st0_l(t):
        pos[t] = len(pos)
        i = pos[t] % 3
        xb = buf("xt%d" % i)
        xsrc_ap, xsrc_buf = x(t)
        S.dma("sync" if pos[t] % 2 == 0 else "scalar", xt[i], xsrc_ap, reads=[buf(xsrc_buf)], writes=[xb])
        k = ri[0] % 4
        ri[0] += 1
        kk[t] = k
        act(junk, xt[i], AF.Square, [xb], [buf("junk"), buf("ss%d" % k)], accum_out=ss[k][:, 0:1])
        rstd_from(ss[k][:, 0:1], k, DM)

    def st0_m(t):
        i, j, k = pos[t] % 3, pos[t] % 2, kk[t]
        dve("scalar_tensor_tensor", hb[j], xt[i], rs[k][:, 0:1], gpre_t[:], ALU.mult, ALU.mult,
            reads=[buf("xt%d" % i), buf("rs%d" % k), buf("gpre")], writes=[buf("hb%d" % j)])

    def st0_n(t):
        j = pos[t] % 2
        pb = 6 + j
        pT = ps[:, pb, 0:512].bitcast(BF16)
        for c in range(8):
            tp(pT[:, c * 128:(c + 1) * 128], hb[j][:, c * 128:(c + 1) * 128], [buf("hb%d" % j)], [bank[pb]])
        if pos[t] % 2 == 0:
            act(hT[:, :, t * 128:(t + 1) * 128], pT.rearrange("p (c t) -> p c t", c=8), AF.Copy, [bank[pb]], [buf("hT")])
        else:
            dve("tensor_copy", hT[:, :, t * 128:(t + 1) * 128], pT.rearrange("p (c t) -> p c t", c=8),
                reads=[bank[pb]], writes=[buf("hT")])

    order = list(range(0, 12)) + list(range(16, 28)) + list(range(12, 16)) + list(range(28, 32))
    for m in range(32 + 2):
        if m < 32:
            st0_l(order[m])
        if 1 <= m <= 32:
            st0_m(order[m - 1])
        if m >= 2:
            st0_n(order[m - 2])
    S.barrier()

    mem.release(stage_mark)
    QT = [ar.alloc([SEQ], BF16) for _ in range(2)]
    KT = [ar.alloc([SEQ], BF16) for _ in range(2)]
    Vaug = ar.alloc([32, 2, 129], BF16)
    gz = ar.alloc([32, 256], BF16)
    PT = [ar.alloc([2, 512], BF16) for _ in range(3)]
    yTs = ar.alloc([SEQ], BF16)
    otmp = [ar.alloc([128], F32) for _ in range(4)]
    ot = [ar.alloc([128], F32) for _ in range(2)]
    yab = [ar.alloc([128], BF16) for _ in range(4)]
    zt = [ar.alloc([256], F32) for _ in range(2)]
    junkf = ar.alloc([128], F32)
    for i in range(2):
        pool("memset", QT[i], 0.0, writes=[buf("QT%d" % i)])
        pool("memset", KT[i], 0.0, writes=[buf("KT%d" % i)])
    pool("memset", Vaug, 1.0, writes=[buf("Vaug")])
    lp = zt[0]
    dve("tensor_tensor", lp[:, 0:64], lq_t[:, 0:64], lq_t[:, 64:128], ALU.mult, reads=[buf("lq")], writes=[buf("zt0")])
    dve("tensor_tensor", lp[:, 64:128], lq_t[:, 128:192], lq_t[:, 192:256], ALU.mult, reads=[buf("lq")], writes=[buf("zt0")])
    dve("reduce_sum", sm[:, 1:2], lp[:, 0:64], AX.X, reads=[buf("zt0")], writes=[buf("sm1")])
    dve("reduce_sum", sm[:, 2:3], lp[:, 64:128], AX.X, reads=[buf("zt0")], writes=[buf("sm2")])
    act(sm[:, 1:2], sm[:, 1:2], AF.Exp, [buf("sm1")], [buf("sm1")])
    act(sm[:, 2:3], sm[:, 2:3], AF.Exp, [buf("sm2")], [buf("sm2")])
    dve("tensor_tensor", sm[:, 0:1], sm[:, 2:3], sm[:, 1:2], ALU.subtract, reads=[buf("sm1"), buf("sm2")], writes=[buf("neglam")])
    dve("tensor_scalar", sm[:, 0:1], sm[:, 0:1], -lam_init, None, ALU.add, reads=[buf("neglam")], writes=[buf("neglam")])
    neglam = sm[:, 0:1]

    pr = [0]

    def next_bank():
        b = pr[0] % 6
        pr[0] += 1
        return b

    for t in range(32):
        pb = next_bank()
        for kc in range(8):
            mm(ps[:, pb, :], hT[:, kc, t * 128:(t + 1) * 128], wb[:, kc, 512:1024], [buf("hT"), buf("wb")], [bank[pb]],
               start=(kc == 0), stop=(kc == 7))
        dve("tensor_copy", Vaug[:, t, :, 0:128], ps[:, pb, 0:256].rearrange("p (h e) -> p h e", h=2),
            reads=[bank[pb]], writes=[buf("Vaug")])
        zi = t % 2
        act(zt[zi], ps[:, pb, 256:512], AF.Silu, [bank[pb]], [buf("zt%d" % zi)])
        dve("scalar_tensor_tensor", gz[:, t, :], zt[zi], 1.0 - lam_init, na_t[:], ALU.mult, ALU.mult,
            reads=[buf("zt%d" % zi), buf("na")], writes=[buf("gz")])

    gcnt = [0]
    for hl in range(2):
        S.dma("sync", QT[0][64:68, :], aug[hl, 0], writes=[buf("QT0")])
        S.dma("sync", QT[1][0:4, :], aug[hl, 0], writes=[buf("QT1")])
        S.dma("sync", KT[0][64:68, :], aug[hl, 1], writes=[buf("KT0")])
        S.dma("sync", KT[1][0:4, :], aug[hl, 1], writes=[buf("KT1")])
        for blk in range(8):
            for qk in range(2):
                c0 = qk * 256 + hl * 128
                dst = QT if qk == 0 else KT
                nm = "QT" if qk == 0 else "KT"
                pb = next_bank()
                for kc in range(8):
                    mm(ps[:, pb, :], wb[:, kc, c0:c0 + 128], hT[:, kc, blk * 512:(blk + 1) * 512],
                       [buf("wb"), buf("hT")], [bank[pb]], start=(kc == 0), stop=(kc == 7))
                act(dst[0][0:64, blk * 512:(blk + 1) * 512], ps[0:64, pb, :], AF.Copy, [bank[pb]], [buf(nm + "0")])
                dve("tensor_copy", dst[1][64:128, blk * 512:(blk + 1) * 512], ps[64:128, pb, :],
                    reads=[bank[pb]], writes=[buf(nm + "1")])
        if hl == 1:
            load_w(1024, 1284)
        jobs = []
        for I in range(8):
            for c in range(2):
                groups = []
                for J in range(I):
                    for hf in range(2):
                        groups.append((False, I - J, [4 * J + 2 * hf, 4 * J + 2 * hf + 1]))
                for hf in range(2):
                    groups.append((True, 0, [4 * I + 2 * hf, 4 * I + 2 * hf + 1]))
                for gx, (diag, d, tiles) in enumerate(groups):
                    jobs.append((I, c, diag, d, tiles, gx == 0, gx == len(groups) - 1))
        NJ = len(jobs)
        pending_tp = []

        def rec_st(n):
            I, c, diag, d, tiles, first, last = jobs[n]
            gi = n % 3
            b0 = 2 * gi
            for jj, j in enumerate(tiles):
                n0 = 128 * (j - 4 * I) if diag else 0
                mm(ps[:, b0 + jj, n0:512], KT[c][:, j * 128:(j + 1) * 128], QT[c][:, I * 512 + n0:(I + 1) * 512],
                   [buf("KT%d" % c), buf("QT%d" % c)], [bank[b0 + jj]], start=True, stop=True)

        def rec_exp(n):
            I, c, diag, d, tiles, first, last = jobs[n]
            gi = n % 3
            b0 = 2 * gi
            ptb = buf("PT%d" % gi)
            bcol = abias_t[:, hl * 8 + d:hl * 8 + d + 1]
            if not diag:
                act(PT[gi], ps[:, b0:b0 + 2, :], AF.Exp, [bank[b0], bank[b0 + 1], buf("abias")], [ptb],
                    scale=0.125, bias=bcol)
            else:
                for jj, j in enumerate(tiles):
                    n0 = 128 * (j - 4 * I)
                    act(PT[gi][:, jj, n0:512], ps[:, b0 + jj, n0:512], AF.Exp, [bank[b0 + jj], buf("abias")], [ptb],
                        scale=0.125, bias=bcol)
                    dve("tensor_tensor", PT[gi][:, jj, n0:n0 + 128], PT[gi][:, jj, n0:n0 + 128], trib[:], ALU.mult,
                        reads=[ptb, buf("trib")], writes=[ptb])

        def rec_pv(n, m):
            I, c, diag, d, tiles, first, last = jobs[n]
            gi = n % 3
            b0 = 2 * gi
            ptb = buf("PT%d" % gi)
            for jj, j in enumerate(tiles):
                r = j - 4 * I
                us = range(r, 4) if diag else range(4)
                for u in us:
                    ob_ = 6 + u // 2
                    oc = (u % 2) * 129
                    mm(ps[:, ob_, oc:oc + 129], PT[gi][:, jj, u * 128:(u + 1) * 128], Vaug[:, j, hl, :],
                       [ptb, buf("Vaug")], [bank[ob_]],
                       start=(first and jj == 0 and u in (0, 2)), stop=(diag and r == u), skip_group_check=True)
            if not last:
                return
            for u in range(4):
                ob_ = 6 + u // 2
                oc = (u % 2) * 129
                O = ps[:, ob_, oc:oc + 128]
                Osum = ps[:, ob_, oc + 128:oc + 129]
                k = ri[0] % 4
                ri[0] += 1
                dve("reciprocal", rs[k][:, 0:1], Osum, reads=[bank[ob_]], writes=[buf("rs%d" % k)])
                if c == 0:
                    dve("tensor_scalar", otmp[u], O, rs[k][:, 0:1], None, ALU.mult,
                        reads=[bank[ob_], buf("rs%d" % k)], writes=[buf("otmp%d" % u)])
                else:
                    dve("tensor_tensor", rs[k][:, 0:1], rs[k][:, 0:1], neglam, ALU.mult,
                        reads=[buf("rs%d" % k), buf("neglam")], writes=[buf("rs%d" % k)])
                    oi = u % 2
                    dve("scalar_tensor_tensor", ot[oi], O, rs[k][:, 0:1], otmp[u], ALU.mult, ALU.add,
                        reads=[bank[ob_], buf("rs%d" % k), buf("otmp%d" % u)], writes=[buf("ot%d" % oi)])
                    k2 = ri[0] % 4
                    ri[0] += 1
                    dve("scalar_tensor_tensor", junkf, ot[oi], 1.0, ot[oi], ALU.mult, ALU.mult,
                        reads=[buf("ot%d" % oi)], writes=[buf("junkf"), buf("ss%d" % k2)], accum_out=ss[k2][:, 0:1])
                    act(rs[k2][:, 0:1], ss[k2][:, 0:1], AF.Ln, [buf("ss%d" % k2)], [buf("rs%d" % k2)], scale=1.0 / 128, bias=EPS)
                    act(rs[k2][:, 0:1], rs[k2][:, 0:1], AF.Exp, [buf("rs%d" % k2)], [buf("rs%d" % k2)], scale=-0.5)
                    tt = 4 * I + u
                    dve("scalar_tensor_tensor", yab[u], ot[oi], rs[k2][:, 0:1], gz[:, tt, hl * 128:(hl + 1) * 128],
                        ALU.mult, ALU.mult, reads=[buf("ot%d" % oi), buf("rs%d" % k2), buf("gz")], writes=[buf("yab%d" % u)])
                    pending_tp.append((m + 2, u, tt, b0))

        def flush_tp(m, force=False):
            while pending_tp and (force or pending_tp[0][0] <= m):
                _, u, tt, b0 = pending_tp.pop(0)
                pT = ps[:, b0, u * 64:(u + 1) * 64].bitcast(BF16)
                tp(pT, yab[u], [buf("yab%d" % u)], [bank[b0]])
                act(yTs[:, tt * 128:(tt + 1) * 128], pT, AF.Copy, [bank[b0]], [buf("yTs")])

        for m in range(NJ + 2):
            if m < NJ:
                rec_st(m)
            if 1 <= m <= NJ:
                rec_exp(m - 1)
            if m >= 2:
                rec_pv(m - 2, m)
            flush_tp(m)
        flush_tp(0, force=True)
        S.dma("gpsimd", yTo[0][hl * 128:(hl + 1) * 128, :], yTs, reads=[buf("yTs")], writes=[buf("yA")])
    if T.get("after_A") is not None:
        T["after_A"]()
    S.barrier()

    mem.release(stage_mark)
    qbT = [ar.alloc([SEQ], BF16) for _ in range(2)]
    kbT = [ar.alloc([SEQ], BF16) for _ in range(2)]
    vbaug = ar.alloc([32, 2, 129], BF16)
    sgo = ar.alloc([32, 256], BF16)
    gzb = ar.alloc([32, 256], BF16)
    yTb = [ar.alloc([1024], BF16) for _ in range(2)]
    xpre = [ar.alloc([515], F32) for _ in range(2)]
    ct = [ar.alloc([512], F32) for _ in range(2)]
    zt = [ar.alloc([256], F32) for _ in range(2)]
    gts = ar.alloc([32, 4], F32)
    IG = ar.alloc([2, 32], F32)
    LF = ar.alloc([2, 32], F32)
    EB = ar.alloc([64], F32)
    DEC = ar.alloc([64], F32)
    EK = ar.alloc([64], F32)
    SmT = [ar.alloc([128], BF16) for _ in range(4)]
    Ktm = [ar.alloc([128], BF16) for _ in range(4)]
    G = [ar.alloc([129], F32) for _ in range(2)]
    Cbf = [ar.alloc([129], BF16) for _ in range(2)]
    hbt = [ar.alloc([128], F32) for _ in range(2)]
    ybb = [ar.alloc([128], BF16) for _ in range(4)]
    dn = [ar.alloc([4], F32) for _ in range(2)]
    junkf3 = [ar.alloc([128], F32) for _ in range(2)]
    pool("memset", vbaug, 1.0, writes=[buf("vbaug")])
    for t in range(32):
        pb = next_bank()
        for kc in range(8):
            mm(ps[:, pb, :], hT[:, kc, t * 128:(t + 1) * 128], wb[:, kc, 512:1024], [buf("hT"), buf("wb")], [bank[pb]],
               start=(kc == 0), stop=(kc == 7))
        dve("tensor_copy", vbaug[:, t, :, 0:128], ps[:, pb, 0:256].rearrange("p (h e) -> p h e", h=2),
            reads=[bank[pb]], writes=[buf("vbaug")])
        act(sgo[:, t, :], ps[:, pb, 256:512], AF.Sigmoid, [bank[pb]], [buf("sgo")])
        pb = next_bank()
        for kc in range(8):
            mm(ps[:, pb, 0:260], hT[:, kc, t * 128:(t + 1) * 128], wb[:, kc, 1024:1284], [buf("hT"), buf("wb")], [bank[pb]],
               start=(kc == 0), stop=(kc == 7))
        zi = t % 2
        act(zt[zi], ps[:, pb, 0:256], AF.Silu, [bank[pb]], [buf("zt%d" % zi)])
        dve("tensor_tensor", gzb[:, t, :], zt[zi], nb_t[:], ALU.mult, reads=[buf("zt%d" % zi), buf("nb")], writes=[buf("gzb")])
        dve("tensor_copy", gts[:, t, :], ps[:, pb, 256:260], reads=[bank[pb]], writes=[buf("gts")])
    for h in range(2):
        dve("tensor_scalar", IG[:, h, :], gts[:, :, h], bif_t[:, h:h + 1], None, ALU.add,
            reads=[buf("gts"), buf("bif")], writes=[buf("IG")])
        dve("tensor_scalar", LF[:, h, :], gts[:, :, 2 + h], bif_t[:, 2 + h:3 + h], None, ALU.add,
            reads=[buf("gts"), buf("bif")], writes=[buf("LF")])
    LFf = LF.rearrange("p a b -> p (a b)")
    IGf = IG.rearrange("p a b -> p (a b)")
    act(LFf, LFf, AF.Exp, [buf("LF")], [buf("LF")], scale=-1.0)
    act(LFf, LFf, AF.Ln, [buf("LF")], [buf("LF")], bias=1.0)
    dve("tensor_scalar", LFf, LFf, -1.0, None, ALU.mult, reads=[buf("LF")], writes=[buf("LF")])
    mm(ps[:, 6, 0:64], tri_f, LFf, [buf("cst"), buf("LF")], [bank[6]], start=True, stop=True)
    mm(ps[:, 7, 0:64], ones_f, LFf, [buf("cst"), buf("LF")], [bank[7]], start=True, stop=True)
    act(EB, ps[:, 6, 0:64], AF.Exp, [bank[6]], [buf("EB")])
    act(DEC, ps[:, 7, 0:64], AF.Exp, [bank[7]], [buf("DEC")])
    dve("tensor_tensor", EK, IGf, ps[:, 6, 0:64], ALU.subtract, reads=[buf("IG"), bank[6]], writes=[buf("EK")])
    act(EK, EK, AF.Exp, [buf("EK")], [buf("EK")], bias=math.log(128.0 ** -0.5))
    xi = [0]
    for grp in range(4):
        dst = (qbT if grp < 2 else kbT)[grp % 2]
        dnm = ("qbT" if grp < 2 else "kbT") + str(grp % 2)
        for blk in range(8):
            pb = next_bank()
            for kc in range(8):
                mm(ps[:, pb, :], wb[:, kc, grp * 128:(grp + 1) * 128], hT[:, kc, blk * 512:(blk + 1) * 512],
                   [buf("wb"), buf("hT")], [bank[pb]], start=(kc == 0), stop=(kc == 7))
            i = xi[0] % 2
            xi[0] += 1
            xp = xpre[i]
            xpb = buf("xpre%d" % i)
            if blk == 0:
                pool("memset", xp[:, 0:3], 0.0, writes=[xpb])
            else:
                pool("tensor_copy", xp[:, 0:3], xpre[1 - i][:, 512:515], reads=[buf("xpre%d" % (1 - i))], writes=[xpb])
            act(xp[:, 3:515], ps[:, pb, :], AF.Copy, [bank[pb]], [xpb])
            cw = convw_t[:, grp * 4:(grp + 1) * 4]
            dve("tensor_scalar", ct[0], xp[:, 0:512], cw[:, 0:1], None, ALU.mult, reads=[xpb, buf("convw")], writes=[buf("ct0")])
            dve("scalar_tensor_tensor", ct[1], xp[:, 1:513], cw[:, 1:2], ct[0], ALU.mult, ALU.add,
                reads=[xpb, buf("convw"), buf("ct0")], writes=[buf("ct1")])
            dve("scalar_tensor_tensor", ct[0], xp[:, 2:514], cw[:, 2:3], ct[1], ALU.mult, ALU.add,
                reads=[xpb, buf("convw"), buf("ct1")], writes=[buf("ct0")])
            dve("scalar_tensor_tensor", ct[1], xp[:, 3:515], cw[:, 3:4], ct[0], ALU.mult, ALU.add,
                reads=[xpb, buf("convw"), buf("ct0")], writes=[buf("ct1")])
            act(dst[:, blk * 512:(blk + 1) * 512], ct[1], AF.Silu, [buf("ct1")], [buf(dnm)])
    if T.get("prefetch") is not None:
        T["prefetch"]([buf("hT"), buf("wb")])
    def ph_x(c, h):
        col = h * 32 + c
        i4 = 2 * (c % 2) + h
        cs = slice(c * 128, (c + 1) * 128)
        qb_, kb_ = buf("qbT%d" % h), buf("kbT%d" % h)
        pTk = ps[:, h, 128:192].bitcast(BF16)
        return [
            lambda: mm(ps[:, h, 0:128], kbT[h][:, cs], qbT[h][:, cs], [qb_, kb_], [bank[h]], start=True, stop=True),
            lambda: tp(pTk, kbT[h][:, cs], [kb_], [bank[h]]),
            lambda: dve("scalar_tensor_tensor", SmT[i4], ps[:, h, 0:128], EK[:, col:col + 1], tri_f, ALU.mult, ALU.mult,
                        reads=[bank[h], buf("EK"), buf("cst")], writes=[buf("SmT%d" % i4)]),
            lambda: act(Ktm[i4], pTk, AF.Copy, [bank[h], buf("EK")], [buf("Ktm%d" % i4)], scale=EK[:, col:col + 1]),
        ]

    def ph_y(c, h):
        col = h * 32 + c
        i4 = 2 * (c % 2) + h
        hb_ = 2 + 2 * (c % 2) + h
        ub_ = 6 + h
        cs = slice(c * 128, (c + 1) * 128)
        qb_ = buf("qbT%d" % h)
        ops = [lambda: mm(ps[:, hb_, 0:129], SmT[i4], vbaug[:, c, h, :], [buf("SmT%d" % i4), buf("vbaug")], [bank[hb_]],
                          start=True, stop=(c == 0))]
        if c > 0:
            ops.append(lambda: mm(ps[:, hb_, 0:129], qbT[h][:, cs], Cbf[h], [qb_, buf("Cbf%d" % h)], [bank[hb_]],
                                  start=False, stop=True))
        ops.append(lambda: mm(ps[:, ub_, 0:129], Ktm[i4], vbaug[:, c, h, :], [buf("Ktm%d" % i4), buf("vbaug")], [bank[ub_]],
                              start=True, stop=True))
        if c == 0:
            ops.append(lambda: dve("tensor_copy", G[h], ps[:, ub_, 0:129], reads=[bank[ub_]], writes=[buf("G%d" % h)]))
        else:
            ops.append(lambda: dve("scalar_tensor_tensor", G[h], G[h], DEC[:, col - 1:col], ps[:, ub_, 0:129], ALU.mult, ALU.add,
                                   reads=[buf("G%d" % h), buf("DEC"), bank[ub_]], writes=[buf("G%d" % h)]))
        if c < 31:
            ops.append(lambda: act(Cbf[h], G[h], AF.Copy, [buf("G%d" % h), buf("DEC")], [buf("Cbf%d" % h)],
                                   scale=DEC[:, col:col + 1]))
        return ops

    def ph_z1(c, h):
        col = h * 32 + c
        i4 = 2 * (c % 2) + h
        hb_ = 2 + 2 * (c % 2) + h
        dnb = buf("dn%d" % h)
        d0, d1, d2 = dn[h][:, 0:1], dn[h][:, 1:2], dn[h][:, 2:3]
        k = ri[0] % 4
        ri[0] += 1
        ssb, rsb = buf("ss%d" % k), buf("rs%d" % k)
        return [
            lambda: dve("tensor_tensor", d0, ps[:, hb_, 128:129], EB[:, col:col + 1], ALU.mult,
                        reads=[bank[hb_], buf("EB")], writes=[dnb]),
            lambda: dve("tensor_scalar", d1, d0, -1.0, 1.0, ALU.mult, ALU.max, reads=[dnb], writes=[dnb]),
            lambda: dve("tensor_tensor", d0, d0, d1, ALU.max, reads=[dnb], writes=[dnb]),
            lambda: dve("reciprocal", d1, d0, reads=[dnb], writes=[dnb]),
            lambda: dve("tensor_tensor", d2, d1, EB[:, col:col + 1], ALU.mult, reads=[dnb, buf("EB")], writes=[dnb]),
            lambda: dve("scalar_tensor_tensor", hbt[h], ps[:, hb_, 0:128], d2, sgo[:, c, h * 128:(h + 1) * 128],
                        ALU.mult, ALU.mult, reads=[bank[hb_], dnb, buf("sgo")], writes=[buf("hbt%d" % h)]),
            lambda: dve("scalar_tensor_tensor", junkf3[h], hbt[h], 1.0, hbt[h], ALU.mult, ALU.mult,
                        reads=[buf("hbt%d" % h)], writes=[buf("junkf3_%d" % h), ssb], accum_out=ss[k][:, 0:1]),
            lambda: act(rs[k][:, 0:1], ss[k][:, 0:1], AF.Ln, [ssb], [rsb], scale=1.0 / 128, bias=EPS),
            lambda: act(rs[k][:, 0:1], rs[k][:, 0:1], AF.Exp, [rsb], [rsb], scale=-0.5),
            lambda: dve("scalar_tensor_tensor", ybb[i4], hbt[h], rs[k][:, 0:1], gzb[:, c, h * 128:(h + 1) * 128],
                        ALU.mult, ALU.mult, reads=[buf("hbt%d" % h), rsb, buf("gzb")], writes=[buf("ybb%d" % i4)]),
        ]

    def ph_z2(c, h):
        i4 = 2 * (c % 2) + h
        pTy = ps[:, h, 192:256].bitcast(BF16)
        c8 = c % 8
        ops = [lambda: tp(pTy, ybb[i4], [buf("ybb%d" % i4)], [bank[h]]),
               lambda: act(yTb[h][:, c8 * 128:(c8 + 1) * 128], pTy, AF.Copy, [bank[h]], [buf("yTb%d" % h)])]
        if c8 == 7:
            ops.append(lambda: S.dma("gpsimd", yTo[1][h * 128:(h + 1) * 128, (c - 7) * 128:(c + 1) * 128], yTb[h],
                                     reads=[buf("yTb%d" % h)], writes=[buf("yB")]))
        return ops

    def interleave(fn, c):
        a_, b_ = fn(c, 0), fn(c, 1)
        for j in range(max(len(a_), len(b_))):
            if j < len(a_):
                a_[j]()
            if j < len(b_):
                b_[j]()

    for q in range(32 + 3):
        if q < 32:
            interleave(ph_x, q)
        if 1 <= q <= 32:
            interleave(ph_y, q - 1)
        if 2 <= q <= 33:
            interleave(ph_z1, q - 2)
        if q >= 3:
            interleave(ph_z2, q - 3)
    S.barrier()


_SLOPES = [2.0 ** (-8.0 * (h + 1) / 4) for h in range(4)]
_CST = None


def _p1_inputs(inp, l, g, x_full):
    w_in = inp["w_in"][l]
    sl = slice(g * 256, (g + 1) * 256)
    parts = [w_in[:, 0:512][:, sl], w_in[:, 512:1024][:, sl], w_in[:, 1024:1536][:, sl], w_in[:, 1536:2048][:, sl],
             w_in[:, 2048:2560][:, sl], w_in[:, 2560:3072][:, sl], w_in[:, 3072:3584][:, sl], w_in[:, 3584:4096][:, sl],
             w_in[:, 4096:4608][:, sl], w_in[:, 4608 + 2 * g:4610 + 2 * g], w_in[:, 4612 + 2 * g:4614 + 2 * g]]
    w1 = np.ascontiguousarray(np.concatenate(parts, axis=1), dtype=np.float32)
    assert w1.shape == (DM, W1_COLS)
    v = np.arange(SEQ) % 512
    hi = ((v // 16) * 16).astype(np.float32)
    lo = (v % 16).astype(np.float32)
    one = np.ones(SEQ, np.float32)
    aug = np.zeros((2, 2, 4, SEQ), np.float32)
    abias = np.zeros((128, 16), np.float32)
    for hl in range(2):
        s = _SLOPES[2 * g + hl]
        aug[hl, 0] = np.stack([-s * 8.0 * hi, -s * 8.0 * lo, one, one])
        aug[hl, 1] = np.stack([one, one, s * 8.0 * hi, s * 8.0 * lo])
        for d in range(8):
            abias[:, hl * 8 + d] = -s * 512.0 * d
    cq = inp["conv_qk"][l]
    convw = np.zeros((128, 16), np.float32)
    for grp in range(4):
        ch0 = (0 if grp < 2 else 512) + (2 * g + grp % 2) * 128
        convw[:, grp * 4:(grp + 1) * 4] = cq[:, ch0:ch0 + 128].T
    bi = inp["b_if"][l]
    bif = np.broadcast_to(np.array([bi[2 * g], bi[2 * g + 1], bi[4 + 2 * g], bi[5 + 2 * g]], np.float32), (128, 4))
    return {
        "w1": w1,
        "gpre": np.ascontiguousarray(np.broadcast_to(inp["norm_pre"][l], (128, DM))),
        "cst": _consts_f32(),
        "aug": aug.astype(ml_dtypes.bfloat16),
        "abias": abias,
        "na": np.ascontiguousarray(np.broadcast_to(inp["norm_a"][l][sl], (128, 256))),
        "nb": np.ascontiguousarray(np.broadcast_to(inp["norm_b"][l][sl], (128, 256))),
        "convw": convw,
        "bif": np.ascontiguousarray(bif),
        "lq": np.ascontiguousarray(np.broadcast_to(inp["lambda_qk"][l].reshape(256), (128, 256))),
    }


def _host_maps(inp):
    x = np.asarray(inp["x"], dtype=np.float32)
    maps = []
    for c in range(8):
        b, g = c // 2, c % 2
        m = {"x": np.ascontiguousarray(x[b]), "xown": np.ascontiguousarray(x[b, g * HALF:(g + 1) * HALF]),
             "cst": _consts_f32()}
        selv = np.zeros((128, 2), np.float32)
        selv[:, g] = 1.0
        m["sel"] = selv
        for l in range(2):
            p1 = _p1_inputs(inp, l, g, None)
            for k in ("w1", "gpre", "na", "nb", "convw", "bif", "lq"):
                m["%s_%d" % (k, l)] = p1[k]
            if l == 0:
                m["aug"] = p1["aug"]
                m["abias"] = p1["abias"]
            w_in = inp["w_in"][l]
            m["w2_%d" % l] = np.ascontiguousarray(w_in[:, 4616:6664], dtype=np.float32)
            m["wab_%d" % l] = np.ascontiguousarray(np.concatenate([inp["w_a"][l], inp["w_b"][l]], axis=0), dtype=np.float32)
            m["wout_%d" % l] = np.ascontiguousarray(inp["w_out"][l], dtype=np.float32)
            m["gpost_%d" % l] = np.ascontiguousarray(np.broadcast_to(inp["norm_post"][l], (128, DM)), dtype=np.float32)
        maps.append(m)
    return maps


PAIRS = [[0, 1], [2, 3], [4, 5], [6, 7]]


def build_fused():
    nc = bass.Bass("TRN2", target_bir_lowering=False)

    def ext(name, shape, dt=F32):
        return nc.dram_tensor(name, shape, dt, kind="ExternalInput").ap()

    x = ext("x", [SEQ, DM])
    xown = ext("xown", [HALF, DM])
    cst = ext("cst", [128, 384])
    sel = ext("sel", [128, 2])
    aug = ext("aug", [2, 2, 4, SEQ], BF16)
    abias = ext("abias", [128, 16])
    L = []
    for l in range(2):
        L.append({
            "w1": ext("w1_%d" % l, [DM, W1_COLS]), "gpre": ext("gpre_%d" % l, [128, DM]),
            "na": ext("na_%d" % l, [128, 256]), "nb": ext("nb_%d" % l, [128, 256]),
            "convw": ext("convw_%d" % l, [128, 16]), "bif": ext("bif_%d" % l, [128, 4]),
            "lq": ext("lq_%d" % l, [128, 256]), "w2": ext("w2_%d" % l, [DM, 2048]),
            "wab": ext("wab_%d" % l, [1024, DM]), "wout": ext("wout_%d" % l, [DM, DM]),
            "gpost": ext("gpost_%d" % l, [128, DM]), "aug": aug, "abias": abias, "sel": sel,
        })
    out = nc.dram_tensor("out", [HALF, DM], F32, kind="ExternalOutput").ap()
    ysend = [nc.dram_tensor("ysend%d" % k, [256, SEQ], BF16) for k in range(2)]
    yrecv = [nc.dram_tensor("yrecv%d" % k, [512, SEQ], BF16) for k in range(2)]
    xnew = [nc.dram_tensor("xnew%d" % k, [512, DM], F32) for k in range(4)]
    xg = [nc.dram_tensor("xg%d" % k, [1024, DM], F32) for k in range(4)]
    ctx = Ctx(nc, cst)
    S = ctx.S

    def gather(src, dst, sbuf, dbuf):
        S.op("gpsimd", lambda e: e.collective_compute("AllGather", ALU.bypass, replica_groups=PAIRS,
                                                       ins=[src.ap().opt()], outs=[dst.ap().opt()]),
             [ctx.pb(sbuf)], [ctx.pb(dbuf)], dma=True, cc=True)

    for l in range(2):
        T1 = dict(L[l])
        if l == 0:
            T1["x"] = lambda t: (x[t * 128:(t + 1) * 128, :], "xin")
        else:
            def xsrc1(t):
                hh, tt = t // 16, t % 16
                k, r = tt // 4, tt % 4
                return xg[k].ap()[hh * 512 + r * 128:hh * 512 + (r + 1) * 128, :], "xin%d" % k
            T1["x"] = xsrc1
            for k in range(4):
                T1["xin%d_buf" % k] = "xg%d" % k
        T1.update({"yTo": [ysend[0].ap(), ysend[1].ap()], "yA_buf": "ysend0", "yB_buf": "ysend1",
                   "after_A": lambda: gather(ysend[0], yrecv[0], "ysend0", "yrecv0")})
        def prefetch(dead_bufs, l=l):
            tmp = Arena(ctx.mem.t, ctx.mem.n)
            w2b = tmp.alloc([8, 2048], BF16)
            wabb = tmp.alloc([8, 1024], BF16)
            woutb = tmp.alloc([8, 1024], BF16)
            p2_weight_dmas(S, w2b, wabb, woutb, L[l]["w2"], L[l]["wab"], L[l]["wout"], dead_bufs)
        T1["prefetch"] = prefetch
        emit_p1(ctx, l, T1)
        gather(ysend[1], yrecv[1], "ysend1", "yrecv1")
        T2 = dict(L[l])
        T2.update({"ysrc": [yrecv[0].ap(), yrecv[1].ap()], "ysrcA_buf": "yrecv0", "ysrcB_buf": "yrecv1",
                   "w_prefetched": True})
        if l == 0:
            T2["x"] = lambda t: (xown[t * 128:(t + 1) * 128, :], "xin")
            T2["xo"] = lambda t: (xnew[t // 4].ap()[(t % 4) * 128:(t % 4 + 1) * 128, :], "xo%d" % (t // 4))
            for k in range(4):
                T2["xo%d_buf" % k] = "xnew%d" % k
            T2["after_blk"] = lambda blk: gather(xnew[blk], xg[blk], "xnew%d" % blk, "xg%d" % blk)
        else:
            T2["x"] = lambda t: (xnew[t // 4].ap()[(t % 4) * 128:(t % 4 + 1) * 128, :], "xin%d" % (t // 4))
            for k in range(4):
                T2["xin%d_buf" % k] = "xnew%d" % k
            T2["xo"] = lambda t: (out[t * 128:(t + 1) * 128, :], "xo0")
            T2["xo0_buf"] = "out"
        emit_p2(ctx, T2)
    S.emit()
    return nc


def kernel(**inputs):
    inp = {k: np.asarray(v) for k, v in inputs.items()}
    maps = _host_maps(inp)
    nc = build_fused()
    res = run_bass_kernel_spmd(nc, maps, core_ids=list(range(8))).results
    out = np.empty((4, SEQ, DM), np.float32)
    for c in range(8):
        b, g = c // 2, c % 2
        out[b, g * HALF:(g + 1) * HALF] = res[c]["out"]
    return out
```

```python
import math
import contextlib
import ml_dtypes
import numpy as np
import concourse.bass as bass
import concourse.mybir as mybir
from concourse.bass_utils import run_bass_kernel_spmd

F32 = mybir.dt.float32
BF16 = mybir.dt.bfloat16
AF = mybir.ActivationFunctionType
ALU = mybir.AluOpType
AX = mybir.AxisListType


class Buf:
    __slots__ = ("name", "w", "r", "psum")

    def __init__(self, name="", psum=False):
        self.name = name
        self.w = []
        self.r = []
        self.psum = psum


class Op:
    __slots__ = ("eng", "fn", "deps", "is_dma", "need_inc", "cnt", "slot", "val", "pos", "is_cc")


ENGS = ("sync", "scalar", "vector", "gpsimd", "tensor")
NSLOT = 12


class Sched:
    def __init__(self, nc):
        self.nc = nc
        self.q = {e: [] for e in ENGS}
        self.ndma = {e: 0 for e in ENGS}

    def op(self, eng, fn, reads=(), writes=(), dma=False, cc=False):
        o = Op()
        o.is_cc = cc
        o.eng = eng
        o.fn = fn
        o.is_dma = dma
        o.need_inc = False
        o.cnt = 0
        o.deps = set()
        for b in reads:
            for w in b.w:
                o.deps.add(w)
            if b.psum:
                for r in b.r:
                    if r.eng != eng:
                        o.deps.add(r)
        for b in writes:
            append = dma and (not cc) and len(b.w) > 0 and (not b.r) and all(w.is_dma and not w.is_cc for w in b.w)
            if not append:
                for w in b.w:
                    o.deps.add(w)
            else:
                for w in b.w:
                    o.deps.update(d for d in w.deps)
            for r in b.r:
                o.deps.add(r)
        o.deps.discard(o)
        for b in reads:
            if b not in writes:
                b.r.append(o)
        for b in writes:
            append = dma and (not cc) and len(b.w) > 0 and (not b.r) and all(w.is_dma and not w.is_cc for w in b.w)
            if append:
                b.w.append(o)
            else:
                b.w = [o]
            b.r = []
        if cc:
            assert dma and eng == "gpsimd"
            self.ncc = getattr(self, "ncc", 0) + 1
            o.slot = -1
            o.val = self.ncc
        elif dma:
            n = self.ndma[eng]
            self.ndma[eng] = n + 1
            o.slot = n % NSLOT
            o.val = 16 * (n // NSLOT + 1)
        o.pos = len(self.q[eng])
        self.q[eng].append(o)
        return o

    def dma(self, eng, out, in_, reads=(), writes=(), **kw):
        return self.op(eng, lambda e: e.dma_start(out=out, in_=in_, **kw), reads, writes, dma=True)

    def emit(self):
        nc = self.nc
        for e in ENGS:
            for o in self.q[e]:
                for d in o.deps:
                    if d.is_dma:
                        continue
                    if d.eng == "tensor" and o.eng == "tensor":
                        continue
                    d.need_inc = True
        for e in ENGS:
            c = 0
            for o in self.q[e]:
                if (not o.is_dma) and o.need_inc:
                    c += 1
                    o.cnt = c
        import contextlib
        with contextlib.ExitStack() as st:
            esem = {e: st.enter_context(nc.semaphore("s_" + e)) for e in ENGS}
            dsem = {e: [st.enter_context(nc.semaphore("d_%s_%d" % (e, i))) for i in range(NSLOT)]
                    for e in ENGS if self.ndma[e] > 0}
            ccsem = st.enter_context(nc.semaphore("s_cc"))
            block = st.enter_context(nc.Block())

            def run(ename, eng):
                waited = {}

                def wait(sem, val):
                    k = sem.name
                    if waited.get(k, 0) >= val:
                        return
                    waited[k] = val
                    eng.wait_ge(sem, val)

                for o in self.q[ename]:
                    need = {}
                    for d in o.deps:
                        if d.is_cc:
                            sem, val = ccsem, d.val
                        elif d.is_dma:
                            sem, val = dsem[d.eng][d.slot], d.val
                        else:
                            if d.eng == "tensor" and ename == "tensor":
                                continue
                            sem, val = esem[d.eng], d.cnt
                        if sem.name not in need or need[sem.name][1] < val:
                            need[sem.name] = (sem, val)
                    for nm in sorted(need):
                        wait(*need[nm])
                    if o.is_cc:
                        o.fn(eng).then_inc(ccsem, 1)
                    elif o.is_dma:
                        if o.val > 16:
                            wait(dsem[ename][o.slot], o.val - 16)
                        o.fn(eng).then_inc(dsem[ename][o.slot], 16)
                    else:
                        ins = o.fn(eng)
                        if o.need_inc:
                            ins.then_inc(esem[ename], 1)
                n = self.ndma[ename]
                for s in range(min(n, NSLOT)):
                    last = ((n - 1 - s) // NSLOT) * NSLOT + s
                    wait(dsem[ename][s], 16 * (last // NSLOT + 1))

            @block.sync
            def _(eng):
                run("sync", eng)

            @block.scalar
            def _(eng):
                run("scalar", eng)

            @block.vector
            def _(eng):
                run("vector", eng)

            @block.gpsimd
            def _(eng):
                run("gpsimd", eng)

            @block.tensor
            def _(eng):
                run("tensor", eng)

    def barrier(self):
        last = []
        for e in ENGS:
            seen = set()
            got_c = False
            got_cc = False
            for o in reversed(self.q[e]):
                if o.is_cc:
                    continue
                elif o.is_dma:
                    if o.slot not in seen:
                        seen.add(o.slot)
                        last.append(o)
                elif not got_c:
                    last.append(o)
                    got_c = True
                if got_c and len(seen) >= NSLOT:
                    break
        for e in ENGS:
            o = self.op(e, lambda eng: eng.nop(), (), ())
            o.deps = set(last)


EPS = 1e-6
SEQ = 4096
DM = 1024
HALF = 2048


def _consts_f32():
    ident = np.eye(128, dtype=np.float32)
    tri = np.triu(np.ones((128, 128), dtype=np.float32))
    ones = np.ones((128, 128), dtype=np.float32)
    return np.concatenate([ident, tri, ones], axis=1)


def p2_weight_dmas(S, w2b, wabb, woutb, w2, wab, wout, wbufs):
    for kc in range(8):
        for c0 in range(0, 2048, 1024):
            S.dma("gpsimd", w2b[:, kc, c0:c0 + 1024], w2[kc * 128:(kc + 1) * 128, c0:c0 + 1024], writes=wbufs)
    for kc in range(8):
        S.dma("gpsimd", wabb[:, kc, :], wab[kc * 128:(kc + 1) * 128, :], writes=wbufs)
    for kc in range(8):
        S.dma("gpsimd", woutb[:, kc, :], wout[kc * 128:(kc + 1) * 128, :], writes=wbufs)


def emit_p2(ctx, T):
    nc, S, ps, bank, mem = ctx.nc, ctx.S, ctx.ps, ctx.bank, ctx.mem
    x, ysrc, sel, w2, wab, wout, gpre, gpost, xo = (T[k] for k in
                                                     ("x", "ysrc", "sel", "w2", "wab", "wout", "gpre", "gpost", "xo"))
    cst_t, idb = ctx.cst_t, ctx.idb
    mem.reset()
    w2b = mem.alloc([8, 2048], BF16)
    wabb = mem.alloc([8, 1024], BF16)
    woutb = mem.alloc([8, 1024], BF16)
    gpre_t = mem.alloc([DM], F32)
    gpost_t = mem.alloc([DM], F32)
    sel_t = mem.alloc([8], F32)
    xt = [[mem.alloc([DM], F32) for t in range(4)] for i in range(2)]
    junk = mem.alloc([DM], BF16)
    hb = [mem.alloc([DM], BF16) for i in range(2)]
    hT = [mem.alloc([8, 512], BF16) for i in range(2)]
    yt = [mem.alloc([8, 512], BF16) for i in range(2)]
    yh = [mem.alloc([8, 512], BF16) for i in range(2)]
    sg = mem.alloc([16, 512], BF16)
    t1 = [mem.alloc([512], F32) for i in range(2)]
    t2 = [mem.alloc([512], F32) for i in range(2)]
    mT = mem.alloc([8, 512], BF16)
    xn = [mem.alloc([DM], F32) for i in range(2)]
    ss = [mem.alloc([8], F32)[:, 0:1] for i in range(4)]
    rs = [mem.alloc([8], F32)[:, 0:1] for i in range(4)]
    B = {}

    def buf(name):
        if name in PERSIST:
            return ctx.pb(T.get(name + "_buf", name))
        if name not in B:
            B[name] = Buf(name)
        return B[name]

    S.dma("sync", gpre_t[:], gpre, writes=[buf("gpre")])
    S.dma("sync", gpost_t[:], gpost, writes=[buf("gpost")])
    S.dma("sync", sel_t[:, 0:2], sel, writes=[buf("sel")])

    if not T.get("w_prefetched"):
        p2_weight_dmas(S, w2b, wabb, woutb, w2, wab, wout, [buf("w2b"), buf("wabb"), buf("woutb")])

    ri = [0]

    def prologue(blk):
        xs = xt[blk % 2]
        hTb = hT[blk % 2]
        for t in range(4):
            xb = buf("xt%d_%d" % (blk % 2, t))
            xsrc_ap, xsrc_buf = x(blk * 4 + t)
            S.dma("sync", xs[t][:], xsrc_ap, reads=[buf(xsrc_buf)], writes=[xb])
            k = ri[0] % 4
            ri[0] += 1
            S.op("scalar", lambda e, t=t, k=k, xs=xs: e.activation(junk[:], xs[t][:], AF.Square, accum_out=ss[k]),
                 [xb], [buf("junk"), buf("ss%d" % k)])
            S.op("scalar", lambda e, k=k: e.activation(rs[k], ss[k], AF.Sqrt, scale=1.0 / DM, bias=EPS),
                 [buf("ss%d" % k)], [buf("rs%d" % k)])
            S.op("vector", lambda e, k=k: e.reciprocal(rs[k], rs[k]), [buf("rs%d" % k)], [buf("rs%d" % k)])
            hbi = t % 2
            S.op("vector", lambda e, t=t, k=k, xs=xs, hbi=hbi: e.scalar_tensor_tensor(
                hb[hbi][:], xs[t][:], rs[k], gpre_t[:], ALU.mult, ALU.mult),
                [xb, buf("rs%d" % k), buf("gpre")], [buf("hb%d" % hbi)])
            pb = 6 + (t % 2)
            pT = ps[:, pb, 0:512].bitcast(BF16)
            for c in range(8):
                S.op("tensor", lambda e, c=c, hbi=hbi, pT=pT: e.transpose(
                    pT[:, c * 128:(c + 1) * 128], hb[hbi][:, c * 128:(c + 1) * 128], idb[:]),
                    [buf("hb%d" % hbi), buf("idb")], [bank[pb]])
            S.op("scalar", lambda e, t=t, pT=pT, hTb=hTb: e.activation(
                hTb[:, :, t * 128:(t + 1) * 128], pT.rearrange("p (c t) -> p c t", c=8), AF.Copy),
                [bank[pb]], [buf("hT%d" % (blk % 2))])
        yb = buf("yt%d" % (blk % 2))
        ytb = yt[blk % 2]
        for hh in range(2):
            for ab in range(2):
                for r in range(2):
                    src = ysrc[ab][r * 256:(r + 1) * 256, hh * HALF + blk * 512:hh * HALF + (blk + 1) * 512].rearrange(
                        "(i p) t -> p i t", p=128)
                    c0_ = ab * 4 + r * 2
                    S.dma("sync", yh[hh][:, c0_:c0_ + 2, :], src, reads=[buf("ysrcA" if ab == 0 else "ysrcB")],
                          writes=[buf("yh%d" % hh)])
        S.op("vector", lambda e, ytb=ytb: e.tensor_scalar(ytb[:], yh[0][:], sel_t[:, 0:1], None, ALU.mult),
             [buf("yh0"), buf("sel")], [yb])
        S.op("vector", lambda e, ytb=ytb: e.scalar_tensor_tensor(ytb[:], yh[1][:], sel_t[:, 1:2], ytb[:], ALU.mult, ALU.add),
             [buf("yh1"), buf("sel"), yb], [yb])

    def gates(blk):
        hTb = hT[blk % 2]
        for fg in range(16):
            pb = fg % 2
            for kc in range(8):
                S.op("tensor", lambda e, fg=fg, kc=kc, pb=pb, hTb=hTb: e.matmul(
                    ps[:, pb, :], w2b[:, kc, fg * 128:(fg + 1) * 128], hTb[:, kc, :], start=(kc == 0), stop=(kc == 7)),
                    [buf("w2b"), buf("hT%d" % (blk % 2))], [bank[pb]])
            S.op("scalar", lambda e, fg=fg, pb=pb: e.activation(sg[:, fg, :], ps[:, pb, :], AF.Sigmoid),
                 [bank[pb]], [buf("sg")])

    def merge_out(blk):
        xs = xt[blk % 2]
        yb = buf("yt%d" % (blk % 2))
        ytb = yt[blk % 2]
        for dg in range(8):
            mb = 2 if dg % 2 == 0 else 0
            for ab in range(2):
                pb = mb + ab
                for kc in range(4):
                    S.op("tensor", lambda e, dg=dg, ab=ab, kc=kc, pb=pb, ytb=ytb: e.matmul(
                        ps[:, pb, :], wabb[:, ab * 4 + kc, dg * 128:(dg + 1) * 128], ytb[:, ab * 4 + kc, :],
                        start=(kc == 0), stop=(kc == 3)),
                        [buf("wabb"), yb], [bank[pb]])
            i = dg % 2
            S.op("vector", lambda e, dg=dg, i=i, mb=mb: e.tensor_tensor(t1[i][:], ps[:, mb, :], sg[:, dg, :], ALU.mult),
                 [bank[mb], buf("sg")], [buf("t1_%d" % i)])
            S.op("vector", lambda e, dg=dg, i=i, mb=mb: e.tensor_tensor(t2[i][:], ps[:, mb + 1, :], sg[:, 8 + dg, :], ALU.mult),
                 [bank[mb + 1], buf("sg")], [buf("t2_%d" % i)])
            S.op("gpsimd", lambda e, dg=dg, i=i: e.tensor_tensor(mT[:, dg, :], t1[i][:], t2[i][:], ALU.add),
                 [buf("t1_%d" % i), buf("t2_%d" % i)], [buf("mT")])
        for t in range(4):
            xb = buf("xt%d_%d" % (blk % 2, t))
            ob = 4 if t % 2 == 0 else 2
            for nb in range(2):
                for kc in range(8):
                    S.op("tensor", lambda e, t=t, nb=nb, kc=kc, ob=ob: e.matmul(
                        ps[:, ob + nb, :], mT[:, kc, t * 128:(t + 1) * 128], woutb[:, kc, nb * 512:(nb + 1) * 512],
                        start=(kc == 0), stop=(kc == 7)),
                        [buf("mT"), buf("woutb")], [bank[ob + nb]])
            k = ri[0] % 4
            ri[0] += 1
            po = ps[:, ob:ob + 2, :].rearrange("p a b -> p (a b)")
            S.op("scalar", lambda e, k=k, po=po: e.activation(junk[:], po, AF.Square, accum_out=ss[k]),
                 [bank[ob], bank[ob + 1]], [buf("junk"), buf("ss%d" % k)])
            S.op("scalar", lambda e, k=k: e.activation(rs[k], ss[k], AF.Sqrt, scale=1.0 / DM, bias=EPS),
                 [buf("ss%d" % k)], [buf("rs%d" % k)])
            S.op("vector", lambda e, k=k: e.reciprocal(rs[k], rs[k]), [buf("rs%d" % k)], [buf("rs%d" % k)])
            xi = t % 2
            S.op("vector", lambda e, k=k, xi=xi, po=po: e.scalar_tensor_tensor(
                xn[xi][:], po, rs[k], gpost_t[:], ALU.mult, ALU.mult),
                [bank[ob], bank[ob + 1], buf("rs%d" % k), buf("gpost")], [buf("xn%d" % xi)])
            S.op("gpsimd", lambda e, xi=xi, t=t, xs=xs: e.tensor_tensor(xn[xi][:], xn[xi][:], xs[t][:], ALU.add),
                 [buf("xn%d" % xi), xb], [buf("xn%d" % xi)])
            xdst_ap, xdst_buf = xo(blk * 4 + t)
            S.dma("gpsimd", xdst_ap, xn[xi][:], reads=[buf("xn%d" % xi)], writes=[buf(xdst_buf)])
        if T.get("after_blk") is not None:
            T["after_blk"](blk)

    prologue(0)
    for blk in range(4):
        gates(blk)
        if blk + 1 < 4:
            prologue(blk + 1)
        merge_out(blk)
    S.barrier()


class Arena:
    def __init__(self, t, nbytes):
        self.t = t
        self.n = nbytes
        self.off = 0

    def reset(self):
        self.off = 0

    def mark(self):
        return self.off

    def release(self, m):
        self.off = m

    def alloc(self, shape, dtype):
        esz = 4 if dtype == F32 else 2
        n = int(np.prod(shape)) * esz
        self.off = (self.off + 31) // 32 * 32
        a = self.off
        assert a + n <= self.n, ("arena overflow", a, n, self.n)
        self.off = a + n
        v = self.t[:, a // 2:(a + n) // 2]
        if dtype == F32:
            v = v.bitcast(F32)
        if len(shape) == 2:
            v = v.rearrange("p (a b) -> p a b", a=shape[0])
        elif len(shape) == 3:
            v = v.rearrange("p (a b c) -> p a b c", a=shape[0], b=shape[1])
        return v


SBUF_ARENA_BYTES = 210432


class Ctx:
    def __init__(self, nc, cst):
        self.nc = nc
        self.S = Sched(nc)
        sb = nc.alloc_sbuf_tensor
        self.cst_t = sb("cst_t", [128, 384], F32)
        self.idb = sb("idb", [128, 128], BF16)
        self.trib = sb("trib", [128, 128], BF16)
        self.mem = Arena(sb("arena", [128, SBUF_ARENA_BYTES // 2], BF16), SBUF_ARENA_BYTES)
        self.ps = nc.alloc_psum_tensor("ps", [128, 8, 512], F32)
        self.bank = [Buf("bank%d" % i, psum=True) for i in range(8)]
        self.pbuf = {}
        S = self.S
        S.dma("sync", self.cst_t[:], cst, writes=[self.pb("cst")])
        S.op("vector", lambda e: e.tensor_copy(self.idb[:], self.cst_t[:, 0:128]), [self.pb("cst")], [self.pb("idb")])
        S.op("vector", lambda e: e.tensor_copy(self.trib[:], self.cst_t[:, 128:256]), [self.pb("cst")], [self.pb("trib")])

    def pb(self, name):
        if name not in self.pbuf:
            self.pbuf[name] = Buf(name)
        return self.pbuf[name]


PERSIST = ("cst", "idb", "trib", "yA", "yB", "ysrcA", "ysrcB", "xin", "xo0", "xo1", "xo2", "xo3", "xin0", "xin1", "xin2", "xin3")


W1_COLS = 2308
ARENA_BYTES = 102 * 1024


def emit_p1(ctx, layer, T):
    lam_init = 0.8 - 0.6 * math.exp(-0.3 * layer)
    nc, S, ps, bank, mem = ctx.nc, ctx.S, ctx.ps, ctx.bank, ctx.mem
    x, w1, gpre, aug, abias, na, nb, convw, bif, lq, yTo = (T[k] for k in
                                                             ("x", "w1", "gpre", "aug", "abias", "na", "nb", "convw", "bif", "lq", "yTo"))
    cst_t, idb, trib = ctx.cst_t, ctx.idb, ctx.trib
    mem.reset()
    hT = mem.alloc([8, SEQ], BF16)
    wb = mem.alloc([8, 1284], BF16)
    gpre_t = mem.alloc([DM], F32)
    abias_t = mem.alloc([16], F32)
    na_t = mem.alloc([256], F32)
    nb_t = mem.alloc([256], F32)
    convw_t = mem.alloc([16], F32)
    bif_t = mem.alloc([8], F32)
    lq_t = mem.alloc([256], F32)
    sm = mem.alloc([64], F32)
    ss = [mem.alloc([8], F32) for i in range(4)]
    rs = [mem.alloc([8], F32) for i in range(4)]
    ar = mem
    stage_mark = mem.mark()
    B = {}
    tri_f = cst_t[:, 128:256]
    ones_f = cst_t[:, 256:384]
    xin_name = T.get("xin_name", "xin")

    def buf(name):
        if name in PERSIST:
            return ctx.pb(T.get(name + "_buf", name))
        if name not in B:
            B[name] = Buf(name)
        return B[name]

    def act(out, in_, func, reads, writes, **kw):
        return S.op("scalar", lambda e: e.activation(out, in_, func, **kw), reads, writes)

    def dve(method, *args, reads=(), writes=(), **kw):
        return S.op("vector", lambda e: getattr(e, method)(*args, **kw), reads, writes)

    def pool(method, *args, reads=(), writes=(), **kw):
        return S.op("gpsimd", lambda e: getattr(e, method)(*args, **kw), reads, writes)

    def mm(out, lhsT, rhs, reads, writes, **kw):
        return S.op("tensor", lambda e: e.matmul(out, lhsT, rhs, **kw), reads, writes)

    def tp(out, in_, reads, writes):
        return S.op("tensor", lambda e: e.transpose(out, in_, idb[:]), list(reads) + [buf("idb")], writes)

    for name, t, src in (("gpre", gpre_t, gpre), ("abias", abias_t, abias), ("na", na_t, na),
                         ("nb", nb_t, nb), ("convw", convw_t, convw), ("lq", lq_t, lq)):
        S.dma("sync", t[:], src, writes=[buf(name)])
    S.dma("sync", bif_t[:, 0:4], bif, writes=[buf("bif")])

    si = [0]

    def load_w(c0, ncols):
        for kc in range(8):
            S.dma("gpsimd", wb[:, kc, 0:ncols], w1[kc * 128:(kc + 1) * 128, c0:c0 + ncols], writes=[buf("wb")])

    load_w(0, 1024)

    ri = [0]

    def rstd_from(ssk, k, n):
        act(rs[k][:, 0:1], ssk, AF.Sqrt, [buf("ss%d" % k)], [buf("rs%d" % k)], scale=1.0 / n, bias=EPS)
        dve("reciprocal", rs[k][:, 0:1], rs[k][:, 0:1], reads=[buf("rs%d" % k)], writes=[buf("rs%d" % k)])

    mem.release(stage_mark)
    xt = [ar.alloc([DM], F32) for _ in range(3)]
    junk = ar.alloc([DM], BF16)
    hb = [ar.alloc([DM], BF16) for _ in range(2)]
    kk = {}

    pos = {}

    def st0_l(t):
        pos[t] = len(pos)
        i = pos[t] % 3
        xb = buf("xt%d" % i)
        xsrc_ap, xsrc_buf = x(t)
        S.dma("sync", xt[i], xsrc_ap, reads=[buf(xsrc_buf)], writes=[xb])
        k = ri[0] % 4
        ri[0] += 1
        kk[t] = k
        act(junk, xt[i], AF.Square, [xb], [buf("junk"), buf("ss%d" % k)], accum_out=ss[k][:, 0:1])
        rstd_from(ss[k][:, 0:1], k, DM)

    def st0_m(t):
        i, j, k = pos[t] % 3, pos[t] % 2, kk[t]
        dve("scalar_tensor_tensor", hb[j], xt[i], rs[k][:, 0:1], gpre_t[:], ALU.mult, ALU.mult,
            reads=[buf("xt%d" % i), buf("rs%d" % k), buf("gpre")], writes=[buf("hb%d" % j)])

    def st0_n(t):
        j = pos[t] % 2
        pb = 6 + j
        pT = ps[:, pb, 0:512].bitcast(BF16)
        for c in range(8):
            tp(pT[:, c * 128:(c + 1) * 128], hb[j][:, c * 128:(c + 1) * 128], [buf("hb%d" % j)], [bank[pb]])
        if pos[t] % 2 == 0:
            act(hT[:, :, t * 128:(t + 1) * 128], pT.rearrange("p (c t) -> p c t", c=8), AF.Copy, [bank[pb]], [buf("hT")])
        else:
            dve("tensor_copy", hT[:, :, t * 128:(t + 1) * 128], pT.rearrange("p (c t) -> p c t", c=8),
                reads=[bank[pb]], writes=[buf("hT")])

    order = list(range(0, 12)) + list(range(16, 28)) + list(range(12, 16)) + list(range(28, 32))
    for m in range(32 + 2):
        if m < 32:
            st0_l(order[m])
        if 1 <= m <= 32:
            st0_m(order[m - 1])
        if m >= 2:
            st0_n(order[m - 2])
    S.barrier()

    mem.release(stage_mark)
    QT = [ar.alloc([SEQ], BF16) for _ in range(2)]
    KT = [ar.alloc([SEQ], BF16) for _ in range(2)]
    Vaug = ar.alloc([32, 2, 129], BF16)
    gz = ar.alloc([32, 256], BF16)
    PT = [ar.alloc([2, 512], BF16) for _ in range(3)]
    yTs = ar.alloc([SEQ], BF16)
    otmp = [ar.alloc([128], F32) for _ in range(4)]
    ot = [ar.alloc([128], F32) for _ in range(2)]
    yab = [ar.alloc([128], BF16) for _ in range(4)]
    zt = [ar.alloc([256], F32) for _ in range(2)]
    junkf = ar.alloc([128], F32)
    for i in range(2):
        pool("memset", QT[i], 0.0, writes=[buf("QT%d" % i)])
        pool("memset", KT[i], 0.0, writes=[buf("KT%d" % i)])
    pool("memset", Vaug, 1.0, writes=[buf("Vaug")])
    lp = zt[0]
    dve("tensor_tensor", lp[:, 0:64], lq_t[:, 0:64], lq_t[:, 64:128], ALU.mult, reads=[buf("lq")], writes=[buf("zt0")])
    dve("tensor_tensor", lp[:, 64:128], lq_t[:, 128:192], lq_t[:, 192:256], ALU.mult, reads=[buf("lq")], writes=[buf("zt0")])
    dve("reduce_sum", sm[:, 1:2], lp[:, 0:64], AX.X, reads=[buf("zt0")], writes=[buf("sm1")])
    dve("reduce_sum", sm[:, 2:3], lp[:, 64:128], AX.X, reads=[buf("zt0")], writes=[buf("sm2")])
    act(sm[:, 1:2], sm[:, 1:2], AF.Exp, [buf("sm1")], [buf("sm1")])
    act(sm[:, 2:3], sm[:, 2:3], AF.Exp, [buf("sm2")], [buf("sm2")])
    dve("tensor_tensor", sm[:, 0:1], sm[:, 2:3], sm[:, 1:2], ALU.subtract, reads=[buf("sm1"), buf("sm2")], writes=[buf("neglam")])
    dve("tensor_scalar", sm[:, 0:1], sm[:, 0:1], -lam_init, None, ALU.add, reads=[buf("neglam")], writes=[buf("neglam")])
    neglam = sm[:, 0:1]

    pr = [0]

    def next_bank():
        b = pr[0] % 6
        pr[0] += 1
        return b

    for t in range(32):
        pb = next_bank()
        for kc in range(8):
            mm(ps[:, pb, :], hT[:, kc, t * 128:(t + 1) * 128], wb[:, kc, 512:1024], [buf("hT"), buf("wb")], [bank[pb]],
               start=(kc == 0), stop=(kc == 7))
        dve("tensor_copy", Vaug[:, t, :, 0:128], ps[:, pb, 0:256].rearrange("p (h e) -> p h e", h=2),
            reads=[bank[pb]], writes=[buf("Vaug")])
        zi = t % 2
        act(zt[zi], ps[:, pb, 256:512], AF.Silu, [bank[pb]], [buf("zt%d" % zi)])
        dve("scalar_tensor_tensor", gz[:, t, :], zt[zi], 1.0 - lam_init, na_t[:], ALU.mult, ALU.mult,
            reads=[buf("zt%d" % zi), buf("na")], writes=[buf("gz")])

    gcnt = [0]
    for hl in range(2):
        S.dma("sync", QT[0][64:68, :], aug[hl, 0], writes=[buf("QT0")])
        S.dma("sync", QT[1][0:4, :], aug[hl, 0], writes=[buf("QT1")])
        S.dma("sync", KT[0][64:68, :], aug[hl, 1], writes=[buf("KT0")])
        S.dma("sync", KT[1][0:4, :], aug[hl, 1], writes=[buf("KT1")])
        for blk in range(8):
            for qk in range(2):
                c0 = qk * 256 + hl * 128
                dst = QT if qk == 0 else KT
                nm = "QT" if qk == 0 else "KT"
                pb = next_bank()
                for kc in range(8):
                    mm(ps[:, pb, :], wb[:, kc, c0:c0 + 128], hT[:, kc, blk * 512:(blk + 1) * 512],
                       [buf("wb"), buf("hT")], [bank[pb]], start=(kc == 0), stop=(kc == 7))
                act(dst[0][0:64, blk * 512:(blk + 1) * 512], ps[0:64, pb, :], AF.Copy, [bank[pb]], [buf(nm + "0")])
                dve("tensor_copy", dst[1][64:128, blk * 512:(blk + 1) * 512], ps[64:128, pb, :],
                    reads=[bank[pb]], writes=[buf(nm + "1")])
        if hl == 1:
            load_w(1024, 1284)
        jobs = []
        for I in range(8):
            for c in range(2):
                groups = []
                for J in range(I):
                    for hf in range(2):
                        groups.append((False, I - J, [4 * J + 2 * hf, 4 * J + 2 * hf + 1]))
                for hf in range(2):
                    groups.append((True, 0, [4 * I + 2 * hf, 4 * I + 2 * hf + 1]))
                for gx, (diag, d, tiles) in enumerate(groups):
                    jobs.append((I, c, diag, d, tiles, gx == 0, gx == len(groups) - 1))
        NJ = len(jobs)
        pending_tp = []

        def rec_st(n):
            I, c, diag, d, tiles, first, last = jobs[n]
            gi = n % 3
            b0 = 2 * gi
            for jj, j in enumerate(tiles):
                n0 = 128 * (j - 4 * I) if diag else 0
                mm(ps[:, b0 + jj, n0:512], KT[c][:, j * 128:(j + 1) * 128], QT[c][:, I * 512 + n0:(I + 1) * 512],
                   [buf("KT%d" % c), buf("QT%d" % c)], [bank[b0 + jj]], start=True, stop=True)

        def rec_exp(n):
            I, c, diag, d, tiles, first, last = jobs[n]
            gi = n % 3
            b0 = 2 * gi
            ptb = buf("PT%d" % gi)
            bcol = abias_t[:, hl * 8 + d:hl * 8 + d + 1]
            if not diag:
                act(PT[gi], ps[:, b0:b0 + 2, :], AF.Exp, [bank[b0], bank[b0 + 1], buf("abias")], [ptb],
                    scale=0.125, bias=bcol)
            else:
                for jj, j in enumerate(tiles):
                    n0 = 128 * (j - 4 * I)
                    act(PT[gi][:, jj, n0:512], ps[:, b0 + jj, n0:512], AF.Exp, [bank[b0 + jj], buf("abias")], [ptb],
                        scale=0.125, bias=bcol)
                    dve("tensor_tensor", PT[gi][:, jj, n0:n0 + 128], PT[gi][:, jj, n0:n0 + 128], trib[:], ALU.mult,
                        reads=[ptb, buf("trib")], writes=[ptb])

        def rec_pv(n, m):
            I, c, diag, d, tiles, first, last = jobs[n]
            gi = n % 3
            b0 = 2 * gi
            ptb = buf("PT%d" % gi)
            for jj, j in enumerate(tiles):
                r = j - 4 * I
                us = range(r, 4) if diag else range(4)
                for u in us:
                    ob_ = 6 + u // 2
                    oc = (u % 2) * 129
                    mm(ps[:, ob_, oc:oc + 129], PT[gi][:, jj, u * 128:(u + 1) * 128], Vaug[:, j, hl, :],
                       [ptb, buf("Vaug")], [bank[ob_]],
                       start=(first and jj == 0 and u in (0, 2)), stop=(diag and r == u), skip_group_check=True)
            if not last:
                return
            for u in range(4):
                ob_ = 6 + u // 2
                oc = (u % 2) * 129
                O = ps[:, ob_, oc:oc + 128]
                Osum = ps[:, ob_, oc + 128:oc + 129]
                k = ri[0] % 4
                ri[0] += 1
                dve("reciprocal", rs[k][:, 0:1], Osum, reads=[bank[ob_]], writes=[buf("rs%d" % k)])
                if c == 0:
                    dve("tensor_scalar", otmp[u], O, rs[k][:, 0:1], None, ALU.mult,
                        reads=[bank[ob_], buf("rs%d" % k)], writes=[buf("otmp%d" % u)])
                else:
                    dve("tensor_tensor", rs[k][:, 0:1], rs[k][:, 0:1], neglam, ALU.mult,
                        reads=[buf("rs%d" % k), buf("neglam")], writes=[buf("rs%d" % k)])
                    oi = u % 2
                    dve("scalar_tensor_tensor", ot[oi], O, rs[k][:, 0:1], otmp[u], ALU.mult, ALU.add,
                        reads=[bank[ob_], buf("rs%d" % k), buf("otmp%d" % u)], writes=[buf("ot%d" % oi)])
                    k2 = ri[0] % 4
                    ri[0] += 1
                    dve("scalar_tensor_tensor", junkf, ot[oi], 1.0, ot[oi], ALU.mult, ALU.mult,
                        reads=[buf("ot%d" % oi)], writes=[buf("junkf"), buf("ss%d" % k2)], accum_out=ss[k2][:, 0:1])
                    act(rs[k2][:, 0:1], ss[k2][:, 0:1], AF.Ln, [buf("ss%d" % k2)], [buf("rs%d" % k2)], scale=1.0 / 128, bias=EPS)
                    act(rs[k2][:, 0:1], rs[k2][:, 0:1], AF.Exp, [buf("rs%d" % k2)], [buf("rs%d" % k2)], scale=-0.5)
                    tt = 4 * I + u
                    dve("scalar_tensor_tensor", yab[u], ot[oi], rs[k2][:, 0:1], gz[:, tt, hl * 128:(hl + 1) * 128],
                        ALU.mult, ALU.mult, reads=[buf("ot%d" % oi), buf("rs%d" % k2), buf("gz")], writes=[buf("yab%d" % u)])
                    pending_tp.append((m + 2, u, tt, b0))

        def flush_tp(m, force=False):
            while pending_tp and (force or pending_tp[0][0] <= m):
                _, u, tt, b0 = pending_tp.pop(0)
                pT = ps[:, b0, u * 64:(u + 1) * 64].bitcast(BF16)
                tp(pT, yab[u], [buf("yab%d" % u)], [bank[b0]])
                act(yTs[:, tt * 128:(tt + 1) * 128], pT, AF.Copy, [bank[b0]], [buf("yTs")])

        for m in range(NJ + 2):
            if m < NJ:
                rec_st(m)
            if 1 <= m <= NJ:
                rec_exp(m - 1)
            if m >= 2:
                rec_pv(m - 2, m)
            flush_tp(m)
        flush_tp(0, force=True)
        S.dma("gpsimd", yTo[0][hl * 128:(hl + 1) * 128, :], yTs, reads=[buf("yTs")], writes=[buf("yA")])
    if T.get("after_A") is not None:
        T["after_A"]()
    S.barrier()

    mem.release(stage_mark)
    qbT = [ar.alloc([SEQ], BF16) for _ in range(2)]
    kbT = [ar.alloc([SEQ], BF16) for _ in range(2)]
    vbaug = ar.alloc([32, 2, 129], BF16)
    sgo = ar.alloc([32, 256], BF16)
    gzb = ar.alloc([32, 256], BF16)
    yTb = [ar.alloc([1024], BF16) for _ in range(2)]
    xpre = [ar.alloc([515], F32) for _ in range(2)]
    ct = [ar.alloc([512], F32) for _ in range(2)]
    zt = [ar.alloc([256], F32) for _ in range(2)]
    gts = ar.alloc([32, 4], F32)
    IG = ar.alloc([2, 32], F32)
    LF = ar.alloc([2, 32], F32)
    EB = ar.alloc([64], F32)
    DEC = ar.alloc([64], F32)
    EK = ar.alloc([64], F32)
    SmT = [ar.alloc([128], BF16) for _ in range(4)]
    Ktm = [ar.alloc([128], BF16) for _ in range(4)]
    G = [ar.alloc([129], F32) for _ in range(2)]
    Cbf = [ar.alloc([129], BF16) for _ in range(2)]
    hbt = [ar.alloc([128], F32) for _ in range(2)]
    ybb = [ar.alloc([128], BF16) for _ in range(4)]
    dn = [ar.alloc([4], F32) for _ in range(2)]
    junkf3 = [ar.alloc([128], F32) for _ in range(2)]
    pool("memset", vbaug, 1.0, writes=[buf("vbaug")])
    for t in range(32):
        pb = next_bank()
        for kc in range(8):
            mm(ps[:, pb, :], hT[:, kc, t * 128:(t + 1) * 128], wb[:, kc, 512:1024], [buf("hT"), buf("wb")], [bank[pb]],
               start=(kc == 0), stop=(kc == 7))
        dve("tensor_copy", vbaug[:, t, :, 0:128], ps[:, pb, 0:256].rearrange("p (h e) -> p h e", h=2),
            reads=[bank[pb]], writes=[buf("vbaug")])
        act(sgo[:, t, :], ps[:, pb, 256:512], AF.Sigmoid, [bank[pb]], [buf("sgo")])
    for t in range(32):
        pb = next_bank()
        for kc in range(8):
            mm(ps[:, pb, 0:260], hT[:, kc, t * 128:(t + 1) * 128], wb[:, kc, 1024:1284], [buf("hT"), buf("wb")], [bank[pb]],
               start=(kc == 0), stop=(kc == 7))
        zi = t % 2
        act(zt[zi], ps[:, pb, 0:256], AF.Silu, [bank[pb]], [buf("zt%d" % zi)])
        dve("tensor_tensor", gzb[:, t, :], zt[zi], nb_t[:], ALU.mult, reads=[buf("zt%d" % zi), buf("nb")], writes=[buf("gzb")])
        dve("tensor_copy", gts[:, t, :], ps[:, pb, 256:260], reads=[bank[pb]], writes=[buf("gts")])
    for h in range(2):
        dve("tensor_scalar", IG[:, h, :], gts[:, :, h], bif_t[:, h:h + 1], None, ALU.add,
            reads=[buf("gts"), buf("bif")], writes=[buf("IG")])
        dve("tensor_scalar", LF[:, h, :], gts[:, :, 2 + h], bif_t[:, 2 + h:3 + h], None, ALU.add,
            reads=[buf("gts"), buf("bif")], writes=[buf("LF")])
    LFf = LF.rearrange("p a b -> p (a b)")
    IGf = IG.rearrange("p a b -> p (a b)")
    act(LFf, LFf, AF.Exp, [buf("LF")], [buf("LF")], scale=-1.0)
    act(LFf, LFf, AF.Ln, [buf("LF")], [buf("LF")], bias=1.0)
    dve("tensor_scalar", LFf, LFf, -1.0, None, ALU.mult, reads=[buf("LF")], writes=[buf("LF")])
    mm(ps[:, 6, 0:64], tri_f, LFf, [buf("cst"), buf("LF")], [bank[6]], start=True, stop=True)
    mm(ps[:, 7, 0:64], ones_f, LFf, [buf("cst"), buf("LF")], [bank[7]], start=True, stop=True)
    act(EB, ps[:, 6, 0:64], AF.Exp, [bank[6]], [buf("EB")])
    act(DEC, ps[:, 7, 0:64], AF.Exp, [bank[7]], [buf("DEC")])
    dve("tensor_tensor", EK, IGf, ps[:, 6, 0:64], ALU.subtract, reads=[buf("IG"), bank[6]], writes=[buf("EK")])
    act(EK, EK, AF.Exp, [buf("EK")], [buf("EK")], bias=math.log(128.0 ** -0.5))
    xi = [0]
    for grp in range(4):
        dst = (qbT if grp < 2 else kbT)[grp % 2]
        dnm = ("qbT" if grp < 2 else "kbT") + str(grp % 2)
        for blk in range(8):
            pb = next_bank()
            for kc in range(8):
                mm(ps[:, pb, :], wb[:, kc, grp * 128:(grp + 1) * 128], hT[:, kc, blk * 512:(blk + 1) * 512],
                   [buf("wb"), buf("hT")], [bank[pb]], start=(kc == 0), stop=(kc == 7))
            i = xi[0] % 2
            xi[0] += 1
            xp = xpre[i]
            xpb = buf("xpre%d" % i)
            if blk == 0:
                pool("memset", xp[:, 0:3], 0.0, writes=[xpb])
            else:
                pool("tensor_copy", xp[:, 0:3], xpre[1 - i][:, 512:515], reads=[buf("xpre%d" % (1 - i))], writes=[xpb])
            act(xp[:, 3:515], ps[:, pb, :], AF.Copy, [bank[pb]], [xpb])
            cw = convw_t[:, grp * 4:(grp + 1) * 4]
            dve("tensor_scalar", ct[0], xp[:, 0:512], cw[:, 0:1], None, ALU.mult, reads=[xpb, buf("convw")], writes=[buf("ct0")])
            dve("scalar_tensor_tensor", ct[1], xp[:, 1:513], cw[:, 1:2], ct[0], ALU.mult, ALU.add,
                reads=[xpb, buf("convw"), buf("ct0")], writes=[buf("ct1")])
            dve("scalar_tensor_tensor", ct[0], xp[:, 2:514], cw[:, 2:3], ct[1], ALU.mult, ALU.add,
                reads=[xpb, buf("convw"), buf("ct1")], writes=[buf("ct0")])
            dve("scalar_tensor_tensor", ct[1], xp[:, 3:515], cw[:, 3:4], ct[0], ALU.mult, ALU.add,
                reads=[xpb, buf("convw"), buf("ct0")], writes=[buf("ct1")])
            act(dst[:, blk * 512:(blk + 1) * 512], ct[1], AF.Silu, [buf("ct1")], [buf(dnm)])
    if T.get("prefetch") is not None:
        T["prefetch"]([buf("hT"), buf("wb")])
    def ph_x(c, h):
        col = h * 32 + c
        i4 = 2 * (c % 2) + h
        cs = slice(c * 128, (c + 1) * 128)
        qb_, kb_ = buf("qbT%d" % h), buf("kbT%d" % h)
        pTk = ps[:, h, 128:192].bitcast(BF16)
        return [
            lambda: mm(ps[:, h, 0:128], kbT[h][:, cs], qbT[h][:, cs], [qb_, kb_], [bank[h]], start=True, stop=True),
            lambda: tp(pTk, kbT[h][:, cs], [kb_], [bank[h]]),
            lambda: dve("scalar_tensor_tensor", SmT[i4], ps[:, h, 0:128], EK[:, col:col + 1], tri_f, ALU.mult, ALU.mult,
                        reads=[bank[h], buf("EK"), buf("cst")], writes=[buf("SmT%d" % i4)]),
            lambda: act(Ktm[i4], pTk, AF.Copy, [bank[h], buf("EK")], [buf("Ktm%d" % i4)], scale=EK[:, col:col + 1]),
        ]

    def ph_y(c, h):
        col = h * 32 + c
        i4 = 2 * (c % 2) + h
        hb_ = 2 + 2 * (c % 2) + h
        ub_ = 6 + h
        cs = slice(c * 128, (c + 1) * 128)
        qb_ = buf("qbT%d" % h)
        ops = [lambda: mm(ps[:, hb_, 0:129], SmT[i4], vbaug[:, c, h, :], [buf("SmT%d" % i4), buf("vbaug")], [bank[hb_]],
                          start=True, stop=(c == 0))]
        if c > 0:
            ops.append(lambda: mm(ps[:, hb_, 0:129], qbT[h][:, cs], Cbf[h], [qb_, buf("Cbf%d" % h)], [bank[hb_]],
                                  start=False, stop=True))
        ops.append(lambda: mm(ps[:, ub_, 0:129], Ktm[i4], vbaug[:, c, h, :], [buf("Ktm%d" % i4), buf("vbaug")], [bank[ub_]],
                              start=True, stop=True))
        if c == 0:
            ops.append(lambda: dve("tensor_copy", G[h], ps[:, ub_, 0:129], reads=[bank[ub_]], writes=[buf("G%d" % h)]))
        else:
            ops.append(lambda: dve("scalar_tensor_tensor", G[h], G[h], DEC[:, col - 1:col], ps[:, ub_, 0:129], ALU.mult, ALU.add,
                                   reads=[buf("G%d" % h), buf("DEC"), bank[ub_]], writes=[buf("G%d" % h)]))
        if c < 31:
            ops.append(lambda: act(Cbf[h], G[h], AF.Copy, [buf("G%d" % h), buf("DEC")], [buf("Cbf%d" % h)],
                                   scale=DEC[:, col:col + 1]))
        return ops

    def ph_z1(c, h):
        col = h * 32 + c
        i4 = 2 * (c % 2) + h
        hb_ = 2 + 2 * (c % 2) + h
        dnb = buf("dn%d" % h)
        d0, d1, d2 = dn[h][:, 0:1], dn[h][:, 1:2], dn[h][:, 2:3]
        k = ri[0] % 4
        ri[0] += 1
        ssb, rsb = buf("ss%d" % k), buf("rs%d" % k)
        return [
            lambda: dve("tensor_tensor", d0, ps[:, hb_, 128:129], EB[:, col:col + 1], ALU.mult,
                        reads=[bank[hb_], buf("EB")], writes=[dnb]),
            lambda: dve("tensor_scalar", d1, d0, -1.0, 1.0, ALU.mult, ALU.max, reads=[dnb], writes=[dnb]),
            lambda: dve("tensor_tensor", d0, d0, d1, ALU.max, reads=[dnb], writes=[dnb]),
            lambda: dve("reciprocal", d1, d0, reads=[dnb], writes=[dnb]),
            lambda: dve("tensor_tensor", d2, d1, EB[:, col:col + 1], ALU.mult, reads=[dnb, buf("EB")], writes=[dnb]),
            lambda: dve("scalar_tensor_tensor", hbt[h], ps[:, hb_, 0:128], d2, sgo[:, c, h * 128:(h + 1) * 128],
                        ALU.mult, ALU.mult, reads=[bank[hb_], dnb, buf("sgo")], writes=[buf("hbt%d" % h)]),
            lambda: dve("scalar_tensor_tensor", junkf3[h], hbt[h], 1.0, hbt[h], ALU.mult, ALU.mult,
                        reads=[buf("hbt%d" % h)], writes=[buf("junkf3_%d" % h), ssb], accum_out=ss[k][:, 0:1]),
            lambda: act(rs[k][:, 0:1], ss[k][:, 0:1], AF.Ln, [ssb], [rsb], scale=1.0 / 128, bias=EPS),
            lambda: act(rs[k][:, 0:1], rs[k][:, 0:1], AF.Exp, [rsb], [rsb], scale=-0.5),
            lambda: dve("scalar_tensor_tensor", ybb[i4], hbt[h], rs[k][:, 0:1], gzb[:, c, h * 128:(h + 1) * 128],
                        ALU.mult, ALU.mult, reads=[buf("hbt%d" % h), rsb, buf("gzb")], writes=[buf("ybb%d" % i4)]),
        ]

    def ph_z2(c, h):
        i4 = 2 * (c % 2) + h
        pTy = ps[:, h, 192:256].bitcast(BF16)
        c8 = c % 8
        ops = [lambda: tp(pTy, ybb[i4], [buf("ybb%d" % i4)], [bank[h]]),
               lambda: act(yTb[h][:, c8 * 128:(c8 + 1) * 128], pTy, AF.Copy, [bank[h]], [buf("yTb%d" % h)])]
        if c8 == 7:
            ops.append(lambda: S.dma("gpsimd", yTo[1][h * 128:(h + 1) * 128, (c - 7) * 128:(c + 1) * 128], yTb[h],
                                     reads=[buf("yTb%d" % h)], writes=[buf("yB")]))
        return ops

    def interleave(fn, c):
        a_, b_ = fn(c, 0), fn(c, 1)
        for j in range(max(len(a_), len(b_))):
            if j < len(a_):
                a_[j]()
            if j < len(b_):
                b_[j]()

    for q in range(32 + 3):
        if q < 32:
            interleave(ph_x, q)
        if 1 <= q <= 32:
            interleave(ph_y, q - 1)
        if 2 <= q <= 33:
            interleave(ph_z1, q - 2)
        if q >= 3:
            interleave(ph_z2, q - 3)
    S.barrier()


_SLOPES = [2.0 ** (-8.0 * (h + 1) / 4) for h in range(4)]
_CST = None


def _p1_inputs(inp, l, g, x_full):
    w_in = inp["w_in"][l]
    sl = slice(g * 256, (g + 1) * 256)
    parts = [w_in[:, 0:512][:, sl], w_in[:, 512:1024][:, sl], w_in[:, 1024:1536][:, sl], w_in[:, 1536:2048][:, sl],
             w_in[:, 2048:2560][:, sl], w_in[:, 2560:3072][:, sl], w_in[:, 3072:3584][:, sl], w_in[:, 3584:4096][:, sl],
             w_in[:, 4096:4608][:, sl], w_in[:, 4608 + 2 * g:4610 + 2 * g], w_in[:, 4612 + 2 * g:4614 + 2 * g]]
    w1 = np.ascontiguousarray(np.concatenate(parts, axis=1), dtype=np.float32)
    assert w1.shape == (DM, W1_COLS)
    v = np.arange(SEQ) % 512
    hi = ((v // 16) * 16).astype(np.float32)
    lo = (v % 16).astype(np.float32)
    one = np.ones(SEQ, np.float32)
    aug = np.zeros((2, 2, 4, SEQ), np.float32)
    abias = np.zeros((128, 16), np.float32)
    for hl in range(2):
        s = _SLOPES[2 * g + hl]
        aug[hl, 0] = np.stack([-s * 8.0 * hi, -s * 8.0 * lo, one, one])
        aug[hl, 1] = np.stack([one, one, s * 8.0 * hi, s * 8.0 * lo])
        for d in range(8):
            abias[:, hl * 8 + d] = -s * 512.0 * d
    cq = inp["conv_qk"][l]
    convw = np.zeros((128, 16), np.float32)
    for grp in range(4):
        ch0 = (0 if grp < 2 else 512) + (2 * g + grp % 2) * 128
        convw[:, grp * 4:(grp + 1) * 4] = cq[:, ch0:ch0 + 128].T
    bi = inp["b_if"][l]
    bif = np.broadcast_to(np.array([bi[2 * g], bi[2 * g + 1], bi[4 + 2 * g], bi[5 + 2 * g]], np.float32), (128, 4))
    return {
        "w1": w1,
        "gpre": np.ascontiguousarray(np.broadcast_to(inp["norm_pre"][l], (128, DM))),
        "cst": _consts_f32(),
        "aug": aug.astype(ml_dtypes.bfloat16),
        "abias": abias,
        "na": np.ascontiguousarray(np.broadcast_to(inp["norm_a"][l][sl], (128, 256))),
        "nb": np.ascontiguousarray(np.broadcast_to(inp["norm_b"][l][sl], (128, 256))),
        "convw": convw,
        "bif": np.ascontiguousarray(bif),
        "lq": np.ascontiguousarray(np.broadcast_to(inp["lambda_qk"][l].reshape(256), (128, 256))),
    }


def _host_maps(inp):
    x = np.asarray(inp["x"], dtype=np.float32)
    maps = []
    for c in range(8):
        b, g = c // 2, c % 2
        m = {"x": np.ascontiguousarray(x[b]), "xown": np.ascontiguousarray(x[b, g * HALF:(g + 1) * HALF]),
             "cst": _consts_f32()}
        selv = np.zeros((128, 2), np.float32)
        selv[:, g] = 1.0
        m["sel"] = selv
        for l in range(2):
            p1 = _p1_inputs(inp, l, g, None)
            for k in ("w1", "gpre", "na", "nb", "convw", "bif", "lq"):
                m["%s_%d" % (k, l)] = p1[k]
            if l == 0:
                m["aug"] = p1["aug"]
                m["abias"] = p1["abias"]
            w_in = inp["w_in"][l]
            m["w2_%d" % l] = np.ascontiguousarray(w_in[:, 4616:6664], dtype=np.float32)
            m["wab_%d" % l] = np.ascontiguousarray(np.concatenate([inp["w_a"][l], inp["w_b"][l]], axis=0), dtype=np.float32)
            m["wout_%d" % l] = np.ascontiguousarray(inp["w_out"][l], dtype=np.float32)
            m["gpost_%d" % l] = np.ascontiguousarray(np.broadcast_to(inp["norm_post"][l], (128, DM)), dtype=np.float32)
        maps.append(m)
    return maps


PAIRS = [[0, 1], [2, 3], [4, 5], [6, 7]]


def build_fused():
    nc = bass.Bass("TRN2", target_bir_lowering=False)

    def ext(name, shape, dt=F32):
        return nc.dram_tensor(name, shape, dt, kind="ExternalInput").ap()

    x = ext("x", [SEQ, DM])
    xown = ext("xown", [HALF, DM])
    cst = ext("cst", [128, 384])
    sel = ext("sel", [128, 2])
    aug = ext("aug", [2, 2, 4, SEQ], BF16)
    abias = ext("abias", [128, 16])
    L = []
    for l in range(2):
        L.append({
            "w1": ext("w1_%d" % l, [DM, W1_COLS]), "gpre": ext("gpre_%d" % l, [128, DM]),
            "na": ext("na_%d" % l, [128, 256]), "nb": ext("nb_%d" % l, [128, 256]),
            "convw": ext("convw_%d" % l, [128, 16]), "bif": ext("bif_%d" % l, [128, 4]),
            "lq": ext("lq_%d" % l, [128, 256]), "w2": ext("w2_%d" % l, [DM, 2048]),
            "wab": ext("wab_%d" % l, [1024, DM]), "wout": ext("wout_%d" % l, [DM, DM]),
            "gpost": ext("gpost_%d" % l, [128, DM]), "aug": aug, "abias": abias, "sel": sel,
        })
    out = nc.dram_tensor("out", [HALF, DM], F32, kind="ExternalOutput").ap()
    ysend = [nc.dram_tensor("ysend%d" % k, [256, SEQ], BF16) for k in range(2)]
    yrecv = [nc.dram_tensor("yrecv%d" % k, [512, SEQ], BF16) for k in range(2)]
    xnew = [nc.dram_tensor("xnew%d" % k, [512, DM], F32) for k in range(4)]
    xg = [nc.dram_tensor("xg%d" % k, [1024, DM], F32) for k in range(4)]
    ctx = Ctx(nc, cst)
    S = ctx.S

    def gather(src, dst, sbuf, dbuf):
        S.op("gpsimd", lambda e: e.collective_compute("AllGather", ALU.bypass, replica_groups=PAIRS,
                                                       ins=[src.ap().opt()], outs=[dst.ap().opt()]),
             [ctx.pb(sbuf)], [ctx.pb(dbuf)], dma=True, cc=True)

    for l in range(2):
        T1 = dict(L[l])
        if l == 0:
            T1["x"] = lambda t: (x[t * 128:(t + 1) * 128, :], "xin")
        else:
            def xsrc1(t):
                hh, tt = t // 16, t % 16
                k, r = tt // 4, tt % 4
                return xg[k].ap()[hh * 512 + r * 128:hh * 512 + (r + 1) * 128, :], "xin%d" % k
            T1["x"] = xsrc1
            for k in range(4):
                T1["xin%d_buf" % k] = "xg%d" % k
        T1.update({"yTo": [ysend[0].ap(), ysend[1].ap()], "yA_buf": "ysend0", "yB_buf": "ysend1",
                   "after_A": lambda: gather(ysend[0], yrecv[0], "ysend0", "yrecv0")})
        def prefetch(dead_bufs, l=l):
            tmp = Arena(ctx.mem.t, ctx.mem.n)
            w2b = tmp.alloc([8, 2048], BF16)
            wabb = tmp.alloc([8, 1024], BF16)
            woutb = tmp.alloc([8, 1024], BF16)
            p2_weight_dmas(S, w2b, wabb, woutb, L[l]["w2"], L[l]["wab"], L[l]["wout"], dead_bufs)
        T1["prefetch"] = prefetch
        emit_p1(ctx, l, T1)
        gather(ysend[1], yrecv[1], "ysend1", "yrecv1")
        T2 = dict(L[l])
        T2.update({"ysrc": [yrecv[0].ap(), yrecv[1].ap()], "ysrcA_buf": "yrecv0", "ysrcB_buf": "yrecv1",
                   "w_prefetched": True})
        if l == 0:
            T2["x"] = lambda t: (xown[t * 128:(t + 1) * 128, :], "xin")
            T2["xo"] = lambda t: (xnew[t // 4].ap()[(t % 4) * 128:(t % 4 + 1) * 128, :], "xo%d" % (t // 4))
            for k in range(4):
                T2["xo%d_buf" % k] = "xnew%d" % k
            T2["after_blk"] = lambda blk: gather(xnew[blk], xg[blk], "xnew%d" % blk, "xg%d" % blk)
        else:
            T2["x"] = lambda t: (xnew[t // 4].ap()[(t % 4) * 128:(t % 4 + 1) * 128, :], "xin%d" % (t // 4))
            for k in range(4):
                T2["xin%d_buf" % k] = "xnew%d" % k
            T2["xo"] = lambda t: (out[t * 128:(t + 1) * 128, :], "xo0")
            T2["xo0_buf"] = "out"
        emit_p2(ctx, T2)
    S.emit()
    return nc


def kernel(**inputs):
    inp = {k: np.asarray(v) for k, v in inputs.items()}
    maps = _host_maps(inp)
    nc = build_fused()
    res = run_bass_kernel_spmd(nc, maps, core_ids=list(range(8))).results
    out = np.empty((4, SEQ, DM), np.float32)
    for c in range(8):
        b, g = c // 2, c % 2
        out[b, g * HALF:(g + 1) * HALF] = res[c]["out"]
    return out
```

```python
import math
import contextlib
import ml_dtypes
import numpy as np
import concourse.bass as bass
import concourse.mybir as mybir
from concourse.bass_utils import run_bass_kernel_spmd

F32 = mybir.dt.float32
BF16 = mybir.dt.bfloat16
AF = mybir.ActivationFunctionType
ALU = mybir.AluOpType
AX = mybir.AxisListType


class Buf:
    __slots__ = ("name", "w", "r", "psum")

    def __init__(self, name="", psum=False):
        self.name = name
        self.w = []
        self.r = []
        self.psum = psum


class Op:
    __slots__ = ("eng", "fn", "deps", "is_dma", "need_inc", "cnt", "slot", "val", "pos", "is_cc")


ENGS = ("sync", "scalar", "vector", "gpsimd", "tensor")
NSLOT = 12


class Sched:
    def __init__(self, nc):
        self.nc = nc
        self.q = {e: [] for e in ENGS}
        self.ndma = {e: 0 for e in ENGS}

    def op(self, eng, fn, reads=(), writes=(), dma=False, cc=False):
        o = Op()
        o.is_cc = cc
        o.eng = eng
        o.fn = fn
        o.is_dma = dma
        o.need_inc = False
        o.cnt = 0
        o.deps = set()
        for b in reads:
            for w in b.w:
                o.deps.add(w)
            if b.psum:
                for r in b.r:
                    if r.eng != eng:
                        o.deps.add(r)
        for b in writes:
            append = dma and (not cc) and len(b.w) > 0 and (not b.r) and all(w.is_dma and not w.is_cc for w in b.w)
            if not append:
                for w in b.w:
                    o.deps.add(w)
            else:
                for w in b.w:
                    o.deps.update(d for d in w.deps)
            for r in b.r:
                o.deps.add(r)
        o.deps.discard(o)
        for b in reads:
            if b not in writes:
                b.r.append(o)
        for b in writes:
            append = dma and (not cc) and len(b.w) > 0 and (not b.r) and all(w.is_dma and not w.is_cc for w in b.w)
            if append:
                b.w.append(o)
            else:
                b.w = [o]
            b.r = []
        if cc:
            assert dma and eng == "gpsimd"
            self.ncc = getattr(self, "ncc", 0) + 1
            o.slot = -1
            o.val = self.ncc
        elif dma:
            n = self.ndma[eng]
            self.ndma[eng] = n + 1
            o.slot = n % NSLOT
            o.val = 16 * (n // NSLOT + 1)
        o.pos = len(self.q[eng])
        self.q[eng].append(o)
        return o

    def dma(self, eng, out, in_, reads=(), writes=(), **kw):
        return self.op(eng, lambda e: e.dma_start(out=out, in_=in_, **kw), reads, writes, dma=True)

    def emit(self):
        nc = self.nc
        for e in ENGS:
            for o in self.q[e]:
                for d in o.deps:
                    if d.is_dma:
                        continue
                    if d.eng == "tensor" and o.eng == "tensor":
                        continue
                    d.need_inc = True
        for e in ENGS:
            c = 0
            for o in self.q[e]:
                if (not o.is_dma) and o.need_inc:
                    c += 1
                    o.cnt = c
        import contextlib
        with contextlib.ExitStack() as st:
            esem = {e: st.enter_context(nc.semaphore("s_" + e)) for e in ENGS}
            dsem = {e: [st.enter_context(nc.semaphore("d_%s_%d" % (e, i))) for i in range(NSLOT)]
                    for e in ENGS if self.ndma[e] > 0}
            ccsem = st.enter_context(nc.semaphore("s_cc"))
            block = st.enter_context(nc.Block())

            def run(ename, eng):
                waited = {}

                def wait(sem, val):
                    k = sem.name
                    if waited.get(k, 0) >= val:
                        return
                    waited[k] = val
                    eng.wait_ge(sem, val)

                for o in self.q[ename]:
                    need = {}
                    for d in o.deps:
                        if d.is_cc:
                            sem, val = ccsem, d.val
                        elif d.is_dma:
                            sem, val = dsem[d.eng][d.slot], d.val
                        else:
                            if d.eng == "tensor" and ename == "tensor":
                                continue
                            sem, val = esem[d.eng], d.cnt
                        if sem.name not in need or need[sem.name][1] < val:
                            need[sem.name] = (sem, val)
                    for nm in sorted(need):
                        wait(*need[nm])
                    if o.is_cc:
                        o.fn(eng).then_inc(ccsem, 1)
                    elif o.is_dma:
                        if o.val > 16:
                            wait(dsem[ename][o.slot], o.val - 16)
                        o.fn(eng).then_inc(dsem[ename][o.slot], 16)
                    else:
                        ins = o.fn(eng)
                        if o.need_inc:
                            ins.then_inc(esem[ename], 1)
                n = self.ndma[ename]
                for s in range(min(n, NSLOT)):
                    last = ((n - 1 - s) // NSLOT) * NSLOT + s
                    wait(dsem[ename][s], 16 * (last // NSLOT + 1))

            @block.sync
            def _(eng):
                run("sync", eng)

            @block.scalar
            def _(eng):
                run("scalar", eng)

            @block.vector
            def _(eng):
                run("vector", eng)

            @block.gpsimd
            def _(eng):
                run("gpsimd", eng)

            @block.tensor
            def _(eng):
                run("tensor", eng)

    def barrier(self):
        last = []
        for e in ENGS:
            seen = set()
            got_c = False
            got_cc = False
            for o in reversed(self.q[e]):
                if o.is_cc:
                    continue
                elif o.is_dma:
                    if o.slot not in seen:
                        seen.add(o.slot)
                        last.append(o)
                elif not got_c:
                    last.append(o)
                    got_c = True
                if got_c and len(seen) >= NSLOT:
                    break
        for e in ENGS:
            o = self.op(e, lambda eng: eng.nop(), (), ())
            o.deps = set(last)


EPS = 1e-6
SEQ = 4096
DM = 1024
HALF = 2048


def _consts_f32():
    ident = np.eye(128, dtype=np.float32)
    tri = np.triu(np.ones((128, 128), dtype=np.float32))
    ones = np.ones((128, 128), dtype=np.float32)
    return np.concatenate([ident, tri, ones], axis=1)


def p2_weight_dmas(S, w2b, wabb, woutb, w2, wab, wout, wbufs):
    for kc in range(8):
        for c0 in range(0, 2048, 1024):
            S.dma("gpsimd", w2b[:, kc, c0:c0 + 1024], w2[kc * 128:(kc + 1) * 128, c0:c0 + 1024], writes=wbufs)
    for kc in range(8):
        S.dma("gpsimd", wabb[:, kc, :], wab[kc * 128:(kc + 1) * 128, :], writes=wbufs)
    for kc in range(8):
        S.dma("gpsimd", woutb[:, kc, :], wout[kc * 128:(kc + 1) * 128, :], writes=wbufs)


def emit_p2(ctx, T):
    nc, S, ps, bank, mem = ctx.nc, ctx.S, ctx.ps, ctx.bank, ctx.mem
    x, ysrc, sel, w2, wab, wout, gpre, gpost, xo = (T[k] for k in
                                                     ("x", "ysrc", "sel", "w2", "wab", "wout", "gpre", "gpost", "xo"))
    cst_t, idb = ctx.cst_t, ctx.idb
    mem.reset()
    w2b = mem.alloc([8, 2048], BF16)
    wabb = mem.alloc([8, 1024], BF16)
    woutb = mem.alloc([8, 1024], BF16)
    gpre_t = mem.alloc([DM], F32)
    gpost_t = mem.alloc([DM], F32)
    sel_t = mem.alloc([8], F32)
    xt = [[mem.alloc([DM], F32) for t in range(4)] for i in range(2)]
    junk = mem.alloc([DM], BF16)
    hb = [mem.alloc([DM], BF16) for i in range(2)]
    hT = [mem.alloc([8, 512], BF16) for i in range(2)]
    yt = [mem.alloc([8, 512], BF16) for i in range(2)]
    yh = [mem.alloc([8, 512], BF16) for i in range(2)]
    sg = mem.alloc([16, 512], BF16)
    t1 = [mem.alloc([512], F32) for i in range(2)]
    t2 = [mem.alloc([512], F32) for i in range(2)]
    mT = mem.alloc([8, 512], BF16)
    xn = [mem.alloc([DM], F32) for i in range(2)]
    ss = [mem.alloc([8], F32)[:, 0:1] for i in range(4)]
    rs = [mem.alloc([8], F32)[:, 0:1] for i in range(4)]
    B = {}

    def buf(name):
        if name in PERSIST:
            return ctx.pb(T.get(name + "_buf", name))
        if name not in B:
            B[name] = Buf(name)
        return B[name]

    S.dma("sync", gpre_t[:], gpre, writes=[buf("gpre")])
    S.dma("sync", gpost_t[:], gpost, writes=[buf("gpost")])
    S.dma("sync", sel_t[:, 0:2], sel, writes=[buf("sel")])

    if not T.get("w_prefetched"):
        p2_weight_dmas(S, w2b, wabb, woutb, w2, wab, wout, [buf("w2b"), buf("wabb"), buf("woutb")])

    ri = [0]

    def prologue(blk):
        xs = xt[blk % 2]
        hTb = hT[blk % 2]
        for t in range(4):
            xb = buf("xt%d_%d" % (blk % 2, t))
            xsrc_ap, xsrc_buf = x(blk * 4 + t)
            S.dma("sync", xs[t][:], xsrc_ap, reads=[buf(xsrc_buf)], writes=[xb])
            k = ri[0] % 4
            ri[0] += 1
            S.op("scalar", lambda e, t=t, k=k, xs=xs: e.activation(junk[:], xs[t][:], AF.Square, accum_out=ss[k]),
                 [xb], [buf("junk"), buf("ss%d" % k)])
            S.op("scalar", lambda e, k=k: e.activation(rs[k], ss[k], AF.Sqrt, scale=1.0 / DM, bias=EPS),
                 [buf("ss%d" % k)], [buf("rs%d" % k)])
            S.op("vector", lambda e, k=k: e.reciprocal(rs[k], rs[k]), [buf("rs%d" % k)], [buf("rs%d" % k)])
            hbi = t % 2
            S.op("vector", lambda e, t=t, k=k, xs=xs, hbi=hbi: e.scalar_tensor_tensor(
                hb[hbi][:], xs[t][:], rs[k], gpre_t[:], ALU.mult, ALU.mult),
                [xb, buf("rs%d" % k), buf("gpre")], [buf("hb%d" % hbi)])
            pb = 6 + (t % 2)
            pT = ps[:, pb, 0:512].bitcast(BF16)
            for c in range(8):
                S.op("tensor", lambda e, c=c, hbi=hbi, pT=pT: e.transpose(
                    pT[:, c * 128:(c + 1) * 128], hb[hbi][:, c * 128:(c + 1) * 128], idb[:]),
                    [buf("hb%d" % hbi), buf("idb")], [bank[pb]])
            S.op("scalar", lambda e, t=t, pT=pT, hTb=hTb: e.activation(
                hTb[:, :, t * 128:(t + 1) * 128], pT.rearrange("p (c t) -> p c t", c=8), AF.Copy),
                [bank[pb]], [buf("hT%d" % (blk % 2))])
        yb = buf("yt%d" % (blk % 2))
        ytb = yt[blk % 2]
        for hh in range(2):
            for ab in range(2):
                for r in range(2):
                    src = ysrc[ab][r * 256:(r + 1) * 256, hh * HALF + blk * 512:hh * HALF + (blk + 1) * 512].rearrange(
                        "(i p) t -> p i t", p=128)
                    c0_ = ab * 4 + r * 2
                    S.dma("sync", yh[hh][:, c0_:c0_ + 2, :], src, reads=[buf("ysrcA" if ab == 0 else "ysrcB")],
                          writes=[buf("yh%d" % hh)])
        S.op("vector", lambda e, ytb=ytb: e.tensor_scalar(ytb[:], yh[0][:], sel_t[:, 0:1], None, ALU.mult),
             [buf("yh0"), buf("sel")], [yb])
        S.op("vector", lambda e, ytb=ytb: e.scalar_tensor_tensor(ytb[:], yh[1][:], sel_t[:, 1:2], ytb[:], ALU.mult, ALU.add),
             [buf("yh1"), buf("sel"), yb], [yb])

    def gates(blk):
        hTb = hT[blk % 2]
        for fg in range(16):
            pb = fg % 2
            for kc in range(8):
                S.op("tensor", lambda e, fg=fg, kc=kc, pb=pb, hTb=hTb: e.matmul(
                    ps[:, pb, :], w2b[:, kc, fg * 128:(fg + 1) * 128], hTb[:, kc, :], start=(kc == 0), stop=(kc == 7)),
                    [buf("w2b"), buf("hT%d" % (blk % 2))], [bank[pb]])
            S.op("scalar", lambda e, fg=fg, pb=pb: e.activation(sg[:, fg, :], ps[:, pb, :], AF.Sigmoid),
                 [bank[pb]], [buf("sg")])

    def merge_out(blk):
        xs = xt[blk % 2]
        yb = buf("yt%d" % (blk % 2))
        ytb = yt[blk % 2]
        for dg in range(8):
            mb = 2 if dg % 2 == 0 else 0
            for ab in range(2):
                pb = mb + ab
                for kc in range(4):
                    S.op("tensor", lambda e, dg=dg, ab=ab, kc=kc, pb=pb, ytb=ytb: e.matmul(
                        ps[:, pb, :], wabb[:, ab * 4 + kc, dg * 128:(dg + 1) * 128], ytb[:, ab * 4 + kc, :],
                        start=(kc == 0), stop=(kc == 3)),
                        [buf("wabb"), yb], [bank[pb]])
            i = dg % 2
            S.op("vector", lambda e, dg=dg, i=i, mb=mb: e.tensor_tensor(t1[i][:], ps[:, mb, :], sg[:, dg, :], ALU.mult),
                 [bank[mb], buf("sg")], [buf("t1_%d" % i)])
            S.op("vector", lambda e, dg=dg, i=i, mb=mb: e.tensor_tensor(t2[i][:], ps[:, mb + 1, :], sg[:, 8 + dg, :], ALU.mult),
                 [bank[mb + 1], buf("sg")], [buf("t2_%d" % i)])
            S.op("gpsimd", lambda e, dg=dg, i=i: e.tensor_tensor(mT[:, dg, :], t1[i][:], t2[i][:], ALU.add),
                 [buf("t1_%d" % i), buf("t2_%d" % i)], [buf("mT")])
        for t in range(4):
            xb = buf("xt%d_%d" % (blk % 2, t))
            ob = 4 if t % 2 == 0 else 2
            for nb in range(2):
                for kc in range(8):
                    S.op("tensor", lambda e, t=t, nb=nb, kc=kc, ob=ob: e.matmul(
                        ps[:, ob + nb, :], mT[:, kc, t * 128:(t + 1) * 128], woutb[:, kc, nb * 512:(nb + 1) * 512],
                        start=(kc == 0), stop=(kc == 7)),
                        [buf("mT"), buf("woutb")], [bank[ob + nb]])
            k = ri[0] % 4
            ri[0] += 1
            po = ps[:, ob:ob + 2, :].rearrange("p a b -> p (a b)")
            S.op("scalar", lambda e, k=k, po=po: e.activation(junk[:], po, AF.Square, accum_out=ss[k]),
                 [bank[ob], bank[ob + 1]], [buf("junk"), buf("ss%d" % k)])
            S.op("scalar", lambda e, k=k: e.activation(rs[k], ss[k], AF.Sqrt, scale=1.0 / DM, bias=EPS),
                 [buf("ss%d" % k)], [buf("rs%d" % k)])
            S.op("vector", lambda e, k=k: e.reciprocal(rs[k], rs[k]), [buf("rs%d" % k)], [buf("rs%d" % k)])
            xi = t % 2
            S.op("vector", lambda e, k=k, xi=xi, po=po: e.scalar_tensor_tensor(
                xn[xi][:], po, rs[k], gpost_t[:], ALU.mult, ALU.mult),
                [bank[ob], bank[ob + 1], buf("rs%d" % k), buf("gpost")], [buf("xn%d" % xi)])
            S.op("gpsimd", lambda e, xi=xi, t=t, xs=xs: e.tensor_tensor(xn[xi][:], xn[xi][:], xs[t][:], ALU.add),
                 [buf("xn%d" % xi), xb], [buf("xn%d" % xi)])
            xdst_ap, xdst_buf = xo(blk * 4 + t)
            S.dma("gpsimd", xdst_ap, xn[xi][:], reads=[buf("xn%d" % xi)], writes=[buf(xdst_buf)])
        if T.get("after_blk") is not None:
            T["after_blk"](blk)

    prologue(0)
    for blk in range(4):
        gates(blk)
        if blk + 1 < 4:
            prologue(blk + 1)
        merge_out(blk)
    S.barrier()


class Arena:
    def __init__(self, t, nbytes):
        self.t = t
        self.n = nbytes
        self.off = 0

    def reset(self):
        self.off = 0

    def mark(self):
        return self.off

    def release(self, m):
        self.off = m

    def alloc(self, shape, dtype):
        esz = 4 if dtype == F32 else 2
        n = int(np.prod(shape)) * esz
        self.off = (self.off + 31) // 32 * 32
        a = self.off
        assert a + n <= self.n, ("arena overflow", a, n, self.n)
        self.off = a + n
        v = self.t[:, a // 2:(a + n) // 2]
        if dtype == F32:
            v = v.bitcast(F32)
        if len(shape) == 2:
            v = v.rearrange("p (a b) -> p a b", a=shape[0])
        elif len(shape) == 3:
            v = v.rearrange("p (a b c) -> p a b c", a=shape[0], b=shape[1])
        return v


SBUF_ARENA_BYTES = 210432


class Ctx:
    def __init__(self, nc, cst):
        self.nc = nc
        self.S = Sched(nc)
        sb = nc.alloc_sbuf_tensor
        self.cst_t = sb("cst_t", [128, 384], F32)
        self.idb = sb("idb", [128, 128], BF16)
        self.trib = sb("trib", [128, 128], BF16)
        self.mem = Arena(sb("arena", [128, SBUF_ARENA_BYTES // 2], BF16), SBUF_ARENA_BYTES)
        self.ps = nc.alloc_psum_tensor("ps", [128, 8, 512], F32)
        self.bank = [Buf("bank%d" % i, psum=True) for i in range(8)]
        self.pbuf = {}
        S = self.S
        S.dma("sync", self.cst_t[:], cst, writes=[self.pb("cst")])
        S.op("vector", lambda e: e.tensor_copy(self.idb[:], self.cst_t[:, 0:128]), [self.pb("cst")], [self.pb("idb")])
        S.op("vector", lambda e: e.tensor_copy(self.trib[:], self.cst_t[:, 128:256]), [self.pb("cst")], [self.pb("trib")])

    def pb(self, name):
        if name not in self.pbuf:
            self.pbuf[name] = Buf(name)
        return self.pbuf[name]


PERSIST = ("cst", "idb", "trib", "yA", "yB", "ysrcA", "ysrcB", "xin", "xo0", "xo1", "xo2", "xo3", "xin0", "xin1", "xin2", "xin3")


W1_COLS = 2308
ARENA_BYTES = 102 * 1024


def emit_p1(ctx, layer, T):
    lam_init = 0.8 - 0.6 * math.exp(-0.3 * layer)
    nc, S, ps, bank, mem = ctx.nc, ctx.S, ctx.ps, ctx.bank, ctx.mem
    x, w1, gpre, aug, abias, na, nb, convw, bif, lq, yTo = (T[k] for k in
                                                             ("x", "w1", "gpre", "aug", "abias", "na", "nb", "convw", "bif", "lq", "yTo"))
    cst_t, idb, trib = ctx.cst_t, ctx.idb, ctx.trib
    mem.reset()
    hT = mem.alloc([8, SEQ], BF16)
    wb = mem.alloc([8, 1284], BF16)
    gpre_t = mem.alloc([DM], F32)
    abias_t = mem.alloc([16], F32)
    na_t = mem.alloc([256], F32)
    nb_t = mem.alloc([256], F32)
    convw_t = mem.alloc([16], F32)
    bif_t = mem.alloc([8], F32)
    lq_t = mem.alloc([256], F32)
    sm = mem.alloc([64], F32)
    ss = [mem.alloc([8], F32) for i in range(4)]
    rs = [mem.alloc([8], F32) for i in range(4)]
    ar = mem
    stage_mark = mem.mark()
    B = {}
    tri_f = cst_t[:, 128:256]
    ones_f = cst_t[:, 256:384]
    xin_name = T.get("xin_name", "xin")

    def buf(name):
        if name in PERSIST:
            return ctx.pb(T.get(name + "_buf", name))
        if name not in B:
            B[name] = Buf(name)
        return B[name]

    def act(out, in_, func, reads, writes, **kw):
        return S.op("scalar", lambda e: e.activation(out, in_, func, **kw), reads, writes)

    def dve(method, *args, reads=(), writes=(), **kw):
        return S.op("vector", lambda e: getattr(e, method)(*args, **kw), reads, writes)

    def pool(method, *args, reads=(), writes=(), **kw):
        return S.op("gpsimd", lambda e: getattr(e, method)(*args, **kw), reads, writes)

    def mm(out, lhsT, rhs, reads, writes, **kw):
        return S.op("tensor", lambda e: e.matmul(out, lhsT, rhs, **kw), reads, writes)

    def tp(out, in_, reads, writes):
        return S.op("tensor", lambda e: e.transpose(out, in_, idb[:]), list(reads) + [buf("idb")], writes)

    for name, t, src in (("gpre", gpre_t, gpre), ("abias", abias_t, abias), ("na", na_t, na),
                         ("nb", nb_t, nb), ("convw", convw_t, convw), ("lq", lq_t, lq)):
        S.dma("sync", t[:], src, writes=[buf(name)])
    S.dma("sync", bif_t[:, 0:4], bif, writes=[buf("bif")])

    si = [0]

    def load_w(c0, ncols):
        for kc in range(8):
            S.dma("gpsimd", wb[:, kc, 0:ncols], w1[kc * 128:(kc + 1) * 128, c0:c0 + ncols], writes=[buf("wb")])

    load_w(0, 1024)

    ri = [0]

    def rstd_from(ssk, k, n):
        act(rs[k][:, 0:1], ssk, AF.Sqrt, [buf("ss%d" % k)], [buf("rs%d" % k)], scale=1.0 / n, bias=EPS)
        dve("reciprocal", rs[k][:, 0:1], rs[k][:, 0:1], reads=[buf("rs%d" % k)], writes=[buf("rs%d" % k)])

    mem.release(stage_mark)
    xt = [ar.alloc([DM], F32) for _ in range(3)]
    junk = ar.alloc([DM], BF16)
    hb = [ar.alloc([DM], BF16) for _ in range(2)]
    kk = {}

    pos = {}

    def st0_l(t):
        pos[t] = len(pos)
        i = pos[t] % 3
        xb = buf("xt%d" % i)
        xsrc_ap, xsrc_buf = x(t)
        S.dma("sync", xt[i], xsrc_ap, reads=[buf(xsrc_buf)], writes=[xb])
        k = ri[0] % 4
        ri[0] += 1
        kk[t] = k
        act(junk, xt[i], AF.Square, [xb], [buf("junk"), buf("ss%d" % k)], accum_out=ss[k][:, 0:1])
        rstd_from(ss[k][:, 0:1], k, DM)

    def st0_m(t):
        i, j, k = pos[t] % 3, pos[t] % 2, kk[t]
        dve("scalar_tensor_tensor", hb[j], xt[i], rs[k][:, 0:1], gpre_t[:], ALU.mult, ALU.mult,
            reads=[buf("xt%d" % i), buf("rs%d" % k), buf("gpre")], writes=[buf("hb%d" % j)])

    def st0_n(t):
        j = pos[t] % 2
        pb = 6 + j
        pT = ps[:, pb, 0:512].bitcast(BF16)
        for c in range(8):
            tp(pT[:, c * 128:(c + 1) * 128], hb[j][:, c * 128:(c + 1) * 128], [buf("hb%d" % j)], [bank[pb]])
        if pos[t] % 2 == 0:
            act(hT[:, :, t * 128:(t + 1) * 128], pT.rearrange("p (c t) -> p c t", c=8), AF.Copy, [bank[pb]], [buf("hT")])
        else:
            dve("tensor_copy", hT[:, :, t * 128:(t + 1) * 128], pT.rearrange("p (c t) -> p c t", c=8),
                reads=[bank[pb]], writes=[buf("hT")])

    order = list(range(0, 12)) + list(range(16, 28)) + list(range(12, 16)) + list(range(28, 32))
    for m in range(32 + 2):
        if m < 32:
            st0_l(order[m])
        if 1 <= m <= 32:
            st0_m(order[m - 1])
        if m >= 2:
            st0_n(order[m - 2])
    S.barrier()

    mem.release(stage_mark)
    QT = [ar.alloc([SEQ], BF16) for _ in range(2)]
    KT = [ar.alloc([SEQ], BF16) for _ in range(2)]
    Vaug = ar.alloc([32, 2, 129], BF16)
    gz = ar.alloc([32, 256], BF16)
    PT = [ar.alloc([2, 512], BF16) for _ in range(3)]
    yTs = ar.alloc([SEQ], BF16)
    otmp = [ar.alloc([128], F32) for _ in range(4)]
    ot = [ar.alloc([128], F32) for _ in range(4)]
    ssn = [ar.alloc([8], F32) for _ in range(4)]
    rsn = [ar.alloc([8], F32) for _ in range(4)]
    yab = [ar.alloc([128], BF16) for _ in range(4)]
    zt = [ar.alloc([256], F32) for _ in range(2)]
    junkf = ar.alloc([128], F32)
    for i in range(2):
        pool("memset", QT[i], 0.0, writes=[buf("QT%d" % i)])
        pool("memset", KT[i], 0.0, writes=[buf("KT%d" % i)])
    pool("memset", Vaug, 1.0, writes=[buf("Vaug")])
    lp = zt[0]
    dve("tensor_tensor", lp[:, 0:64], lq_t[:, 0:64], lq_t[:, 64:128], ALU.mult, reads=[buf("lq")], writes=[buf("zt0")])
    dve("tensor_tensor", lp[:, 64:128], lq_t[:, 128:192], lq_t[:, 192:256], ALU.mult, reads=[buf("lq")], writes=[buf("zt0")])
    dve("reduce_sum", sm[:, 1:2], lp[:, 0:64], AX.X, reads=[buf("zt0")], writes=[buf("sm1")])
    dve("reduce_sum", sm[:, 2:3], lp[:, 64:128], AX.X, reads=[buf("zt0")], writes=[buf("sm2")])
    act(sm[:, 1:2], sm[:, 1:2], AF.Exp, [buf("sm1")], [buf("sm1")])
    act(sm[:, 2:3], sm[:, 2:3], AF.Exp, [buf("sm2")], [buf("sm2")])
    dve("tensor_tensor", sm[:, 0:1], sm[:, 2:3], sm[:, 1:2], ALU.subtract, reads=[buf("sm1"), buf("sm2")], writes=[buf("neglam")])
    dve("tensor_scalar", sm[:, 0:1], sm[:, 0:1], -lam_init, None, ALU.add, reads=[buf("neglam")], writes=[buf("neglam")])
    neglam = sm[:, 0:1]

    pr = [0]

    def next_bank():
        b = pr[0] % 6
        pr[0] += 1
        return b

    for t in range(32):
        pb = next_bank()
        for kc in range(8):
            mm(ps[:, pb, :], hT[:, kc, t * 128:(t + 1) * 128], wb[:, kc, 512:1024], [buf("hT"), buf("wb")], [bank[pb]],
               start=(kc == 0), stop=(kc == 7))
        dve("tensor_copy", Vaug[:, t, :, 0:128], ps[:, pb, 0:256].rearrange("p (h e) -> p h e", h=2),
            reads=[bank[pb]], writes=[buf("Vaug")])
        zi = t % 2
        act(zt[zi], ps[:, pb, 256:512], AF.Silu, [bank[pb]], [buf("zt%d" % zi)])
        dve("scalar_tensor_tensor", gz[:, t, :], zt[zi], 1.0 - lam_init, na_t[:], ALU.mult, ALU.mult,
            reads=[buf("zt%d" % zi), buf("na")], writes=[buf("gz")])

    gcnt = [0]
    for hl in range(2):
        S.dma("sync", QT[0][64:68, :], aug[hl, 0], writes=[buf("QT0")])
        S.dma("sync", QT[1][0:4, :], aug[hl, 0], writes=[buf("QT1")])
        S.dma("sync", KT[0][64:68, :], aug[hl, 1], writes=[buf("KT0")])
        S.dma("sync", KT[1][0:4, :], aug[hl, 1], writes=[buf("KT1")])
        for blk in range(8):
            for qk in range(2):
                c0 = qk * 256 + hl * 128
                dst = QT if qk == 0 else KT
                nm = "QT" if qk == 0 else "KT"
                pb = next_bank()
                for kc in range(8):
                    mm(ps[:, pb, :], wb[:, kc, c0:c0 + 128], hT[:, kc, blk * 512:(blk + 1) * 512],
                       [buf("wb"), buf("hT")], [bank[pb]], start=(kc == 0), stop=(kc == 7))
                act(dst[0][0:64, blk * 512:(blk + 1) * 512], ps[0:64, pb, :], AF.Copy, [bank[pb]], [buf(nm + "0")])
                dve("tensor_copy", dst[1][64:128, blk * 512:(blk + 1) * 512], ps[64:128, pb, :],
                    reads=[bank[pb]], writes=[buf(nm + "1")])
        if hl == 1:
            load_w(1024, 1284)
        jobs = []
        for I in range(8):
            for c in range(2):
                groups = []
                for J in range(I):
                    for hf in range(2):
                        groups.append((False, I - J, [4 * J + 2 * hf, 4 * J + 2 * hf + 1]))
                for hf in range(2):
                    groups.append((True, 0, [4 * I + 2 * hf, 4 * I + 2 * hf + 1]))
                for gx, (diag, d, tiles) in enumerate(groups):
                    jobs.append((I, c, diag, d, tiles, gx == 0, gx == len(groups) - 1))
        NJ = len(jobs)
        pending_tp = []

        def rec_st(n):
            I, c, diag, d, tiles, first, last = jobs[n]
            gi = n % 3
            b0 = 2 * gi
            for jj, j in enumerate(tiles):
                n0 = 128 * (j - 4 * I) if diag else 0
                mm(ps[:, b0 + jj, n0:512], KT[c][:, j * 128:(j + 1) * 128], QT[c][:, I * 512 + n0:(I + 1) * 512],
                   [buf("KT%d" % c), buf("QT%d" % c)], [bank[b0 + jj]], start=True, stop=True)

        def rec_exp(n):
            I, c, diag, d, tiles, first, last = jobs[n]
            gi = n % 3
            b0 = 2 * gi
            ptb = buf("PT%d" % gi)
            bcol = abias_t[:, hl * 8 + d:hl * 8 + d + 1]
            if not diag:
                act(PT[gi], ps[:, b0:b0 + 2, :], AF.Exp, [bank[b0], bank[b0 + 1], buf("abias")], [ptb],
                    scale=0.125, bias=bcol)
            else:
                for jj, j in enumerate(tiles):
                    n0 = 128 * (j - 4 * I)
                    act(PT[gi][:, jj, n0:512], ps[:, b0 + jj, n0:512], AF.Exp, [bank[b0 + jj], buf("abias")], [ptb],
                        scale=0.125, bias=bcol)
                    dve("tensor_tensor", PT[gi][:, jj, n0:n0 + 128], PT[gi][:, jj, n0:n0 + 128], trib[:], ALU.mult,
                        reads=[ptb, buf("trib")], writes=[ptb])

        def rec_pv(n, m):
            I, c, diag, d, tiles, first, last = jobs[n]
            gi = n % 3
            b0 = 2 * gi
            ptb = buf("PT%d" % gi)
            for jj, j in enumerate(tiles):
                r = j - 4 * I
                us = range(r, 4) if diag else range(4)
                for u in us:
                    ob_ = 6 + u // 2
                    oc = (u % 2) * 129
                    mm(ps[:, ob_, oc:oc + 129], PT[gi][:, jj, u * 128:(u + 1) * 128], Vaug[:, j, hl, :],
                       [ptb, buf("Vaug")], [bank[ob_]],
                       start=(first and jj == 0 and u in (0, 2)), stop=(diag and r == u), skip_group_check=True)
            if not last:
                return
            ks = []
            for u in range(4):
                ob_ = 6 + u // 2
                oc = (u % 2) * 129
                O = ps[:, ob_, oc:oc + 128]
                Osum = ps[:, ob_, oc + 128:oc + 129]
                k = ri[0] % 4
                ri[0] += 1
                dve("reciprocal", rs[k][:, 0:1], Osum, reads=[bank[ob_]], writes=[buf("rs%d" % k)])
                if c == 0:
                    dve("tensor_scalar", otmp[u], O, rs[k][:, 0:1], None, ALU.mult,
                        reads=[bank[ob_], buf("rs%d" % k)], writes=[buf("otmp%d" % u)])
                else:
                    dve("tensor_tensor", rs[k][:, 0:1], rs[k][:, 0:1], neglam, ALU.mult,
                        reads=[buf("rs%d" % k), buf("neglam")], writes=[buf("rs%d" % k)])
                    dve("scalar_tensor_tensor", ot[u], O, rs[k][:, 0:1], otmp[u], ALU.mult, ALU.add,
                        reads=[bank[ob_], buf("rs%d" % k), buf("otmp%d" % u)], writes=[buf("ot%d" % u)])
            if c == 1:
                for u in range(4):
                    k2 = u
                    dve("scalar_tensor_tensor", junkf, ot[u], 1.0, ot[u], ALU.mult, ALU.mult,
                        reads=[buf("ot%d" % u)], writes=[buf("junkf"), buf("ssn%d" % k2)], accum_out=ssn[k2][:, 0:1])
                for u in range(4):
                    k2 = u
                    act(rsn[k2][:, 0:1], ssn[k2][:, 0:1], AF.Ln, [buf("ssn%d" % k2)], [buf("rsn%d" % k2)], scale=1.0 / 128, bias=EPS)
                    act(rsn[k2][:, 0:1], rsn[k2][:, 0:1], AF.Exp, [buf("rsn%d" % k2)], [buf("rsn%d" % k2)], scale=-0.5)
                for u in range(4):
                    k2 = u
                    tt = 4 * I + u
                    dve("scalar_tensor_tensor", yab[u], ot[u], rsn[k2][:, 0:1], gz[:, tt, hl * 128:(hl + 1) * 128],
                        ALU.mult, ALU.mult, reads=[buf("ot%d" % u), buf("rsn%d" % k2), buf("gz")], writes=[buf("yab%d" % u)])
                    pending_tp.append((m + 2, u, tt, b0))

        def flush_tp(m, force=False):
            while pending_tp and (force or pending_tp[0][0] <= m):
                _, u, tt, b0 = pending_tp.pop(0)
                pT = ps[:, b0, u * 64:(u + 1) * 64].bitcast(BF16)
                tp(pT, yab[u], [buf("yab%d" % u)], [bank[b0]])
                act(yTs[:, tt * 128:(tt + 1) * 128], pT, AF.Copy, [bank[b0]], [buf("yTs")])

        for m in range(NJ + 2):
            if m < NJ:
                rec_st(m)
            if 1 <= m <= NJ:
                rec_exp(m - 1)
            if m >= 2:
                rec_pv(m - 2, m)
            flush_tp(m)
        flush_tp(0, force=True)
        S.dma("gpsimd", yTo[0][hl * 128:(hl + 1) * 128, :], yTs, reads=[buf("yTs")], writes=[buf("yA")])
    if T.get("after_A") is not None:
        T["after_A"]()
    S.barrier()

    mem.release(stage_mark)
    qbT = [ar.alloc([SEQ], BF16) for _ in range(2)]
    kbT = [ar.alloc([SEQ], BF16) for _ in range(2)]
    vbaug = ar.alloc([32, 2, 129], BF16)
    sgo = ar.alloc([32, 256], BF16)
    gzb = ar.alloc([32, 256], BF16)
    yTb = [ar.alloc([1024], BF16) for _ in range(2)]
    xpre = [ar.alloc([515], F32) for _ in range(2)]
    ct = [ar.alloc([512], F32) for _ in range(2)]
    zt = [ar.alloc([256], F32) for _ in range(2)]
    gts = ar.alloc([32, 4], F32)
    IG = ar.alloc([2, 32], F32)
    LF = ar.alloc([2, 32], F32)
    EB = ar.alloc([64], F32)
    DEC = ar.alloc([64], F32)
    EK = ar.alloc([64], F32)
    SmT = [ar.alloc([128], BF16) for _ in range(4)]
    Ktm = [ar.alloc([128], BF16) for _ in range(4)]
    G = [ar.alloc([129], F32) for _ in range(2)]
    Cbf = [ar.alloc([129], BF16) for _ in range(2)]
    hbt = [ar.alloc([128], F32) for _ in range(2)]
    ybb = [ar.alloc([128], BF16) for _ in range(4)]
    dn = [ar.alloc([4], F32) for _ in range(2)]
    junkf3 = [ar.alloc([128], F32) for _ in range(2)]
    pool("memset", vbaug, 1.0, writes=[buf("vbaug")])
    for t in range(32):
        pb = next_bank()
        for kc in range(8):
            mm(ps[:, pb, :], hT[:, kc, t * 128:(t + 1) * 128], wb[:, kc, 512:1024], [buf("hT"), buf("wb")], [bank[pb]],
               start=(kc == 0), stop=(kc == 7))
        dve("tensor_copy", vbaug[:, t, :, 0:128], ps[:, pb, 0:256].rearrange("p (h e) -> p h e", h=2),
            reads=[bank[pb]], writes=[buf("vbaug")])
        act(sgo[:, t, :], ps[:, pb, 256:512], AF.Sigmoid, [bank[pb]], [buf("sgo")])
    for t in range(32):
        pb = next_bank()
        for kc in range(8):
            mm(ps[:, pb, 0:260], hT[:, kc, t * 128:(t + 1) * 128], wb[:, kc, 1024:1284], [buf("hT"), buf("wb")], [bank[pb]],
               start=(kc == 0), stop=(kc == 7))
        zi = t % 2
        act(zt[zi], ps[:, pb, 0:256], AF.Silu, [bank[pb]], [buf("zt%d" % zi)])
        dve("tensor_tensor", gzb[:, t, :], zt[zi], nb_t[:], ALU.mult, reads=[buf("zt%d" % zi), buf("nb")], writes=[buf("gzb")])
        dve("tensor_copy", gts[:, t, :], ps[:, pb, 256:260], reads=[bank[pb]], writes=[buf("gts")])
    for h in range(2):
        dve("tensor_scalar", IG[:, h, :], gts[:, :, h], bif_t[:, h:h + 1], None, ALU.add,
            reads=[buf("gts"), buf("bif")], writes=[buf("IG")])
        dve("tensor_scalar", LF[:, h, :], gts[:, :, 2 + h], bif_t[:, 2 + h:3 + h], None, ALU.add,
            reads=[buf("gts"), buf("bif")], writes=[buf("LF")])
    LFf = LF.rearrange("p a b -> p (a b)")
    IGf = IG.rearrange("p a b -> p (a b)")
    act(LFf, LFf, AF.Exp, [buf("LF")], [buf("LF")], scale=-1.0)
    act(LFf, LFf, AF.Ln, [buf("LF")], [buf("LF")], bias=1.0)
    dve("tensor_scalar", LFf, LFf, -1.0, None, ALU.mult, reads=[buf("LF")], writes=[buf("LF")])
    mm(ps[:, 6, 0:64], tri_f, LFf, [buf("cst"), buf("LF")], [bank[6]], start=True, stop=True)
    mm(ps[:, 7, 0:64], ones_f, LFf, [buf("cst"), buf("LF")], [bank[7]], start=True, stop=True)
    act(EB, ps[:, 6, 0:64], AF.Exp, [bank[6]], [buf("EB")])
    act(DEC, ps[:, 7, 0:64], AF.Exp, [bank[7]], [buf("DEC")])
    dve("tensor_tensor", EK, IGf, ps[:, 6, 0:64], ALU.subtract, reads=[buf("IG"), bank[6]], writes=[buf("EK")])
    act(EK, EK, AF.Exp, [buf("EK")], [buf("EK")], bias=math.log(128.0 ** -0.5))
    xi = [0]
    for grp in range(4):
        dst = (qbT if grp < 2 else kbT)[grp % 2]
        dnm = ("qbT" if grp < 2 else "kbT") + str(grp % 2)
        for blk in range(8):
            pb = next_bank()
            for kc in range(8):
                mm(ps[:, pb, :], wb[:, kc, grp * 128:(grp + 1) * 128], hT[:, kc, blk * 512:(blk + 1) * 512],
                   [buf("wb"), buf("hT")], [bank[pb]], start=(kc == 0), stop=(kc == 7))
            i = xi[0] % 2
            xi[0] += 1
            xp = xpre[i]
            xpb = buf("xpre%d" % i)
            if blk == 0:
                pool("memset", xp[:, 0:3], 0.0, writes=[xpb])
            else:
                pool("tensor_copy", xp[:, 0:3], xpre[1 - i][:, 512:515], reads=[buf("xpre%d" % (1 - i))], writes=[xpb])
            act(xp[:, 3:515], ps[:, pb, :], AF.Copy, [bank[pb]], [xpb])
            cw = convw_t[:, grp * 4:(grp + 1) * 4]
            dve("tensor_scalar", ct[0], xp[:, 0:512], cw[:, 0:1], None, ALU.mult, reads=[xpb, buf("convw")], writes=[buf("ct0")])
            dve("scalar_tensor_tensor", ct[1], xp[:, 1:513], cw[:, 1:2], ct[0], ALU.mult, ALU.add,
                reads=[xpb, buf("convw"), buf("ct0")], writes=[buf("ct1")])
            dve("scalar_tensor_tensor", ct[0], xp[:, 2:514], cw[:, 2:3], ct[1], ALU.mult, ALU.add,
                reads=[xpb, buf("convw"), buf("ct1")], writes=[buf("ct0")])
            dve("scalar_tensor_tensor", ct[1], xp[:, 3:515], cw[:, 3:4], ct[0], ALU.mult, ALU.add,
                reads=[xpb, buf("convw"), buf("ct0")], writes=[buf("ct1")])
            act(dst[:, blk * 512:(blk + 1) * 512], ct[1], AF.Silu, [buf("ct1")], [buf(dnm)])
    if T.get("prefetch") is not None:
        T["prefetch"]([buf("hT"), buf("wb")])
    def ph_x(c, h):
        col = h * 32 + c
        i4 = 2 * (c % 2) + h
        cs = slice(c * 128, (c + 1) * 128)
        qb_, kb_ = buf("qbT%d" % h), buf("kbT%d" % h)
        pTk = ps[:, h, 128:192].bitcast(BF16)
        return [
            lambda: mm(ps[:, h, 0:128], kbT[h][:, cs], qbT[h][:, cs], [qb_, kb_], [bank[h]], start=True, stop=True),
            lambda: tp(pTk, kbT[h][:, cs], [kb_], [bank[h]]),
            lambda: dve("scalar_tensor_tensor", SmT[i4], ps[:, h, 0:128], EK[:, col:col + 1], tri_f, ALU.mult, ALU.mult,
                        reads=[bank[h], buf("EK"), buf("cst")], writes=[buf("SmT%d" % i4)]),
            lambda: act(Ktm[i4], pTk, AF.Copy, [bank[h], buf("EK")], [buf("Ktm%d" % i4)], scale=EK[:, col:col + 1]),
        ]

    def ph_y(c, h):
        col = h * 32 + c
        i4 = 2 * (c % 2) + h
        hb_ = 2 + 2 * (c % 2) + h
        ub_ = 6 + h
        cs = slice(c * 128, (c + 1) * 128)
        qb_ = buf("qbT%d" % h)
        ops = [lambda: mm(ps[:, hb_, 0:129], SmT[i4], vbaug[:, c, h, :], [buf("SmT%d" % i4), buf("vbaug")], [bank[hb_]],
                          start=True, stop=(c == 0))]
        if c > 0:
            ops.append(lambda: mm(ps[:, hb_, 0:129], qbT[h][:, cs], Cbf[h], [qb_, buf("Cbf%d" % h)], [bank[hb_]],
                                  start=False, stop=True))
        ops.append(lambda: mm(ps[:, ub_, 0:129], Ktm[i4], vbaug[:, c, h, :], [buf("Ktm%d" % i4), buf("vbaug")], [bank[ub_]],
                              start=True, stop=True))
        if c == 0:
            ops.append(lambda: dve("tensor_copy", G[h], ps[:, ub_, 0:129], reads=[bank[ub_]], writes=[buf("G%d" % h)]))
        else:
            ops.append(lambda: dve("scalar_tensor_tensor", G[h], G[h], DEC[:, col - 1:col], ps[:, ub_, 0:129], ALU.mult, ALU.add,
                                   reads=[buf("G%d" % h), buf("DEC"), bank[ub_]], writes=[buf("G%d" % h)]))
        if c < 31:
            ops.append(lambda: act(Cbf[h], G[h], AF.Copy, [buf("G%d" % h), buf("DEC")], [buf("Cbf%d" % h)],
                                   scale=DEC[:, col:col + 1]))
        return ops

    def ph_z1(c, h):
        col = h * 32 + c
        i4 = 2 * (c % 2) + h
        hb_ = 2 + 2 * (c % 2) + h
        dnb = buf("dn%d" % h)
        d0, d1, d2 = dn[h][:, 0:1], dn[h][:, 1:2], dn[h][:, 2:3]
        k = ri[0] % 4
        ri[0] += 1
        ssb, rsb = buf("ss%d" % k), buf("rs%d" % k)
        return [
            lambda: dve("tensor_tensor", d0, ps[:, hb_, 128:129], EB[:, col:col + 1], ALU.mult,
                        reads=[bank[hb_], buf("EB")], writes=[dnb]),
            lambda: dve("tensor_scalar", d1, d0, -1.0, 1.0, ALU.mult, ALU.max, reads=[dnb], writes=[dnb]),
            lambda: dve("tensor_tensor", d0, d0, d1, ALU.max, reads=[dnb], writes=[dnb]),
            lambda: dve("reciprocal", d1, d0, reads=[dnb], writes=[dnb]),
            lambda: dve("tensor_tensor", d2, d1, EB[:, col:col + 1], ALU.mult, reads=[dnb, buf("EB")], writes=[dnb]),
            lambda: dve("scalar_tensor_tensor", hbt[h], ps[:, hb_, 0:128], d2, sgo[:, c, h * 128:(h + 1) * 128],
                        ALU.mult, ALU.mult, reads=[bank[hb_], dnb, buf("sgo")], writes=[buf("hbt%d" % h)]),
            lambda: dve("scalar_tensor_tensor", junkf3[h], hbt[h], 1.0, hbt[h], ALU.mult, ALU.mult,
                        reads=[buf("hbt%d" % h)], writes=[buf("junkf3_%d" % h), ssb], accum_out=ss[k][:, 0:1]),
            lambda: act(rs[k][:, 0:1], ss[k][:, 0:1], AF.Ln, [ssb], [rsb], scale=1.0 / 128, bias=EPS),
            lambda: act(rs[k][:, 0:1], rs[k][:, 0:1], AF.Exp, [rsb], [rsb], scale=-0.5),
            lambda: dve("scalar_tensor_tensor", ybb[i4], hbt[h], rs[k][:, 0:1], gzb[:, c, h * 128:(h + 1) * 128],
                        ALU.mult, ALU.mult, reads=[buf("hbt%d" % h), rsb, buf("gzb")], writes=[buf("ybb%d" % i4)]),
        ]

    def ph_z2(c, h):
        i4 = 2 * (c % 2) + h
        pTy = ps[:, h, 192:256].bitcast(BF16)
        c8 = c % 8
        ops = [lambda: tp(pTy, ybb[i4], [buf("ybb%d" % i4)], [bank[h]]),
               lambda: act(yTb[h][:, c8 * 128:(c8 + 1) * 128], pTy, AF.Copy, [bank[h]], [buf("yTb%d" % h)])]
        if c8 == 7:
            ops.append(lambda: S.dma("gpsimd", yTo[1][h * 128:(h + 1) * 128, (c - 7) * 128:(c + 1) * 128], yTb[h],
                                     reads=[buf("yTb%d" % h)], writes=[buf("yB")]))
        return ops

    def interleave(fn, c):
        a_, b_ = fn(c, 0), fn(c, 1)
        for j in range(max(len(a_), len(b_))):
            if j < len(a_):
                a_[j]()
            if j < len(b_):
                b_[j]()

    for q in range(32 + 3):
        if q < 32:
            interleave(ph_x, q)
        if 1 <= q <= 32:
            interleave(ph_y, q - 1)
        if 2 <= q <= 33:
            interleave(ph_z1, q - 2)
        if q >= 3:
            interleave(ph_z2, q - 3)
    S.barrier()


_SLOPES = [2.0 ** (-8.0 * (h + 1) / 4) for h in range(4)]
_CST = None


def _p1_inputs(inp, l, g, x_full):
    w_in = inp["w_in"][l]
    sl = slice(g * 256, (g + 1) * 256)
    parts = [w_in[:, 0:512][:, sl], w_in[:, 512:1024][:, sl], w_in[:, 1024:1536][:, sl], w_in[:, 1536:2048][:, sl],
             w_in[:, 2048:2560][:, sl], w_in[:, 2560:3072][:, sl], w_in[:, 3072:3584][:, sl], w_in[:, 3584:4096][:, sl],
             w_in[:, 4096:4608][:, sl], w_in[:, 4608 + 2 * g:4610 + 2 * g], w_in[:, 4612 + 2 * g:4614 + 2 * g]]
    w1 = np.ascontiguousarray(np.concatenate(parts, axis=1), dtype=np.float32)
    assert w1.shape == (DM, W1_COLS)
    v = np.arange(SEQ) % 512
    hi = ((v // 16) * 16).astype(np.float32)
    lo = (v % 16).astype(np.float32)
    one = np.ones(SEQ, np.float32)
    aug = np.zeros((2, 2, 4, SEQ), np.float32)
    abias = np.zeros((128, 16), np.float32)
    for hl in range(2):
        s = _SLOPES[2 * g + hl]
        aug[hl, 0] = np.stack([-s * 8.0 * hi, -s * 8.0 * lo, one, one])
        aug[hl, 1] = np.stack([one, one, s * 8.0 * hi, s * 8.0 * lo])
        for d in range(8):
            abias[:, hl * 8 + d] = -s * 512.0 * d
    cq = inp["conv_qk"][l]
    convw = np.zeros((128, 16), np.float32)
    for grp in range(4):
        ch0 = (0 if grp < 2 else 512) + (2 * g + grp % 2) * 128
        convw[:, grp * 4:(grp + 1) * 4] = cq[:, ch0:ch0 + 128].T
    bi = inp["b_if"][l]
    bif = np.broadcast_to(np.array([bi[2 * g], bi[2 * g + 1], bi[4 + 2 * g], bi[5 + 2 * g]], np.float32), (128, 4))
    return {
        "w1": w1,
        "gpre": np.ascontiguousarray(np.broadcast_to(inp["norm_pre"][l], (128, DM))),
        "cst": _consts_f32(),
        "aug": aug.astype(ml_dtypes.bfloat16),
        "abias": abias,
        "na": np.ascontiguousarray(np.broadcast_to(inp["norm_a"][l][sl], (128, 256))),
        "nb": np.ascontiguousarray(np.broadcast_to(inp["norm_b"][l][sl], (128, 256))),
        "convw": convw,
        "bif": np.ascontiguousarray(bif),
        "lq": np.ascontiguousarray(np.broadcast_to(inp["lambda_qk"][l].reshape(256), (128, 256))),
    }


def _host_maps(inp):
    x = np.asarray(inp["x"], dtype=np.float32)
    maps = []
    for c in range(8):
        b, g = c // 2, c % 2
        m = {"x": np.ascontiguousarray(x[b]), "xown": np.ascontiguousarray(x[b, g * HALF:(g + 1) * HALF]),
             "cst": _consts_f32()}
        selv = np.zeros((128, 2), np.float32)
        selv[:, g] = 1.0
        m["sel"] = selv
        for l in range(2):
            p1 = _p1_inputs(inp, l, g, None)
            for k in ("w1", "gpre", "na", "nb", "convw", "bif", "lq"):
                m["%s_%d" % (k, l)] = p1[k]
            if l == 0:
                m["aug"] = p1["aug"]
                m["abias"] = p1["abias"]
            w_in = inp["w_in"][l]
            m["w2_%d" % l] = np.ascontiguousarray(w_in[:, 4616:6664], dtype=np.float32)
            m["wab_%d" % l] = np.ascontiguousarray(np.concatenate([inp["w_a"][l], inp["w_b"][l]], axis=0), dtype=np.float32)
            m["wout_%d" % l] = np.ascontiguousarray(inp["w_out"][l], dtype=np.float32)
            m["gpost_%d" % l] = np.ascontiguousarray(np.broadcast_to(inp["norm_post"][l], (128, DM)), dtype=np.float32)
        maps.append(m)
    return maps


PAIRS = [[0, 1], [2, 3], [4, 5], [6, 7]]


def build_fused():
    nc = bass.Bass("TRN2", target_bir_lowering=False)

    def ext(name, shape, dt=F32):
        return nc.dram_tensor(name, shape, dt, kind="ExternalInput").ap()

    x = ext("x", [SEQ, DM])
    xown = ext("xown", [HALF, DM])
    cst = ext("cst", [128, 384])
    sel = ext("sel", [128, 2])
    aug = ext("aug", [2, 2, 4, SEQ], BF16)
    abias = ext("abias", [128, 16])
    L = []
    for l in range(2):
        L.append({
            "w1": ext("w1_%d" % l, [DM, W1_COLS]), "gpre": ext("gpre_%d" % l, [128, DM]),
            "na": ext("na_%d" % l, [128, 256]), "nb": ext("nb_%d" % l, [128, 256]),
            "convw": ext("convw_%d" % l, [128, 16]), "bif": ext("bif_%d" % l, [128, 4]),
            "lq": ext("lq_%d" % l, [128, 256]), "w2": ext("w2_%d" % l, [DM, 2048]),
            "wab": ext("wab_%d" % l, [1024, DM]), "wout": ext("wout_%d" % l, [DM, DM]),
            "gpost": ext("gpost_%d" % l, [128, DM]), "aug": aug, "abias": abias, "sel": sel,
        })
    out = nc.dram_tensor("out", [HALF, DM], F32, kind="ExternalOutput").ap()
    ysend = [nc.dram_tensor("ysend%d" % k, [256, SEQ], BF16) for k in range(2)]
    yrecv = [nc.dram_tensor("yrecv%d" % k, [512, SEQ], BF16) for k in range(2)]
    xnew = [nc.dram_tensor("xnew%d" % k, [512, DM], F32) for k in range(4)]
    xg = [nc.dram_tensor("xg%d" % k, [1024, DM], F32) for k in range(4)]
    ctx = Ctx(nc, cst)
    S = ctx.S

    def gather(src, dst, sbuf, dbuf):
        S.op("gpsimd", lambda e: e.collective_compute("AllGather", ALU.bypass, replica_groups=PAIRS,
                                                       ins=[src.ap().opt()], outs=[dst.ap().opt()]),
             [ctx.pb(sbuf)], [ctx.pb(dbuf)], dma=True, cc=True)

    for l in range(2):
        T1 = dict(L[l])
        if l == 0:
            T1["x"] = lambda t: (x[t * 128:(t + 1) * 128, :], "xin")
        else:
            def xsrc1(t):
                hh, tt = t // 16, t % 16
                k, r = tt // 4, tt % 4
                return xg[k].ap()[hh * 512 + r * 128:hh * 512 + (r + 1) * 128, :], "xin%d" % k
            T1["x"] = xsrc1
            for k in range(4):
                T1["xin%d_buf" % k] = "xg%d" % k
        T1.update({"yTo": [ysend[0].ap(), ysend[1].ap()], "yA_buf": "ysend0", "yB_buf": "ysend1",
                   "after_A": lambda: gather(ysend[0], yrecv[0], "ysend0", "yrecv0")})
        def prefetch(dead_bufs, l=l):
            tmp = Arena(ctx.mem.t, ctx.mem.n)
            w2b = tmp.alloc([8, 2048], BF16)
            wabb = tmp.alloc([8, 1024], BF16)
            woutb = tmp.alloc([8, 1024], BF16)
            p2_weight_dmas(S, w2b, wabb, woutb, L[l]["w2"], L[l]["wab"], L[l]["wout"], dead_bufs)
        T1["prefetch"] = prefetch
        emit_p1(ctx, l, T1)
        gather(ysend[1], yrecv[1], "ysend1", "yrecv1")
        T2 = dict(L[l])
        T2.update({"ysrc": [yrecv[0].ap(), yrecv[1].ap()], "ysrcA_buf": "yrecv0", "ysrcB_buf": "yrecv1",
                   "w_prefetched": True})
        if l == 0:
            T2["x"] = lambda t: (xown[t * 128:(t + 1) * 128, :], "xin")
            T2["xo"] = lambda t: (xnew[t // 4].ap()[(t % 4) * 128:(t % 4 + 1) * 128, :], "xo%d" % (t // 4))
            for k in range(4):
                T2["xo%d_buf" % k] = "xnew%d" % k
            T2["after_blk"] = lambda blk: gather(xnew[blk], xg[blk], "xnew%d" % blk, "xg%d" % blk)
        else:
            T2["x"] = lambda t: (xnew[t // 4].ap()[(t % 4) * 128:(t % 4 + 1) * 128, :], "xin%d" % (t // 4))
            for k in range(4):
                T2["xin%d_buf" % k] = "xnew%d" % k
            T2["xo"] = lambda t: (out[t * 128:(t + 1) * 128, :], "xo0")
            T2["xo0_buf"] = "out"
        emit_p2(ctx, T2)
    S.emit()
    return nc


def kernel(**inputs):
    inp = {k: np.asarray(v) for k, v in inputs.items()}
    maps = _host_maps(inp)
    nc = build_fused()
    res = run_bass_kernel_spmd(nc, maps, core_ids=list(range(8))).results
    out = np.empty((4, SEQ, DM), np.float32)
    for c in range(8):
        b, g = c // 2, c % 2
        out[b, g * HALF:(g + 1) * HALF] = res[c]["out"]
    return out
```

```python
import math
import contextlib
import ml_dtypes
import numpy as np
import concourse.bass as bass
import concourse.mybir as mybir
from concourse.bass_utils import run_bass_kernel_spmd

F32 = mybir.dt.float32
BF16 = mybir.dt.bfloat16
AF = mybir.ActivationFunctionType
ALU = mybir.AluOpType
AX = mybir.AxisListType


class Buf:
    __slots__ = ("name", "w", "r", "psum")

    def __init__(self, name="", psum=False):
        self.name = name
        self.w = []
        self.r = []
        self.psum = psum


class Op:
    __slots__ = ("eng", "fn", "deps", "is_dma", "need_inc", "cnt", "slot", "val", "pos", "is_cc")


ENGS = ("sync", "scalar", "vector", "gpsimd", "tensor")
NSLOT = 12


class Sched:
    def __init__(self, nc):
        self.nc = nc
        self.q = {e: [] for e in ENGS}
        self.ndma = {e: 0 for e in ENGS}

    def op(self, eng, fn, reads=(), writes=(), dma=False, cc=False):
        o = Op()
        o.is_cc = cc
        o.eng = eng
        o.fn = fn
        o.is_dma = dma
        o.need_inc = False
        o.cnt = 0
        o.deps = set()
        for b in reads:
            for w in b.w:
                o.deps.add(w)
            if b.psum:
                for r in b.r:
                    if r.eng != eng:
                        o.deps.add(r)
        for b in writes:
            append = dma and (not cc) and len(b.w) > 0 and (not b.r) and all(w.is_dma and not w.is_cc for w in b.w)
            if not append:
                for w in b.w:
                    o.deps.add(w)
            else:
                for w in b.w:
                    o.deps.update(d for d in w.deps)
            for r in b.r:
                o.deps.add(r)
        o.deps.discard(o)
        for b in reads:
            if b not in writes:
                b.r.append(o)
        for b in writes:
            append = dma and (not cc) and len(b.w) > 0 and (not b.r) and all(w.is_dma and not w.is_cc for w in b.w)
            if append:
                b.w.append(o)
            else:
                b.w = [o]
            b.r = []
        if cc:
            assert dma and eng == "gpsimd"
            self.ncc = getattr(self, "ncc", 0) + 1
            o.slot = -1
            o.val = self.ncc
        elif dma:
            n = self.ndma[eng]
            self.ndma[eng] = n + 1
            o.slot = n % NSLOT
            o.val = 16 * (n // NSLOT + 1)
        o.pos = len(self.q[eng])
        self.q[eng].append(o)
        return o

    def dma(self, eng, out, in_, reads=(), writes=(), **kw):
        return self.op(eng, lambda e: e.dma_start(out=out, in_=in_, **kw), reads, writes, dma=True)

    def emit(self):
        nc = self.nc
        for e in ENGS:
            for o in self.q[e]:
                for d in o.deps:
                    if d.is_dma:
                        continue
                    if d.eng == "tensor" and o.eng == "tensor":
                        continue
                    d.need_inc = True
        for e in ENGS:
            c = 0
            for o in self.q[e]:
                if (not o.is_dma) and o.need_inc:
                    c += 1
                    o.cnt = c
        import contextlib
        with contextlib.ExitStack() as st:
            esem = {e: st.enter_context(nc.semaphore("s_" + e)) for e in ENGS}
            dsem = {e: [st.enter_context(nc.semaphore("d_%s_%d" % (e, i))) for i in range(NSLOT)]
                    for e in ENGS if self.ndma[e] > 0}
            ccsem = st.enter_context(nc.semaphore("s_cc"))
            block = st.enter_context(nc.Block())

            def run(ename, eng):
                waited = {}

                def wait(sem, val):
                    k = sem.name
                    if waited.get(k, 0) >= val:
                        return
                    waited[k] = val
                    eng.wait_ge(sem, val)

                for o in self.q[ename]:
                    need = {}
                    for d in o.deps:
                        if d.is_cc:
                            sem, val = ccsem, d.val
                        elif d.is_dma:
                            sem, val = dsem[d.eng][d.slot], d.val
                        else:
                            if d.eng == "tensor" and ename == "tensor":
                                continue
                            sem, val = esem[d.eng], d.cnt
                        if sem.name not in need or need[sem.name][1] < val:
                            need[sem.name] = (sem, val)
                    for nm in sorted(need):
                        wait(*need[nm])
                    if o.is_cc:
                        o.fn(eng).then_inc(ccsem, 1)
                    elif o.is_dma:
                        if o.val > 16:
                            wait(dsem[ename][o.slot], o.val - 16)
                        o.fn(eng).then_inc(dsem[ename][o.slot], 16)
                    else:
                        ins = o.fn(eng)
                        if o.need_inc:
                            ins.then_inc(esem[ename], 1)
                n = self.ndma[ename]
                for s in range(min(n, NSLOT)):
                    last = ((n - 1 - s) // NSLOT) * NSLOT + s
                    wait(dsem[ename][s], 16 * (last // NSLOT + 1))

            @block.sync
            def _(eng):
                run("sync", eng)

            @block.scalar
            def _(eng):
                run("scalar", eng)

            @block.vector
            def _(eng):
                run("vector", eng)

            @block.gpsimd
            def _(eng):
                run("gpsimd", eng)

            @block.tensor
            def _(eng):
                run("tensor", eng)

    def barrier(self):
        last = []
        for e in ENGS:
            seen = set()
            got_c = False
            got_cc = False
            for o in reversed(self.q[e]):
                if o.is_cc:
                    continue
                elif o.is_dma:
                    if o.slot not in seen:
                        seen.add(o.slot)
                        last.append(o)
                elif not got_c:
                    last.append(o)
                    got_c = True
                if got_c and len(seen) >= NSLOT:
                    break
        for e in ENGS:
            o = self.op(e, lambda eng: eng.nop(), (), ())
            o.deps = set(last)


EPS = 1e-6
SEQ = 4096
DM = 1024
HALF = 2048


def _consts_f32():
    ident = np.eye(128, dtype=np.float32)
    tri = np.triu(np.ones((128, 128), dtype=np.float32))
    ones = np.ones((128, 128), dtype=np.float32)
    return np.concatenate([ident, tri, ones], axis=1)


def p2_weight_dmas(S, w2b, wabb, woutb, w2, wab, wout, wbufs):
    for kc in range(8):
        for c0 in range(0, 2048, 1024):
            S.dma("gpsimd", w2b[:, kc, c0:c0 + 1024], w2[kc * 128:(kc + 1) * 128, c0:c0 + 1024], writes=wbufs)
    for kc in range(8):
        S.dma("gpsimd", wabb[:, kc, :], wab[kc * 128:(kc + 1) * 128, :], writes=wbufs)
    for kc in range(8):
        S.dma("gpsimd", woutb[:, kc, :], wout[kc * 128:(kc + 1) * 128, :], writes=wbufs)


def emit_p2(ctx, T):
    nc, S, ps, bank, mem = ctx.nc, ctx.S, ctx.ps, ctx.bank, ctx.mem
    x, ysrc, sel, w2, wab, wout, gpre, gpost, xo = (T[k] for k in
                                                     ("x", "ysrc", "sel", "w2", "wab", "wout", "gpre", "gpost", "xo"))
    cst_t, idb = ctx.cst_t, ctx.idb
    mem.reset()
    w2b = mem.alloc([8, 2048], BF16)
    wabb = mem.alloc([8, 1024], BF16)
    woutb = mem.alloc([8, 1024], BF16)
    gpre_t = mem.alloc([DM], F32)
    gpost_t = mem.alloc([DM], F32)
    sel_t = mem.alloc([8], F32)
    xt = [[mem.alloc([DM], F32) for t in range(4)] for i in range(2)]
    junk = mem.alloc([DM], BF16)
    hb = [mem.alloc([DM], BF16) for i in range(2)]
    hT = [mem.alloc([8, 512], BF16) for i in range(2)]
    yt = [mem.alloc([8, 512], BF16) for i in range(2)]
    yh = [mem.alloc([8, 512], BF16) for i in range(2)]
    sg = mem.alloc([16, 512], BF16)
    t1 = [mem.alloc([512], F32) for i in range(2)]
    t2 = [mem.alloc([512], F32) for i in range(2)]
    mT = mem.alloc([8, 512], BF16)
    xn = [mem.alloc([DM], F32) for i in range(2)]
    ss = [mem.alloc([8], F32)[:, 0:1] for i in range(4)]
    rs = [mem.alloc([8], F32)[:, 0:1] for i in range(4)]
    B = {}

    def buf(name):
        if name in PERSIST:
            return ctx.pb(T.get(name + "_buf", name))
        if name not in B:
            B[name] = Buf(name)
        return B[name]

    S.dma("sync", gpre_t[:], gpre, writes=[buf("gpre")])
    S.dma("sync", gpost_t[:], gpost, writes=[buf("gpost")])
    S.dma("sync", sel_t[:, 0:2], sel, writes=[buf("sel")])

    if not T.get("w_prefetched"):
        p2_weight_dmas(S, w2b, wabb, woutb, w2, wab, wout, [buf("w2b"), buf("wabb"), buf("woutb")])

    ri = [0]

    def prologue(blk):
        xs = xt[blk % 2]
        hTb = hT[blk % 2]
        for t in range(4):
            xb = buf("xt%d_%d" % (blk % 2, t))
            xsrc_ap, xsrc_buf = x(blk * 4 + t)
            S.dma("sync", xs[t][:], xsrc_ap, reads=[buf(xsrc_buf)], writes=[xb])
            k = ri[0] % 4
            ri[0] += 1
            S.op("scalar", lambda e, t=t, k=k, xs=xs: e.activation(junk[:], xs[t][:], AF.Square, accum_out=ss[k]),
                 [xb], [buf("junk"), buf("ss%d" % k)])
            S.op("scalar", lambda e, k=k: e.activation(rs[k], ss[k], AF.Sqrt, scale=1.0 / DM, bias=EPS),
                 [buf("ss%d" % k)], [buf("rs%d" % k)])
            S.op("vector", lambda e, k=k: e.reciprocal(rs[k], rs[k]), [buf("rs%d" % k)], [buf("rs%d" % k)])
            hbi = t % 2
            S.op("vector", lambda e, t=t, k=k, xs=xs, hbi=hbi: e.scalar_tensor_tensor(
                hb[hbi][:], xs[t][:], rs[k], gpre_t[:], ALU.mult, ALU.mult),
                [xb, buf("rs%d" % k), buf("gpre")], [buf("hb%d" % hbi)])
            pb = 6 + (t % 2)
            pT = ps[:, pb, 0:512].bitcast(BF16)
            for c in range(8):
                S.op("tensor", lambda e, c=c, hbi=hbi, pT=pT: e.transpose(
                    pT[:, c * 128:(c + 1) * 128], hb[hbi][:, c * 128:(c + 1) * 128], idb[:]),
                    [buf("hb%d" % hbi), buf("idb")], [bank[pb]])
            S.op("scalar", lambda e, t=t, pT=pT, hTb=hTb: e.activation(
                hTb[:, :, t * 128:(t + 1) * 128], pT.rearrange("p (c t) -> p c t", c=8), AF.Copy),
                [bank[pb]], [buf("hT%d" % (blk % 2))])
        yb = buf("yt%d" % (blk % 2))
        ytb = yt[blk % 2]
        for hh in range(2):
            for ab in range(2):
                for r in range(2):
                    src = ysrc[ab][r * 256:(r + 1) * 256, hh * HALF + blk * 512:hh * HALF + (blk + 1) * 512].rearrange(
                        "(i p) t -> p i t", p=128)
                    c0_ = ab * 4 + r * 2
                    S.dma("sync", yh[hh][:, c0_:c0_ + 2, :], src, reads=[buf("ysrcA" if ab == 0 else "ysrcB")],
                          writes=[buf("yh%d" % hh)])
        S.op("vector", lambda e, ytb=ytb: e.tensor_scalar(ytb[:], yh[0][:], sel_t[:, 0:1], None, ALU.mult),
             [buf("yh0"), buf("sel")], [yb])
        S.op("vector", lambda e, ytb=ytb: e.scalar_tensor_tensor(ytb[:], yh[1][:], sel_t[:, 1:2], ytb[:], ALU.mult, ALU.add),
             [buf("yh1"), buf("sel"), yb], [yb])

    def gates(blk):
        hTb = hT[blk % 2]
        for fg in range(16):
            pb = fg % 2
            for kc in range(8):
                S.op("tensor", lambda e, fg=fg, kc=kc, pb=pb, hTb=hTb: e.matmul(
                    ps[:, pb, :], w2b[:, kc, fg * 128:(fg + 1) * 128], hTb[:, kc, :], start=(kc == 0), stop=(kc == 7)),
                    [buf("w2b"), buf("hT%d" % (blk % 2))], [bank[pb]])
            S.op("scalar", lambda e, fg=fg, pb=pb: e.activation(sg[:, fg, :], ps[:, pb, :], AF.Sigmoid),
                 [bank[pb]], [buf("sg")])

    def merge_out(blk):
        xs = xt[blk % 2]
        yb = buf("yt%d" % (blk % 2))
        ytb = yt[blk % 2]
        for dg in range(8):
            mb = 2 if dg % 2 == 0 else 0
            for ab in range(2):
                pb = mb + ab
                for kc in range(4):
                    S.op("tensor", lambda e, dg=dg, ab=ab, kc=kc, pb=pb, ytb=ytb: e.matmul(
                        ps[:, pb, :], wabb[:, ab * 4 + kc, dg * 128:(dg + 1) * 128], ytb[:, ab * 4 + kc, :],
                        start=(kc == 0), stop=(kc == 3)),
                        [buf("wabb"), yb], [bank[pb]])
            i = dg % 2
            S.op("vector", lambda e, dg=dg, i=i, mb=mb: e.tensor_tensor(t1[i][:], ps[:, mb, :], sg[:, dg, :], ALU.mult),
                 [bank[mb], buf("sg")], [buf("t1_%d" % i)])
            S.op("vector", lambda e, dg=dg, i=i, mb=mb: e.tensor_tensor(t2[i][:], ps[:, mb + 1, :], sg[:, 8 + dg, :], ALU.mult),
                 [bank[mb + 1], buf("sg")], [buf("t2_%d" % i)])
            S.op("gpsimd", lambda e, dg=dg, i=i: e.tensor_tensor(mT[:, dg, :], t1[i][:], t2[i][:], ALU.add),
                 [buf("t1_%d" % i), buf("t2_%d" % i)], [buf("mT")])
        for t in range(4):
            xb = buf("xt%d_%d" % (blk % 2, t))
            ob = 4 if t % 2 == 0 else 2
            for nb in range(2):
                for kc in range(8):
                    S.op("tensor", lambda e, t=t, nb=nb, kc=kc, ob=ob: e.matmul(
                        ps[:, ob + nb, :], mT[:, kc, t * 128:(t + 1) * 128], woutb[:, kc, nb * 512:(nb + 1) * 512],
                        start=(kc == 0), stop=(kc == 7)),
                        [buf("mT"), buf("woutb")], [bank[ob + nb]])
            k = ri[0] % 4
            ri[0] += 1
            po = ps[:, ob:ob + 2, :].rearrange("p a b -> p (a b)")
            S.op("scalar", lambda e, k=k, po=po: e.activation(junk[:], po, AF.Square, accum_out=ss[k]),
                 [bank[ob], bank[ob + 1]], [buf("junk"), buf("ss%d" % k)])
            S.op("scalar", lambda e, k=k: e.activation(rs[k], ss[k], AF.Sqrt, scale=1.0 / DM, bias=EPS),
                 [buf("ss%d" % k)], [buf("rs%d" % k)])
            S.op("vector", lambda e, k=k: e.reciprocal(rs[k], rs[k]), [buf("rs%d" % k)], [buf("rs%d" % k)])
            xi = t % 2
            S.op("vector", lambda e, k=k, xi=xi, po=po: e.scalar_tensor_tensor(
                xn[xi][:], po, rs[k], gpost_t[:], ALU.mult, ALU.mult),
                [bank[ob], bank[ob + 1], buf("rs%d" % k), buf("gpost")], [buf("xn%d" % xi)])
            S.op("gpsimd", lambda e, xi=xi, t=t, xs=xs: e.tensor_tensor(xn[xi][:], xn[xi][:], xs[t][:], ALU.add),
                 [buf("xn%d" % xi), xb], [buf("xn%d" % xi)])
            xdst_ap, xdst_buf = xo(blk * 4 + t)
            S.dma("gpsimd", xdst_ap, xn[xi][:], reads=[buf("xn%d" % xi)], writes=[buf(xdst_buf)])
        if T.get("after_blk") is not None:
            T["after_blk"](blk)

    prologue(0)
    for blk in range(4):
        gates(blk)
        if blk + 1 < 4:
            prologue(blk + 1)
        merge_out(blk)
    S.barrier()


class Arena:
    def __init__(self, t, nbytes):
        self.t = t
        self.n = nbytes
        self.off = 0

    def reset(self):
        self.off = 0

    def mark(self):
        return self.off

    def release(self, m):
        self.off = m

    def alloc(self, shape, dtype):
        esz = 4 if dtype == F32 else 2
        n = int(np.prod(shape)) * esz
        self.off = (self.off + 31) // 32 * 32
        a = self.off
        assert a + n <= self.n, ("arena overflow", a, n, self.n)
        self.off = a + n
        v = self.t[:, a // 2:(a + n) // 2]
        if dtype == F32:
            v = v.bitcast(F32)
        if len(shape) == 2:
            v = v.rearrange("p (a b) -> p a b", a=shape[0])
        elif len(shape) == 3:
            v = v.rearrange("p (a b c) -> p a b c", a=shape[0], b=shape[1])
        return v


SBUF_ARENA_BYTES = 210432


class Ctx:
    def __init__(self, nc, cst):
        self.nc = nc
        self.S = Sched(nc)
        sb = nc.alloc_sbuf_tensor
        self.cst_t = sb("cst_t", [128, 384], F32)
        self.idb = sb("idb", [128, 128], BF16)
        self.trib = sb("trib", [128, 128], BF16)
        self.mem = Arena(sb("arena", [128, SBUF_ARENA_BYTES // 2], BF16), SBUF_ARENA_BYTES)
        self.ps = nc.alloc_psum_tensor("ps", [128, 8, 512], F32)
        self.bank = [Buf("bank%d" % i, psum=True) for i in range(8)]
        self.pbuf = {}
        S = self.S
        S.dma("sync", self.cst_t[:], cst, writes=[self.pb("cst")])
        S.op("vector", lambda e: e.tensor_copy(self.idb[:], self.cst_t[:, 0:128]), [self.pb("cst")], [self.pb("idb")])
        S.op("vector", lambda e: e.tensor_copy(self.trib[:], self.cst_t[:, 128:256]), [self.pb("cst")], [self.pb("trib")])

    def pb(self, name):
        if name not in self.pbuf:
            self.pbuf[name] = Buf(name)
        return self.pbuf[name]


PERSIST = ("cst", "idb", "trib", "yA", "yB", "ysrcA", "ysrcB", "xin", "xo0", "xo1", "xo2", "xo3", "xin0", "xin1", "xin2", "xin3")


W1_COLS = 2308
ARENA_BYTES = 102 * 1024


def emit_p1(ctx, layer, T):
    lam_init = 0.8 - 0.6 * math.exp(-0.3 * layer)
    nc, S, ps, bank, mem = ctx.nc, ctx.S, ctx.ps, ctx.bank, ctx.mem
    x, w1, gpre, aug, abias, na, nb, convw, bif, lq, yTo = (T[k] for k in
                                                             ("x", "w1", "gpre", "aug", "abias", "na", "nb", "convw", "bif", "lq", "yTo"))
    cst_t, idb, trib = ctx.cst_t, ctx.idb, ctx.trib
    mem.reset()
    hT = mem.alloc([8, SEQ], BF16)
    wb = mem.alloc([8, 1284], BF16)
    gpre_t = mem.alloc([DM], F32)
    abias_t = mem.alloc([16], F32)
    na_t = mem.alloc([256], F32)
    nb_t = mem.alloc([256], F32)
    convw_t = mem.alloc([16], F32)
    bif_t = mem.alloc([8], F32)
    lq_t = mem.alloc([256], F32)
    sm = mem.alloc([64], F32)
    ss = [mem.alloc([8], F32) for i in range(4)]
    rs = [mem.alloc([8], F32) for i in range(4)]
    ar = mem
    stage_mark = mem.mark()
    B = {}
    tri_f = cst_t[:, 128:256]
    ones_f = cst_t[:, 256:384]
    xin_name = T.get("xin_name", "xin")

    def buf(name):
        if name in PERSIST:
            return ctx.pb(T.get(name + "_buf", name))
        if name not in B:
            B[name] = Buf(name)
        return B[name]

    def act(out, in_, func, reads, writes, **kw):
        return S.op("scalar", lambda e: e.activation(out, in_, func, **kw), reads, writes)

    def dve(method, *args, reads=(), writes=(), **kw):
        return S.op("vector", lambda e: getattr(e, method)(*args, **kw), reads, writes)

    def pool(method, *args, reads=(), writes=(), **kw):
        return S.op("gpsimd", lambda e: getattr(e, method)(*args, **kw), reads, writes)

    def mm(out, lhsT, rhs, reads, writes, **kw):
        return S.op("tensor", lambda e: e.matmul(out, lhsT, rhs, **kw), reads, writes)

    def tp(out, in_, reads, writes):
        return S.op("tensor", lambda e: e.transpose(out, in_, idb[:]), list(reads) + [buf("idb")], writes)

    for name, t, src in (("gpre", gpre_t, gpre), ("abias", abias_t, abias), ("na", na_t, na),
                         ("nb", nb_t, nb), ("convw", convw_t, convw), ("lq", lq_t, lq)):
        S.dma("sync", t[:], src, writes=[buf(name)])
    S.dma("sync", bif_t[:, 0:4], bif, writes=[buf("bif")])

    si = [0]

    def load_w(c0, ncols):
        for kc in range(8):
            S.dma("gpsimd", wb[:, kc, 0:ncols], w1[kc * 128:(kc + 1) * 128, c0:c0 + ncols], writes=[buf("wb")])

    load_w(0, 1024)

    ri = [0]

    def rstd_from(ssk, k, n):
        act(rs[k][:, 0:1], ssk, AF.Sqrt, [buf("ss%d" % k)], [buf("rs%d" % k)], scale=1.0 / n, bias=EPS)
        dve("reciprocal", rs[k][:, 0:1], rs[k][:, 0:1], reads=[buf("rs%d" % k)], writes=[buf("rs%d" % k)])

    mem.release(stage_mark)
    xt = [ar.alloc([DM], F32) for _ in range(6)]
    junk = ar.alloc([DM], BF16)
    hb = [ar.alloc([DM], BF16) for _ in range(2)]
    kk = {}

    pos = {}

    def st0_l(t):
        pos[t] = len(pos)
        i = pos[t] % 6
        xb = buf("xt%d" % i)
        xsrc_ap, xsrc_buf = x(t)
        S.dma("sync", xt[i], xsrc_ap, reads=[buf(xsrc_buf)], writes=[xb])
        k = ri[0] % 4
        ri[0] += 1
        kk[t] = k
        act(junk, xt[i], AF.Square, [xb], [buf("junk"), buf("ss%d" % k)], accum_out=ss[k][:, 0:1])
        rstd_from(ss[k][:, 0:1], k, DM)

    def st0_m(t):
        i, j, k = pos[t] % 6, pos[t] % 2, kk[t]
        dve("scalar_tensor_tensor", hb[j], xt[i], rs[k][:, 0:1], gpre_t[:], ALU.mult, ALU.mult,
            reads=[buf("xt%d" % i), buf("rs%d" % k), buf("gpre")], writes=[buf("hb%d" % j)])

    def st0_n(t):
        j = pos[t] % 2
        pb = 6 + j
        pT = ps[:, pb, 0:512].bitcast(BF16)
        for c in range(8):
            tp(pT[:, c * 128:(c + 1) * 128], hb[j][:, c * 128:(c + 1) * 128], [buf("hb%d" % j)], [bank[pb]])
        if pos[t] % 2 == 0:
            act(hT[:, :, t * 128:(t + 1) * 128], pT.rearrange("p (c t) -> p c t", c=8), AF.Copy, [bank[pb]], [buf("hT")])
        else:
            dve("tensor_copy", hT[:, :, t * 128:(t + 1) * 128], pT.rearrange("p (c t) -> p c t", c=8),
                reads=[bank[pb]], writes=[buf("hT")])

    order = list(range(0, 12)) + list(range(16, 28)) + list(range(12, 16)) + list(range(28, 32))
    for m in range(32 + 2):
        if m < 32:
            st0_l(order[m])
        if 1 <= m <= 32:
            st0_m(order[m - 1])
        if m >= 2:
            st0_n(order[m - 2])
    S.barrier()

    mem.release(stage_mark)
    QT = [ar.alloc([SEQ], BF16) for _ in range(2)]
    KT = [ar.alloc([SEQ], BF16) for _ in range(2)]
    Vaug = ar.alloc([32, 2, 129], BF16)
    gz = ar.alloc([32, 256], BF16)
    PT = [ar.alloc([2, 512], BF16) for _ in range(3)]
    yTs = ar.alloc([SEQ], BF16)
    otmp = [ar.alloc([128], F32) for _ in range(4)]
    ot = [ar.alloc([128], F32) for _ in range(4)]
    Osb = [ar.alloc([258], F32) for _ in range(2)]
    ssn = [ar.alloc([8], F32) for _ in range(4)]
    rsn = [ar.alloc([8], F32) for _ in range(4)]
    yab = [ar.alloc([128], BF16) for _ in range(4)]
    zt = [ar.alloc([256], F32) for _ in range(2)]
    junkf = ar.alloc([128], F32)
    for i in range(2):
        pool("memset", QT[i], 0.0, writes=[buf("QT%d" % i)])
        pool("memset", KT[i], 0.0, writes=[buf("KT%d" % i)])
    pool("memset", Vaug, 1.0, writes=[buf("Vaug")])
    lp = zt[0]
    dve("tensor_tensor", lp[:, 0:64], lq_t[:, 0:64], lq_t[:, 64:128], ALU.mult, reads=[buf("lq")], writes=[buf("zt0")])
    dve("tensor_tensor", lp[:, 64:128], lq_t[:, 128:192], lq_t[:, 192:256], ALU.mult, reads=[buf("lq")], writes=[buf("zt0")])
    dve("reduce_sum", sm[:, 1:2], lp[:, 0:64], AX.X, reads=[buf("zt0")], writes=[buf("sm1")])
    dve("reduce_sum", sm[:, 2:3], lp[:, 64:128], AX.X, reads=[buf("zt0")], writes=[buf("sm2")])
    act(sm[:, 1:2], sm[:, 1:2], AF.Exp, [buf("sm1")], [buf("sm1")])
    act(sm[:, 2:3], sm[:, 2:3], AF.Exp, [buf("sm2")], [buf("sm2")])
    dve("tensor_tensor", sm[:, 0:1], sm[:, 2:3], sm[:, 1:2], ALU.subtract, reads=[buf("sm1"), buf("sm2")], writes=[buf("neglam")])
    dve("tensor_scalar", sm[:, 0:1], sm[:, 0:1], -lam_init, None, ALU.add, reads=[buf("neglam")], writes=[buf("neglam")])
    neglam = sm[:, 0:1]

    pr = [0]

    def next_bank():
        b = pr[0] % 6
        pr[0] += 1
        return b

    for t in range(32):
        pb = next_bank()
        for kc in range(8):
            mm(ps[:, pb, :], hT[:, kc, t * 128:(t + 1) * 128], wb[:, kc, 512:1024], [buf("hT"), buf("wb")], [bank[pb]],
               start=(kc == 0), stop=(kc == 7))
        dve("tensor_copy", Vaug[:, t, :, 0:128], ps[:, pb, 0:256].rearrange("p (h e) -> p h e", h=2),
            reads=[bank[pb]], writes=[buf("Vaug")])
        zi = t % 2
        act(zt[zi], ps[:, pb, 256:512], AF.Silu, [bank[pb]], [buf("zt%d" % zi)])
        dve("scalar_tensor_tensor", gz[:, t, :], zt[zi], 1.0 - lam_init, na_t[:], ALU.mult, ALU.mult,
            reads=[buf("zt%d" % zi), buf("na")], writes=[buf("gz")])

    gcnt = [0]
    for hl in range(2):
        S.dma("sync", QT[0][64:68, :], aug[hl, 0], writes=[buf("QT0")])
        S.dma("sync", QT[1][0:4, :], aug[hl, 0], writes=[buf("QT1")])
        S.dma("sync", KT[0][64:68, :], aug[hl, 1], writes=[buf("KT0")])
        S.dma("sync", KT[1][0:4, :], aug[hl, 1], writes=[buf("KT1")])
        for blk in range(8):
            for qk in range(2):
                c0 = qk * 256 + hl * 128
                dst = QT if qk == 0 else KT
                nm = "QT" if qk == 0 else "KT"
                pb = next_bank()
                for kc in range(8):
                    mm(ps[:, pb, :], wb[:, kc, c0:c0 + 128], hT[:, kc, blk * 512:(blk + 1) * 512],
                       [buf("wb"), buf("hT")], [bank[pb]], start=(kc == 0), stop=(kc == 7))
                act(dst[0][0:64, blk * 512:(blk + 1) * 512], ps[0:64, pb, :], AF.Copy, [bank[pb]], [buf(nm + "0")])
                dve("tensor_copy", dst[1][64:128, blk * 512:(blk + 1) * 512], ps[64:128, pb, :],
                    reads=[bank[pb]], writes=[buf(nm + "1")])
        if hl == 1:
            load_w(1024, 1284)
        jobs = []
        for I in range(8):
            for c in range(2):
                groups = []
                for J in range(I):
                    for hf in range(2):
                        groups.append((False, I - J, [4 * J + 2 * hf, 4 * J + 2 * hf + 1]))
                for hf in range(2):
                    groups.append((True, 0, [4 * I + 2 * hf, 4 * I + 2 * hf + 1]))
                for gx, (diag, d, tiles) in enumerate(groups):
                    jobs.append((I, c, diag, d, tiles, gx == 0, gx == len(groups) - 1))
        NJ = len(jobs)
        pending_tp = []

        def rec_st(n):
            I, c, diag, d, tiles, first, last = jobs[n]
            gi = n % 3
            b0 = 2 * gi
            for jj, j in enumerate(tiles):
                n0 = 128 * (j - 4 * I) if diag else 0
                mm(ps[:, b0 + jj, n0:512], KT[c][:, j * 128:(j + 1) * 128], QT[c][:, I * 512 + n0:(I + 1) * 512],
                   [buf("KT%d" % c), buf("QT%d" % c)], [bank[b0 + jj]], start=True, stop=True)

        def rec_exp(n):
            I, c, diag, d, tiles, first, last = jobs[n]
            gi = n % 3
            b0 = 2 * gi
            ptbs = [buf("PT%d_%d" % (gi, jj)) for jj in range(2)]
            bcol = abias_t[:, hl * 8 + d:hl * 8 + d + 1]
            if not diag:
                act(PT[gi], ps[:, b0:b0 + 2, :], AF.Exp, [bank[b0], bank[b0 + 1], buf("abias")], ptbs,
                    scale=0.125, bias=bcol)
            else:
                for jj, j in enumerate(tiles):
                    n0 = 128 * (j - 4 * I)
                    act(PT[gi][:, jj, n0:512], ps[:, b0 + jj, n0:512], AF.Exp, [bank[b0 + jj], buf("abias")], [ptbs[jj]],
                        scale=0.125, bias=bcol)
                    dve("tensor_tensor", PT[gi][:, jj, n0:n0 + 128], PT[gi][:, jj, n0:n0 + 128], trib[:], ALU.mult,
                        reads=[ptbs[jj], buf("trib")], writes=[ptbs[jj]])

        def rec_pv(n, m):
            I, c, diag, d, tiles, first, last = jobs[n]
            gi = n % 3
            b0 = 2 * gi
            for jj, j in enumerate(tiles):
                ptb = buf("PT%d_%d" % (gi, jj))
                r = j - 4 * I
                us = range(r, 4) if diag else range(4)
                for u in us:
                    ob_ = 6 + u // 2
                    oc = (u % 2) * 129
                    mm(ps[:, ob_, oc:oc + 129], PT[gi][:, jj, u * 128:(u + 1) * 128], Vaug[:, j, hl, :],
                       [ptb, buf("Vaug")], [bank[ob_]],
                       start=(first and jj == 0 and u in (0, 2)), stop=(diag and r == u), skip_group_check=True)
            if not last:
                return
            for bb in range(2):
                dve("tensor_copy", Osb[bb], ps[:, 6 + bb, 0:258], reads=[bank[6 + bb]], writes=[buf("Osb%d" % bb)])
            for u in range(4):
                bb = u // 2
                oc = (u % 2) * 129
                O = Osb[bb][:, oc:oc + 128]
                Osum = Osb[bb][:, oc + 128:oc + 129]
                osb = buf("Osb%d" % bb)
                k = ri[0] % 4
                ri[0] += 1
                dve("reciprocal", rs[k][:, 0:1], Osum, reads=[osb], writes=[buf("rs%d" % k)])
                if c == 0:
                    dve("tensor_scalar", otmp[u], O, rs[k][:, 0:1], None, ALU.mult,
                        reads=[osb, buf("rs%d" % k)], writes=[buf("otmp%d" % u)])
                else:
                    dve("tensor_tensor", rs[k][:, 0:1], rs[k][:, 0:1], neglam, ALU.mult,
                        reads=[buf("rs%d" % k), buf("neglam")], writes=[buf("rs%d" % k)])
                    dve("scalar_tensor_tensor", ot[u], O, rs[k][:, 0:1], otmp[u], ALU.mult, ALU.add,
                        reads=[osb, buf("rs%d" % k), buf("otmp%d" % u)], writes=[buf("ot%d" % u)])
            if c == 1:
                for u in range(4):
                    k2 = u
                    dve("scalar_tensor_tensor", junkf, ot[u], 1.0, ot[u], ALU.mult, ALU.mult,
                        reads=[buf("ot%d" % u)], writes=[buf("junkf"), buf("ssn%d" % k2)], accum_out=ssn[k2][:, 0:1])
                for u in range(4):
                    k2 = u
                    act(rsn[k2][:, 0:1], ssn[k2][:, 0:1], AF.Ln, [buf("ssn%d" % k2)], [buf("rsn%d" % k2)], scale=1.0 / 128, bias=EPS)
                    act(rsn[k2][:, 0:1], rsn[k2][:, 0:1], AF.Exp, [buf("rsn%d" % k2)], [buf("rsn%d" % k2)], scale=-0.5)
                for u in range(4):
                    k2 = u
                    tt = 4 * I + u
                    dve("scalar_tensor_tensor", yab[u], ot[u], rsn[k2][:, 0:1], gz[:, tt, hl * 128:(hl + 1) * 128],
                        ALU.mult, ALU.mult, reads=[buf("ot%d" % u), buf("rsn%d" % k2), buf("gz")], writes=[buf("yab%d" % u)])
                pending_tp.append((m + 2, I, b0))

        def flush_tp(m, force=False):
            while pending_tp and (force or pending_tp[0][0] <= m):
                _, I_, b0 = pending_tp.pop(0)
                pT4 = ps[:, b0, 0:256].bitcast(BF16)
                for u in range(4):
                    tp(pT4[:, u * 128:(u + 1) * 128], yab[u], [buf("yab%d" % u)], [bank[b0]])
                act(yTs[:, 4 * I_ * 128:(4 * I_ + 4) * 128], pT4, AF.Copy, [bank[b0]], [buf("yTs")])

        for m in range(NJ + 2):
            if m < NJ:
                rec_st(m)
            if 1 <= m <= NJ:
                rec_exp(m - 1)
            if m >= 2:
                rec_pv(m - 2, m)
            flush_tp(m)
        flush_tp(0, force=True)
        S.dma("gpsimd", yTo[0][hl * 128:(hl + 1) * 128, :], yTs, reads=[buf("yTs")], writes=[buf("yA")])
    if T.get("after_A") is not None:
        T["after_A"]()
    S.barrier()

    mem.release(stage_mark)
    qbT = [ar.alloc([SEQ], BF16) for _ in range(2)]
    kbT = [ar.alloc([SEQ], BF16) for _ in range(2)]
    vbaug = ar.alloc([32, 2, 129], BF16)
    sgo = ar.alloc([32, 256], BF16)
    gzb = ar.alloc([32, 256], BF16)
    yTb = [ar.alloc([1024], BF16) for _ in range(2)]
    xpre = [ar.alloc([515], F32) for _ in range(2)]
    ct = [ar.alloc([512], F32) for _ in range(2)]
    zt = [ar.alloc([256], F32) for _ in range(2)]
    gts = ar.alloc([32, 4], F32)
    IG = ar.alloc([2, 32], F32)
    LF = ar.alloc([2, 32], F32)
    EB = ar.alloc([64], F32)
    DEC = ar.alloc([64], F32)
    EK = ar.alloc([64], F32)
    SmT = [ar.alloc([128], BF16) for _ in range(4)]
    Ktm = [ar.alloc([128], BF16) for _ in range(4)]
    G = [ar.alloc([129], F32) for _ in range(2)]
    Cbf = [ar.alloc([129], BF16) for _ in range(2)]
    hbt = [ar.alloc([128], F32) for _ in range(2)]
    ybb = [ar.alloc([128], BF16) for _ in range(4)]
    dn = [ar.alloc([4], F32) for _ in range(2)]
    junkf3 = [ar.alloc([128], F32) for _ in range(2)]
    pool("memset", vbaug, 1.0, writes=[buf("vbaug")])
    for t in range(32):
        pb = next_bank()
        for kc in range(8):
            mm(ps[:, pb, :], hT[:, kc, t * 128:(t + 1) * 128], wb[:, kc, 512:1024], [buf("hT"), buf("wb")], [bank[pb]],
               start=(kc == 0), stop=(kc == 7))
        dve("tensor_copy", vbaug[:, t, :, 0:128], ps[:, pb, 0:256].rearrange("p (h e) -> p h e", h=2),
            reads=[bank[pb]], writes=[buf("vbaug")])
        act(sgo[:, t, :], ps[:, pb, 256:512], AF.Sigmoid, [bank[pb]], [buf("sgo")])
    for t in range(32):
        pb = next_bank()
        for kc in range(8):
            mm(ps[:, pb, 0:260], hT[:, kc, t * 128:(t + 1) * 128], wb[:, kc, 1024:1284], [buf("hT"), buf("wb")], [bank[pb]],
               start=(kc == 0), stop=(kc == 7))
        zi = t % 2
        act(zt[zi], ps[:, pb, 0:256], AF.Silu, [bank[pb]], [buf("zt%d" % zi)])
        dve("tensor_tensor", gzb[:, t, :], zt[zi], nb_t[:], ALU.mult, reads=[buf("zt%d" % zi), buf("nb")], writes=[buf("gzb")])
        dve("tensor_copy", gts[:, t, :], ps[:, pb, 256:260], reads=[bank[pb]], writes=[buf("gts")])
    for h in range(2):
        dve("tensor_scalar", IG[:, h, :], gts[:, :, h], bif_t[:, h:h + 1], None, ALU.add,
            reads=[buf("gts"), buf("bif")], writes=[buf("IG")])
        dve("tensor_scalar", LF[:, h, :], gts[:, :, 2 + h], bif_t[:, 2 + h:3 + h], None, ALU.add,
            reads=[buf("gts"), buf("bif")], writes=[buf("LF")])
    LFf = LF.rearrange("p a b -> p (a b)")
    IGf = IG.rearrange("p a b -> p (a b)")
    act(LFf, LFf, AF.Exp, [buf("LF")], [buf("LF")], scale=-1.0)
    act(LFf, LFf, AF.Ln, [buf("LF")], [buf("LF")], bias=1.0)
    dve("tensor_scalar", LFf, LFf, -1.0, None, ALU.mult, reads=[buf("LF")], writes=[buf("LF")])
    mm(ps[:, 6, 0:64], tri_f, LFf, [buf("cst"), buf("LF")], [bank[6]], start=True, stop=True)
    mm(ps[:, 7, 0:64], ones_f, LFf, [buf("cst"), buf("LF")], [bank[7]], start=True, stop=True)
    act(EB, ps[:, 6, 0:64], AF.Exp, [bank[6]], [buf("EB")])
    act(DEC, ps[:, 7, 0:64], AF.Exp, [bank[7]], [buf("DEC")])
    dve("tensor_tensor", EK, IGf, ps[:, 6, 0:64], ALU.subtract, reads=[buf("IG"), bank[6]], writes=[buf("EK")])
    act(EK, EK, AF.Exp, [buf("EK")], [buf("EK")], bias=math.log(128.0 ** -0.5))
    xi = [0]
    for grp in range(4):
        dst = (qbT if grp < 2 else kbT)[grp % 2]
        dnm = ("qbT" if grp < 2 else "kbT") + str(grp % 2)
        for blk in range(8):
            pb = next_bank()
            for kc in range(8):
                mm(ps[:, pb, :], wb[:, kc, grp * 128:(grp + 1) * 128], hT[:, kc, blk * 512:(blk + 1) * 512],
                   [buf("wb"), buf("hT")], [bank[pb]], start=(kc == 0), stop=(kc == 7))
            i = xi[0] % 2
            xi[0] += 1
            xp = xpre[i]
            xpb = buf("xpre%d" % i)
            if blk == 0:
                pool("memset", xp[:, 0:3], 0.0, writes=[xpb])
            else:
                pool("tensor_copy", xp[:, 0:3], xpre[1 - i][:, 512:515], reads=[buf("xpre%d" % (1 - i))], writes=[xpb])
            act(xp[:, 3:515], ps[:, pb, :], AF.Copy, [bank[pb]], [xpb])
            cw = convw_t[:, grp * 4:(grp + 1) * 4]
            dve("tensor_scalar", ct[0], xp[:, 0:512], cw[:, 0:1], None, ALU.mult, reads=[xpb, buf("convw")], writes=[buf("ct0")])
            dve("scalar_tensor_tensor", ct[1], xp[:, 1:513], cw[:, 1:2], ct[0], ALU.mult, ALU.add,
                reads=[xpb, buf("convw"), buf("ct0")], writes=[buf("ct1")])
            dve("scalar_tensor_tensor", ct[0], xp[:, 2:514], cw[:, 2:3], ct[1], ALU.mult, ALU.add,
                reads=[xpb, buf("convw"), buf("ct1")], writes=[buf("ct0")])
            dve("scalar_tensor_tensor", ct[1], xp[:, 3:515], cw[:, 3:4], ct[0], ALU.mult, ALU.add,
                reads=[xpb, buf("convw"), buf("ct0")], writes=[buf("ct1")])
            act(dst[:, blk * 512:(blk + 1) * 512], ct[1], AF.Silu, [buf("ct1")], [buf(dnm)])
    if T.get("prefetch") is not None:
        T["prefetch"]([buf("hT"), buf("wb")])
    def ph_x(c, h):
        col = h * 32 + c
        i4 = 2 * (c % 2) + h
        cs = slice(c * 128, (c + 1) * 128)
        qb_, kb_ = buf("qbT%d" % h), buf("kbT%d" % h)
        pTk = ps[:, h, 128:192].bitcast(BF16)
        return [
            lambda: mm(ps[:, h, 0:128], kbT[h][:, cs], qbT[h][:, cs], [qb_, kb_], [bank[h]], start=True, stop=True),
            lambda: tp(pTk, kbT[h][:, cs], [kb_], [bank[h]]),
            lambda: dve("scalar_tensor_tensor", SmT[i4], ps[:, h, 0:128], EK[:, col:col + 1], tri_f, ALU.mult, ALU.mult,
                        reads=[bank[h], buf("EK"), buf("cst")], writes=[buf("SmT%d" % i4)]),
            lambda: act(Ktm[i4], pTk, AF.Copy, [bank[h], buf("EK")], [buf("Ktm%d" % i4)], scale=EK[:, col:col + 1]),
        ]

    def ph_y(c, h):
        col = h * 32 + c
        i4 = 2 * (c % 2) + h
        hb_ = 2 + 2 * (c % 2) + h
        ub_ = 6 + h
        cs = slice(c * 128, (c + 1) * 128)
        qb_ = buf("qbT%d" % h)
        ops = [lambda: mm(ps[:, hb_, 0:129], SmT[i4], vbaug[:, c, h, :], [buf("SmT%d" % i4), buf("vbaug")], [bank[hb_]],
                          start=True, stop=(c == 0))]
        if c > 0:
            ops.append(lambda: mm(ps[:, hb_, 0:129], qbT[h][:, cs], Cbf[h], [qb_, buf("Cbf%d" % h)], [bank[hb_]],
                                  start=False, stop=True))
        ops.append(lambda: mm(ps[:, ub_, 0:129], Ktm[i4], vbaug[:, c, h, :], [buf("Ktm%d" % i4), buf("vbaug")], [bank[ub_]],
                              start=True, stop=True))
        if c == 0:
            ops.append(lambda: dve("tensor_copy", G[h], ps[:, ub_, 0:129], reads=[bank[ub_]], writes=[buf("G%d" % h)]))
        else:
            ops.append(lambda: dve("scalar_tensor_tensor", G[h], G[h], DEC[:, col - 1:col], ps[:, ub_, 0:129], ALU.mult, ALU.add,
                                   reads=[buf("G%d" % h), buf("DEC"), bank[ub_]], writes=[buf("G%d" % h)]))
        if c < 31:
            ops.append(lambda: act(Cbf[h], G[h], AF.Copy, [buf("G%d" % h), buf("DEC")], [buf("Cbf%d" % h)],
                                   scale=DEC[:, col:col + 1]))
        return ops

    def ph_z1(c, h):
        col = h * 32 + c
        i4 = 2 * (c % 2) + h
        hb_ = 2 + 2 * (c % 2) + h
        dnb = buf("dn%d" % h)
        d0, d1, d2 = dn[h][:, 0:1], dn[h][:, 1:2], dn[h][:, 2:3]
        k = ri[0] % 4
        ri[0] += 1
        ssb, rsb = buf("ss%d" % k), buf("rs%d" % k)
        return [
            lambda: dve("tensor_tensor", d0, ps[:, hb_, 128:129], EB[:, col:col + 1], ALU.mult,
                        reads=[bank[hb_], buf("EB")], writes=[dnb]),
            lambda: dve("tensor_scalar", d1, d0, -1.0, 1.0, ALU.mult, ALU.max, reads=[dnb], writes=[dnb]),
            lambda: dve("tensor_tensor", d0, d0, d1, ALU.max, reads=[dnb], writes=[dnb]),
            lambda: dve("reciprocal", d1, d0, reads=[dnb], writes=[dnb]),
            lambda: dve("tensor_tensor", d2, d1, EB[:, col:col + 1], ALU.mult, reads=[dnb, buf("EB")], writes=[dnb]),
            lambda: dve("scalar_tensor_tensor", hbt[h], ps[:, hb_, 0:128], d2, sgo[:, c, h * 128:(h + 1) * 128],
                        ALU.mult, ALU.mult, reads=[bank[hb_], dnb, buf("sgo")], writes=[buf("hbt%d" % h)]),
            lambda: dve("scalar_tensor_tensor", junkf3[h], hbt[h], 1.0, hbt[h], ALU.mult, ALU.mult,
                        reads=[buf("hbt%d" % h)], writes=[buf("junkf3_%d" % h), ssb], accum_out=ss[k][:, 0:1]),
            lambda: act(rs[k][:, 0:1], ss[k][:, 0:1], AF.Ln, [ssb], [rsb], scale=1.0 / 128, bias=EPS),
            lambda: act(rs[k][:, 0:1], rs[k][:, 0:1], AF.Exp, [rsb], [rsb], scale=-0.5),
            lambda: dve("scalar_tensor_tensor", ybb[i4], hbt[h], rs[k][:, 0:1], gzb[:, c, h * 128:(h + 1) * 128],
                        ALU.mult, ALU.mult, reads=[buf("hbt%d" % h), rsb, buf("gzb")], writes=[buf("ybb%d" % i4)]),
        ]

    def ph_z2(c, h):
        i4 = 2 * (c % 2) + h
        pTy = ps[:, h, 192:256].bitcast(BF16)
        c8 = c % 8
        ops = [lambda: tp(pTy, ybb[i4], [buf("ybb%d" % i4)], [bank[h]]),
               lambda: act(yTb[h][:, c8 * 128:(c8 + 1) * 128], pTy, AF.Copy, [bank[h]], [buf("yTb%d" % h)])]
        if c8 == 7:
            ops.append(lambda: S.dma("gpsimd", yTo[1][h * 128:(h + 1) * 128, (c - 7) * 128:(c + 1) * 128], yTb[h],
                                     reads=[buf("yTb%d" % h)], writes=[buf("yB")]))
        return ops

    def interleave(fn, c):
        a_, b_ = fn(c, 0), fn(c, 1)
        for j in range(max(len(a_), len(b_))):
            if j < len(a_):
                a_[j]()
            if j < len(b_):
                b_[j]()

    for q in range(32 + 3):
        if q < 32:
            interleave(ph_x, q)
        if 1 <= q <= 32:
            interleave(ph_y, q - 1)
        if 2 <= q <= 33:
            interleave(ph_z1, q - 2)
        if q >= 3:
            interleave(ph_z2, q - 3)
    S.barrier()


_SLOPES = [2.0 ** (-8.0 * (h + 1) / 4) for h in range(4)]
_CST = None


def _p1_inputs(inp, l, g, x_full):
    w_in = inp["w_in"][l]
    sl = slice(g * 256, (g + 1) * 256)
    parts = [w_in[:, 0:512][:, sl], w_in[:, 512:1024][:, sl], w_in[:, 1024:1536][:, sl], w_in[:, 1536:2048][:, sl],
             w_in[:, 2048:2560][:, sl], w_in[:, 2560:3072][:, sl], w_in[:, 3072:3584][:, sl], w_in[:, 3584:4096][:, sl],
             w_in[:, 4096:4608][:, sl], w_in[:, 4608 + 2 * g:4610 + 2 * g], w_in[:, 4612 + 2 * g:4614 + 2 * g]]
    w1 = np.ascontiguousarray(np.concatenate(parts, axis=1), dtype=np.float32)
    assert w1.shape == (DM, W1_COLS)
    v = np.arange(SEQ) % 512
    hi = ((v // 16) * 16).astype(np.float32)
    lo = (v % 16).astype(np.float32)
    one = np.ones(SEQ, np.float32)
    aug = np.zeros((2, 2, 4, SEQ), np.float32)
    abias = np.zeros((128, 16), np.float32)
    for hl in range(2):
        s = _SLOPES[2 * g + hl]
        aug[hl, 0] = np.stack([-s * 8.0 * hi, -s * 8.0 * lo, one, one])
        aug[hl, 1] = np.stack([one, one, s * 8.0 * hi, s * 8.0 * lo])
        for d in range(8):
            abias[:, hl * 8 + d] = -s * 512.0 * d
    cq = inp["conv_qk"][l]
    convw = np.zeros((128, 16), np.float32)
    for grp in range(4):
        ch0 = (0 if grp < 2 else 512) + (2 * g + grp % 2) * 128
        convw[:, grp * 4:(grp + 1) * 4] = cq[:, ch0:ch0 + 128].T
    bi = inp["b_if"][l]
    bif = np.broadcast_to(np.array([bi[2 * g], bi[2 * g + 1], bi[4 + 2 * g], bi[5 + 2 * g]], np.float32), (128, 4))
    return {
        "w1": w1,
        "gpre": np.ascontiguousarray(np.broadcast_to(inp["norm_pre"][l], (128, DM))),
        "cst": _consts_f32(),
        "aug": aug.astype(ml_dtypes.bfloat16),
        "abias": abias,
        "na": np.ascontiguousarray(np.broadcast_to(inp["norm_a"][l][sl], (128, 256))),
        "nb": np.ascontiguousarray(np.broadcast_to(inp["norm_b"][l][sl], (128, 256))),
        "convw": convw,
        "bif": np.ascontiguousarray(bif),
        "lq": np.ascontiguousarray(np.broadcast_to(inp["lambda_qk"][l].reshape(256), (128, 256))),
    }


def _host_maps(inp):
    x = np.asarray(inp["x"], dtype=np.float32)
    maps = []
    for c in range(8):
        b, g = c // 2, c % 2
        m = {"x": np.ascontiguousarray(x[b]), "xown": np.ascontiguousarray(x[b, g * HALF:(g + 1) * HALF]),
             "cst": _consts_f32()}
        selv = np.zeros((128, 2), np.float32)
        selv[:, g] = 1.0
        m["sel"] = selv
        for l in range(2):
            p1 = _p1_inputs(inp, l, g, None)
            for k in ("w1", "gpre", "na", "nb", "convw", "bif", "lq"):
                m["%s_%d" % (k, l)] = p1[k]
            if l == 0:
                m["aug"] = p1["aug"]
                m["abias"] = p1["abias"]
            w_in = inp["w_in"][l]
            m["w2_%d" % l] = np.ascontiguousarray(w_in[:, 4616:6664], dtype=np.float32)
            m["wab_%d" % l] = np.ascontiguousarray(np.concatenate([inp["w_a"][l], inp["w_b"][l]], axis=0), dtype=np.float32)
            m["wout_%d" % l] = np.ascontiguousarray(inp["w_out"][l], dtype=np.float32)
            m["gpost_%d" % l] = np.ascontiguousarray(np.broadcast_to(inp["norm_post"][l], (128, DM)), dtype=np.float32)
        maps.append(m)
    return maps


PAIRS = [[0, 1], [2, 3], [4, 5], [6, 7]]


def build_fused():
    nc = bass.Bass("TRN2", target_bir_lowering=False)

    def ext(name, shape, dt=F32):
        return nc.dram_tensor(name, shape, dt, kind="ExternalInput").ap()

    x = ext("x", [SEQ, DM])
    xown = ext("xown", [HALF, DM])
    cst = ext("cst", [128, 384])
    sel = ext("sel", [128, 2])
    aug = ext("aug", [2, 2, 4, SEQ], BF16)
    abias = ext("abias", [128, 16])
    L = []
    for l in range(2):
        L.append({
            "w1": ext("w1_%d" % l, [DM, W1_COLS]), "gpre": ext("gpre_%d" % l, [128, DM]),
            "na": ext("na_%d" % l, [128, 256]), "nb": ext("nb_%d" % l, [128, 256]),
            "convw": ext("convw_%d" % l, [128, 16]), "bif": ext("bif_%d" % l, [128, 4]),
            "lq": ext("lq_%d" % l, [128, 256]), "w2": ext("w2_%d" % l, [DM, 2048]),
            "wab": ext("wab_%d" % l, [1024, DM]), "wout": ext("wout_%d" % l, [DM, DM]),
            "gpost": ext("gpost_%d" % l, [128, DM]), "aug": aug, "abias": abias, "sel": sel,
        })
    out = nc.dram_tensor("out", [HALF, DM], F32, kind="ExternalOutput").ap()
    ysend = [nc.dram_tensor("ysend%d" % k, [256, SEQ], BF16) for k in range(2)]
    yrecv = [nc.dram_tensor("yrecv%d" % k, [512, SEQ], BF16) for k in range(2)]
    xnew = [nc.dram_tensor("xnew%d" % k, [512, DM], F32) for k in range(4)]
    xg = [nc.dram_tensor("xg%d" % k, [1024, DM], F32) for k in range(4)]
    ctx = Ctx(nc, cst)
    S = ctx.S

    def gather(src, dst, sbuf, dbuf):
        S.op("gpsimd", lambda e: e.collective_compute("AllGather", ALU.bypass, replica_groups=PAIRS,
                                                       ins=[src.ap().opt()], outs=[dst.ap().opt()]),
             [ctx.pb(sbuf)], [ctx.pb(dbuf)], dma=True, cc=True)

    for l in range(2):
        T1 = dict(L[l])
        if l == 0:
            T1["x"] = lambda t: (x[t * 128:(t + 1) * 128, :], "xin")
        else:
            def xsrc1(t):
                hh, tt = t // 16, t % 16
                k, r = tt // 4, tt % 4
                return xg[k].ap()[hh * 512 + r * 128:hh * 512 + (r + 1) * 128, :], "xin%d" % k
            T1["x"] = xsrc1
            for k in range(4):
                T1["xin%d_buf" % k] = "xg%d" % k
        T1.update({"yTo": [ysend[0].ap(), ysend[1].ap()], "yA_buf": "ysend0", "yB_buf": "ysend1",
                   "after_A": lambda: gather(ysend[0], yrecv[0], "ysend0", "yrecv0")})
        def prefetch(dead_bufs, l=l):
            tmp = Arena(ctx.mem.t, ctx.mem.n)
            w2b = tmp.alloc([8, 2048], BF16)
            wabb = tmp.alloc([8, 1024], BF16)
            woutb = tmp.alloc([8, 1024], BF16)
            p2_weight_dmas(S, w2b, wabb, woutb, L[l]["w2"], L[l]["wab"], L[l]["wout"], dead_bufs)
        T1["prefetch"] = prefetch
        emit_p1(ctx, l, T1)
        gather(ysend[1], yrecv[1], "ysend1", "yrecv1")
        T2 = dict(L[l])
        T2.update({"ysrc": [yrecv[0].ap(), yrecv[1].ap()], "ysrcA_buf": "yrecv0", "ysrcB_buf": "yrecv1",
                   "w_prefetched": True})
        if l == 0:
            T2["x"] = lambda t: (xown[t * 128:(t + 1) * 128, :], "xin")
            T2["xo"] = lambda t: (xnew[t // 4].ap()[(t % 4) * 128:(t % 4 + 1) * 128, :], "xo%d" % (t // 4))
            for k in range(4):
                T2["xo%d_buf" % k] = "xnew%d" % k
            T2["after_blk"] = lambda blk: gather(xnew[blk], xg[blk], "xnew%d" % blk, "xg%d" % blk)
        else:
            T2["x"] = lambda t: (xnew[t // 4].ap()[(t % 4) * 128:(t % 4 + 1) * 128, :], "xin%d" % (t // 4))
            for k in range(4):
                T2["xin%d_buf" % k] = "xnew%d" % k
            T2["xo"] = lambda t: (out[t * 128:(t + 1) * 128, :], "xo0")
            T2["xo0_buf"] = "out"
        emit_p2(ctx, T2)
    S.emit()
    return nc


def kernel(**inputs):
    inp = {k: np.asarray(v) for k, v in inputs.items()}
    maps = _host_maps(inp)
    nc = build_fused()
    res = run_bass_kernel_spmd(nc, maps, core_ids=list(range(8))).results
    out = np.empty((4, SEQ, DM), np.float32)
    for c in range(8):
        b, g = c // 2, c % 2
        out[b, g * HALF:(g + 1) * HALF] = res[c]["out"]
    return out
```

```python
import math
import contextlib
import ml_dtypes
import numpy as np
import concourse.bass as bass
import concourse.mybir as mybir
from concourse.bass_utils import run_bass_kernel_spmd

F32 = mybir.dt.float32
BF16 = mybir.dt.bfloat16
AF = mybir.ActivationFunctionType
ALU = mybir.AluOpType
AX = mybir.AxisListType


class Buf:
    __slots__ = ("name", "w", "r", "psum")

    def __init__(self, name="", psum=False):
        self.name = name
        self.w = []
        self.r = []
        self.psum = psum


class Op:
    __slots__ = ("eng", "fn", "deps", "is_dma", "need_inc", "cnt", "slot", "val", "pos", "is_cc")


ENGS = ("sync", "scalar", "vector", "gpsimd", "tensor")
NSLOT = 12


class Sched:
    def __init__(self, nc):
        self.nc = nc
        self.q = {e: [] for e in ENGS}
        self.ndma = {e: 0 for e in ENGS}

    def op(self, eng, fn, reads=(), writes=(), dma=False, cc=False):
        o = Op()
        o.is_cc = cc
        o.eng = eng
        o.fn = fn
        o.is_dma = dma
        o.need_inc = False
        o.cnt = 0
        o.deps = set()
        for b in reads:
            for w in b.w:
                o.deps.add(w)
            if b.psum:
                for r in b.r:
                    if r.eng != eng:
                        o.deps.add(r)
        for b in writes:
            append = dma and (not cc) and len(b.w) > 0 and (not b.r) and all(w.is_dma and not w.is_cc for w in b.w)
            if not append:
                for w in b.w:
                    o.deps.add(w)
            else:
                for w in b.w:
                    o.deps.update(d for d in w.deps)
            for r in b.r:
                o.deps.add(r)
        o.deps.discard(o)
        for b in reads:
            if b not in writes:
                b.r.append(o)
        for b in writes:
            append = dma and (not cc) and len(b.w) > 0 and (not b.r) and all(w.is_dma and not w.is_cc for w in b.w)
            if append:
                b.w.append(o)
            else:
                b.w = [o]
            b.r = []
        if cc:
            assert dma and eng == "gpsimd"
            self.ncc = getattr(self, "ncc", 0) + 1
            o.slot = -1
            o.val = self.ncc
        elif dma:
            n = self.ndma[eng]
            self.ndma[eng] = n + 1
            o.slot = n % NSLOT
            o.val = 16 * (n // NSLOT + 1)
        o.pos = len(self.q[eng])
        self.q[eng].append(o)
        return o

    def dma(self, eng, out, in_, reads=(), writes=(), **kw):
        return self.op(eng, lambda e: e.dma_start(out=out, in_=in_, **kw), reads, writes, dma=True)

    def emit(self):
        nc = self.nc
        for e in ENGS:
            for o in self.q[e]:
                for d in o.deps:
                    if d.is_dma:
                        continue
                    if d.eng == "tensor" and o.eng == "tensor":
                        continue
                    d.need_inc = True
        for e in ENGS:
            c = 0
            for o in self.q[e]:
                if (not o.is_dma) and o.need_inc:
                    c += 1
                    o.cnt = c
        import contextlib
        with contextlib.ExitStack() as st:
            esem = {e: st.enter_context(nc.semaphore("s_" + e)) for e in ENGS}
            dsem = {e: [st.enter_context(nc.semaphore("d_%s_%d" % (e, i))) for i in range(NSLOT)]
                    for e in ENGS if self.ndma[e] > 0}
            ccsem = st.enter_context(nc.semaphore("s_cc"))
            block = st.enter_context(nc.Block())

            def run(ename, eng):
                waited = {}

                def wait(sem, val):
                    k = sem.name
                    if waited.get(k, 0) >= val:
                        return
                    waited[k] = val
                    eng.wait_ge(sem, val)

                for o in self.q[ename]:
                    need = {}
                    for d in o.deps:
                        if d.is_cc:
                            sem, val = ccsem, d.val
                        elif d.is_dma:
                            sem, val = dsem[d.eng][d.slot], d.val
                        else:
                            if d.eng == "tensor" and ename == "tensor":
                                continue
                            sem, val = esem[d.eng], d.cnt
                        if sem.name not in need or need[sem.name][1] < val:
                            need[sem.name] = (sem, val)
                    for nm in sorted(need):
                        wait(*need[nm])
                    if o.is_cc:
                        o.fn(eng).then_inc(ccsem, 1)
                    elif o.is_dma:
                        if o.val > 16:
                            wait(dsem[ename][o.slot], o.val - 16)
                        o.fn(eng).then_inc(dsem[ename][o.slot], 16)
                    else:
                        ins = o.fn(eng)
                        if o.need_inc:
                            ins.then_inc(esem[ename], 1)
                n = self.ndma[ename]
                for s in range(min(n, NSLOT)):
                    last = ((n - 1 - s) // NSLOT) * NSLOT + s
                    wait(dsem[ename][s], 16 * (last // NSLOT + 1))

            @block.sync
            def _(eng):
                run("sync", eng)

            @block.scalar
            def _(eng):
                run("scalar", eng)

            @block.vector
            def _(eng):
                run("vector", eng)

            @block.gpsimd
            def _(eng):
                run("gpsimd", eng)

            @block.tensor
            def _(eng):
                run("tensor", eng)

    def barrier(self):
        last = []
        for e in ENGS:
            seen = set()
            got_c = False
            got_cc = False
            for o in reversed(self.q[e]):
                if o.is_cc:
                    continue
                elif o.is_dma:
                    if o.slot not in seen:
                        seen.add(o.slot)
                        last.append(o)
                elif not got_c:
                    last.append(o)
                    got_c = True
                if got_c and len(seen) >= NSLOT:
                    break
        for e in ENGS:
            o = self.op(e, lambda eng: eng.nop(), (), ())
            o.deps = set(last)


EPS = 1e-6
SEQ = 4096
DM = 1024
HALF = 2048


def _consts_f32():
    ident = np.eye(128, dtype=np.float32)
    tri = np.triu(np.ones((128, 128), dtype=np.float32))
    ones = np.ones((128, 128), dtype=np.float32)
    return np.concatenate([ident, tri, ones], axis=1)


def p2_weight_dmas(S, w2b, wabb, woutb, w2, wab, wout, wbufs):
    for kc in range(8):
        for c0 in range(0, 2048, 1024):
            S.dma("gpsimd", w2b[:, kc, c0:c0 + 1024], w2[kc * 128:(kc + 1) * 128, c0:c0 + 1024], writes=wbufs)
    for kc in range(8):
        S.dma("gpsimd", wabb[:, kc, :], wab[kc * 128:(kc + 1) * 128, :], writes=wbufs)
    for kc in range(8):
        S.dma("gpsimd", woutb[:, kc, :], wout[kc * 128:(kc + 1) * 128, :], writes=wbufs)


def emit_p2(ctx, T):
    nc, S, ps, bank, mem = ctx.nc, ctx.S, ctx.ps, ctx.bank, ctx.mem
    x, ysrc, sel, w2, wab, wout, gpre, gpost, xo = (T[k] for k in
                                                     ("x", "ysrc", "sel", "w2", "wab", "wout", "gpre", "gpost", "xo"))
    cst_t, idb = ctx.cst_t, ctx.idb
    mem.reset()
    w2b = mem.alloc([8, 2048], BF16)
    wabb = mem.alloc([8, 1024], BF16)
    woutb = mem.alloc([8, 1024], BF16)
    gpre_t = mem.alloc([DM], F32)
    gpost_t = mem.alloc([DM], F32)
    sel_t = mem.alloc([8], F32)
    xt = [[mem.alloc([DM], F32) for t in range(4)] for i in range(2)]
    junk = mem.alloc([DM], BF16)
    hb = [mem.alloc([DM], BF16) for i in range(2)]
    hT = [mem.alloc([8, 512], BF16) for i in range(2)]
    yt = [mem.alloc([8, 512], BF16) for i in range(2)]
    yh = [mem.alloc([8, 512], BF16) for i in range(2)]
    sg = mem.alloc([16, 512], BF16)
    t1 = [mem.alloc([512], F32) for i in range(2)]
    t2 = [mem.alloc([512], F32) for i in range(2)]
    mT = mem.alloc([8, 512], BF16)
    xn = [mem.alloc([DM], F32) for i in range(2)]
    ss = [mem.alloc([8], F32)[:, 0:1] for i in range(4)]
    rs = [mem.alloc([8], F32)[:, 0:1] for i in range(4)]
    B = {}

    def buf(name):
        if name in PERSIST:
            return ctx.pb(T.get(name + "_buf", name))
        if name not in B:
            B[name] = Buf(name)
        return B[name]

    S.dma("sync", gpre_t[:], gpre, writes=[buf("gpre")])
    S.dma("sync", gpost_t[:], gpost, writes=[buf("gpost")])
    S.dma("sync", sel_t[:, 0:2], sel, writes=[buf("sel")])

    if not T.get("w_prefetched"):
        p2_weight_dmas(S, w2b, wabb, woutb, w2, wab, wout, [buf("w2b"), buf("wabb"), buf("woutb")])

    ri = [0]

    def prologue(blk):
        xs = xt[blk % 2]
        hTb = hT[blk % 2]
        for t in range(4):
            xb = buf("xt%d_%d" % (blk % 2, t))
            xsrc_ap, xsrc_buf = x(blk * 4 + t)
            S.dma("sync", xs[t][:], xsrc_ap, reads=[buf(xsrc_buf)], writes=[xb])
            k = ri[0] % 4
            ri[0] += 1
            S.op("scalar", lambda e, t=t, k=k, xs=xs: e.activation(junk[:], xs[t][:], AF.Square, accum_out=ss[k]),
                 [xb], [buf("junk"), buf("ss%d" % k)])
            S.op("scalar", lambda e, k=k: e.activation(rs[k], ss[k], AF.Sqrt, scale=1.0 / DM, bias=EPS),
                 [buf("ss%d" % k)], [buf("rs%d" % k)])
            S.op("vector", lambda e, k=k: e.reciprocal(rs[k], rs[k]), [buf("rs%d" % k)], [buf("rs%d" % k)])
            hbi = t % 2
            S.op("vector", lambda e, t=t, k=k, xs=xs, hbi=hbi: e.scalar_tensor_tensor(
                hb[hbi][:], xs[t][:], rs[k], gpre_t[:], ALU.mult, ALU.mult),
                [xb, buf("rs%d" % k), buf("gpre")], [buf("hb%d" % hbi)])
            pb = 6 + (t % 2)
            pT = ps[:, pb, 0:512].bitcast(BF16)
            for c in range(8):
                S.op("tensor", lambda e, c=c, hbi=hbi, pT=pT: e.transpose(
                    pT[:, c * 128:(c + 1) * 128], hb[hbi][:, c * 128:(c + 1) * 128], idb[:]),
                    [buf("hb%d" % hbi), buf("idb")], [bank[pb]])
            S.op("scalar", lambda e, t=t, pT=pT, hTb=hTb: e.activation(
                hTb[:, :, t * 128:(t + 1) * 128], pT.rearrange("p (c t) -> p c t", c=8), AF.Copy),
                [bank[pb]], [buf("hT%d" % (blk % 2))])
        yb = buf("yt%d" % (blk % 2))
        ytb = yt[blk % 2]
        for hh in range(2):
            for ab in range(2):
                for r in range(2):
                    src = ysrc[ab][r * 256:(r + 1) * 256, hh * HALF + blk * 512:hh * HALF + (blk + 1) * 512].rearrange(
                        "(i p) t -> p i t", p=128)
                    c0_ = ab * 4 + r * 2
                    S.dma("sync", yh[hh][:, c0_:c0_ + 2, :], src, reads=[buf("ysrcA" if ab == 0 else "ysrcB")],
                          writes=[buf("yh%d" % hh)])
        S.op("vector", lambda e, ytb=ytb: e.tensor_scalar(ytb[:], yh[0][:], sel_t[:, 0:1], None, ALU.mult),
             [buf("yh0"), buf("sel")], [yb])
        S.op("vector", lambda e, ytb=ytb: e.scalar_tensor_tensor(ytb[:], yh[1][:], sel_t[:, 1:2], ytb[:], ALU.mult, ALU.add),
             [buf("yh1"), buf("sel"), yb], [yb])

    def gates(blk):
        hTb = hT[blk % 2]
        for fg in range(16):
            pb = fg % 2
            for kc in range(8):
                S.op("tensor", lambda e, fg=fg, kc=kc, pb=pb, hTb=hTb: e.matmul(
                    ps[:, pb, :], w2b[:, kc, fg * 128:(fg + 1) * 128], hTb[:, kc, :], start=(kc == 0), stop=(kc == 7)),
                    [buf("w2b"), buf("hT%d" % (blk % 2))], [bank[pb]])
            S.op("scalar", lambda e, fg=fg, pb=pb: e.activation(sg[:, fg, :], ps[:, pb, :], AF.Sigmoid),
                 [bank[pb]], [buf("sg")])

    def merge_out(blk):
        xs = xt[blk % 2]
        yb = buf("yt%d" % (blk % 2))
        ytb = yt[blk % 2]
        for dg in range(8):
            mb = 2 if dg % 2 == 0 else 0
            for ab in range(2):
                pb = mb + ab
                for kc in range(4):
                    S.op("tensor", lambda e, dg=dg, ab=ab, kc=kc, pb=pb, ytb=ytb: e.matmul(
                        ps[:, pb, :], wabb[:, ab * 4 + kc, dg * 128:(dg + 1) * 128], ytb[:, ab * 4 + kc, :],
                        start=(kc == 0), stop=(kc == 3)),
                        [buf("wabb"), yb], [bank[pb]])
            i = dg % 2
            S.op("vector", lambda e, dg=dg, i=i, mb=mb: e.tensor_tensor(t1[i][:], ps[:, mb, :], sg[:, dg, :], ALU.mult),
                 [bank[mb], buf("sg")], [buf("t1_%d" % i)])
            S.op("vector", lambda e, dg=dg, i=i, mb=mb: e.tensor_tensor(t2[i][:], ps[:, mb + 1, :], sg[:, 8 + dg, :], ALU.mult),
                 [bank[mb + 1], buf("sg")], [buf("t2_%d" % i)])
            S.op("gpsimd", lambda e, dg=dg, i=i: e.tensor_tensor(mT[:, dg, :], t1[i][:], t2[i][:], ALU.add),
                 [buf("t1_%d" % i), buf("t2_%d" % i)], [buf("mT")])
        for t in range(4):
            xb = buf("xt%d_%d" % (blk % 2, t))
            ob = 4 if t % 2 == 0 else 2
            for nb in range(2):
                for kc in range(8):
                    S.op("tensor", lambda e, t=t, nb=nb, kc=kc, ob=ob: e.matmul(
                        ps[:, ob + nb, :], mT[:, kc, t * 128:(t + 1) * 128], woutb[:, kc, nb * 512:(nb + 1) * 512],
                        start=(kc == 0), stop=(kc == 7)),
                        [buf("mT"), buf("woutb")], [bank[ob + nb]])
            k = ri[0] % 4
            ri[0] += 1
            po = ps[:, ob:ob + 2, :].rearrange("p a b -> p (a b)")
            S.op("scalar", lambda e, k=k, po=po: e.activation(junk[:], po, AF.Square, accum_out=ss[k]),
                 [bank[ob], bank[ob + 1]], [buf("junk"), buf("ss%d" % k)])
            S.op("scalar", lambda e, k=k: e.activation(rs[k], ss[k], AF.Sqrt, scale=1.0 / DM, bias=EPS),
                 [buf("ss%d" % k)], [buf("rs%d" % k)])
            S.op("vector", lambda e, k=k: e.reciprocal(rs[k], rs[k]), [buf("rs%d" % k)], [buf("rs%d" % k)])
            xi = t % 2
            S.op("vector", lambda e, k=k, xi=xi, po=po: e.scalar_tensor_tensor(
                xn[xi][:], po, rs[k], gpost_t[:], ALU.mult, ALU.mult),
                [bank[ob], bank[ob + 1], buf("rs%d" % k), buf("gpost")], [buf("xn%d" % xi)])
            S.op("gpsimd", lambda e, xi=xi, t=t, xs=xs: e.tensor_tensor(xn[xi][:], xn[xi][:], xs[t][:], ALU.add),
                 [buf("xn%d" % xi), xb], [buf("xn%d" % xi)])
            xdst_ap, xdst_buf = xo(blk * 4 + t)
            S.dma("gpsimd", xdst_ap, xn[xi][:], reads=[buf("xn%d" % xi)], writes=[buf(xdst_buf)])
        if T.get("after_blk") is not None:
            T["after_blk"](blk)

    prologue(0)
    for blk in range(4):
        gates(blk)
        if blk + 1 < 4:
            prologue(blk + 1)
        merge_out(blk)
    S.barrier()


class Arena:
    def __init__(self, t, nbytes):
        self.t = t
        self.n = nbytes
        self.off = 0

    def reset(self):
        self.off = 0

    def mark(self):
        return self.off

    def release(self, m):
        self.off = m

    def alloc(self, shape, dtype):
        esz = 4 if dtype == F32 else 2
        n = int(np.prod(shape)) * esz
        self.off = (self.off + 31) // 32 * 32
        a = self.off
        assert a + n <= self.n, ("arena overflow", a, n, self.n)
        self.off = a + n
        v = self.t[:, a // 2:(a + n) // 2]
        if dtype == F32:
            v = v.bitcast(F32)
        if len(shape) == 2:
            v = v.rearrange("p (a b) -> p a b", a=shape[0])
        elif len(shape) == 3:
            v = v.rearrange("p (a b c) -> p a b c", a=shape[0], b=shape[1])
        return v


SBUF_ARENA_BYTES = 210432


class Ctx:
    def __init__(self, nc, cst):
        self.nc = nc
        self.S = Sched(nc)
        sb = nc.alloc_sbuf_tensor
        self.cst_t = sb("cst_t", [128, 384], F32)
        self.idb = sb("idb", [128, 128], BF16)
        self.trib = sb("trib", [128, 128], BF16)
        self.mem = Arena(sb("arena", [128, SBUF_ARENA_BYTES // 2], BF16), SBUF_ARENA_BYTES)
        self.ps = nc.alloc_psum_tensor("ps", [128, 8, 512], F32)
        self.bank = [Buf("bank%d" % i, psum=True) for i in range(8)]
        self.pbuf = {}
        S = self.S
        S.dma("sync", self.cst_t[:], cst, writes=[self.pb("cst")])
        S.op("vector", lambda e: e.tensor_copy(self.idb[:], self.cst_t[:, 0:128]), [self.pb("cst")], [self.pb("idb")])
        S.op("vector", lambda e: e.tensor_copy(self.trib[:], self.cst_t[:, 128:256]), [self.pb("cst")], [self.pb("trib")])

    def pb(self, name):
        if name not in self.pbuf:
            self.pbuf[name] = Buf(name)
        return self.pbuf[name]


PERSIST = ("cst", "idb", "trib", "yA", "yB", "ysrcA", "ysrcB", "xin", "xo0", "xo1", "xo2", "xo3", "xin0", "xin1", "xin2", "xin3")


W1_COLS = 2308
ARENA_BYTES = 102 * 1024


def emit_p1(ctx, layer, T):
    lam_init = 0.8 - 0.6 * math.exp(-0.3 * layer)
    nc, S, ps, bank, mem = ctx.nc, ctx.S, ctx.ps, ctx.bank, ctx.mem
    x, w1, gpre, aug, abias, na, nb, convw, bif, lq, yTo = (T[k] for k in
                                                             ("x", "w1", "gpre", "aug", "abias", "na", "nb", "convw", "bif", "lq", "yTo"))
    cst_t, idb, trib = ctx.cst_t, ctx.idb, ctx.trib
    mem.reset()
    hT = mem.alloc([8, SEQ], BF16)
    wb = mem.alloc([8, 1284], BF16)
    gpre_t = mem.alloc([DM], F32)
    abias_t = mem.alloc([16], F32)
    na_t = mem.alloc([256], F32)
    nb_t = mem.alloc([256], F32)
    convw_t = mem.alloc([16], F32)
    bif_t = mem.alloc([8], F32)
    lq_t = mem.alloc([256], F32)
    sm = mem.alloc([64], F32)
    ss = [mem.alloc([8], F32) for i in range(4)]
    rs = [mem.alloc([8], F32) for i in range(4)]
    ar = mem
    stage_mark = mem.mark()
    B = {}
    tri_f = cst_t[:, 128:256]
    ones_f = cst_t[:, 256:384]
    xin_name = T.get("xin_name", "xin")

    def buf(name):
        if name in PERSIST:
            return ctx.pb(T.get(name + "_buf", name))
        if name not in B:
            B[name] = Buf(name)
        return B[name]

    def act(out, in_, func, reads, writes, **kw):
        return S.op("scalar", lambda e: e.activation(out, in_, func, **kw), reads, writes)

    def dve(method, *args, reads=(), writes=(), **kw):
        return S.op("vector", lambda e: getattr(e, method)(*args, **kw), reads, writes)

    def pool(method, *args, reads=(), writes=(), **kw):
        return S.op("gpsimd", lambda e: getattr(e, method)(*args, **kw), reads, writes)

    def mm(out, lhsT, rhs, reads, writes, **kw):
        return S.op("tensor", lambda e: e.matmul(out, lhsT, rhs, **kw), reads, writes)

    def tp(out, in_, reads, writes):
        return S.op("tensor", lambda e: e.transpose(out, in_, idb[:]), list(reads) + [buf("idb")], writes)

    for name, t, src in (("gpre", gpre_t, gpre), ("abias", abias_t, abias), ("na", na_t, na),
                         ("nb", nb_t, nb), ("convw", convw_t, convw), ("lq", lq_t, lq)):
        S.dma("sync", t[:], src, writes=[buf(name)])
    S.dma("sync", bif_t[:, 0:4], bif, writes=[buf("bif")])

    si = [0]

    def load_w(c0, ncols):
        for kc in range(8):
            S.dma("gpsimd", wb[:, kc, 0:ncols], w1[kc * 128:(kc + 1) * 128, c0:c0 + ncols], writes=[buf("wb")])

    load_w(0, 1024)

    ri = [0]

    def rstd_from(ssk, k, n):
        act(rs[k][:, 0:1], ssk, AF.Sqrt, [buf("ss%d" % k)], [buf("rs%d" % k)], scale=1.0 / n, bias=EPS)
        dve("reciprocal", rs[k][:, 0:1], rs[k][:, 0:1], reads=[buf("rs%d" % k)], writes=[buf("rs%d" % k)])

    mem.release(stage_mark)
    xt = [ar.alloc([DM], F32) for _ in range(6)]
    junk = ar.alloc([DM], BF16)
    hb = [ar.alloc([DM], BF16) for _ in range(2)]
    kk = {}

    pos = {}

    def st0_l(t):
        pos[t] = len(pos)
        i = pos[t] % 6
        xb = buf("xt%d" % i)
        xsrc_ap, xsrc_buf = x(t)
        S.dma("sync", xt[i], xsrc_ap, reads=[buf(xsrc_buf)], writes=[xb])
        k = ri[0] % 4
        ri[0] += 1
        kk[t] = k
        act(junk, xt[i], AF.Square, [xb], [buf("junk"), buf("ss%d" % k)], accum_out=ss[k][:, 0:1])
        rstd_from(ss[k][:, 0:1], k, DM)

    def st0_m(t):
        i, j, k = pos[t] % 6, pos[t] % 2, kk[t]
        dve("scalar_tensor_tensor", hb[j], xt[i], rs[k][:, 0:1], gpre_t[:], ALU.mult, ALU.mult,
            reads=[buf("xt%d" % i), buf("rs%d" % k), buf("gpre")], writes=[buf("hb%d" % j)])

    def st0_n(t):
        j = pos[t] % 2
        pb = 6 + j
        pT = ps[:, pb, 0:512].bitcast(BF16)
        for c in range(8):
            tp(pT[:, c * 128:(c + 1) * 128], hb[j][:, c * 128:(c + 1) * 128], [buf("hb%d" % j)], [bank[pb]])
        if pos[t] % 2 == 0:
            act(hT[:, :, t * 128:(t + 1) * 128], pT.rearrange("p (c t) -> p c t", c=8), AF.Copy, [bank[pb]], [buf("hT")])
        else:
            dve("tensor_copy", hT[:, :, t * 128:(t + 1) * 128], pT.rearrange("p (c t) -> p c t", c=8),
                reads=[bank[pb]], writes=[buf("hT")])

    order = list(range(0, 12)) + list(range(16, 28)) + list(range(12, 16)) + list(range(28, 32))
    for m in range(32 + 2):
        if m < 32:
            st0_l(order[m])
        if 1 <= m <= 32:
            st0_m(order[m - 1])
        if m >= 2:
            st0_n(order[m - 2])
    S.barrier()

    mem.release(stage_mark)
    QT = [ar.alloc([SEQ], BF16) for _ in range(2)]
    KT = [ar.alloc([SEQ], BF16) for _ in range(2)]
    Vaug = ar.alloc([32, 2, 129], BF16)
    gz = ar.alloc([32, 256], BF16)
    PT = [ar.alloc([2, 512], BF16) for _ in range(3)]
    yTs = ar.alloc([SEQ], BF16)
    otmp = [ar.alloc([128], F32) for _ in range(4)]
    ot = [ar.alloc([128], F32) for _ in range(4)]
    Osb = [ar.alloc([258], F32) for _ in range(2)]
    ssn = [ar.alloc([8], F32) for _ in range(4)]
    rsn = [ar.alloc([8], F32) for _ in range(4)]
    yab = [ar.alloc([128], BF16) for _ in range(4)]
    zt = [ar.alloc([256], F32) for _ in range(2)]
    junkf = ar.alloc([128], F32)
    for i in range(2):
        pool("memset", QT[i], 0.0, writes=[buf("QT%d" % i)])
        pool("memset", KT[i], 0.0, writes=[buf("KT%d" % i)])
    pool("memset", Vaug, 1.0, writes=[buf("Vaug")])
    lp = zt[0]
    dve("tensor_tensor", lp[:, 0:64], lq_t[:, 0:64], lq_t[:, 64:128], ALU.mult, reads=[buf("lq")], writes=[buf("zt0")])
    dve("tensor_tensor", lp[:, 64:128], lq_t[:, 128:192], lq_t[:, 192:256], ALU.mult, reads=[buf("lq")], writes=[buf("zt0")])
    dve("reduce_sum", sm[:, 1:2], lp[:, 0:64], AX.X, reads=[buf("zt0")], writes=[buf("sm1")])
    dve("reduce_sum", sm[:, 2:3], lp[:, 64:128], AX.X, reads=[buf("zt0")], writes=[buf("sm2")])
    act(sm[:, 1:2], sm[:, 1:2], AF.Exp, [buf("sm1")], [buf("sm1")])
    act(sm[:, 2:3], sm[:, 2:3], AF.Exp, [buf("sm2")], [buf("sm2")])
    dve("tensor_tensor", sm[:, 0:1], sm[:, 2:3], sm[:, 1:2], ALU.subtract, reads=[buf("sm1"), buf("sm2")], writes=[buf("neglam")])
    dve("tensor_scalar", sm[:, 0:1], sm[:, 0:1], -lam_init, None, ALU.add, reads=[buf("neglam")], writes=[buf("neglam")])
    neglam = sm[:, 0:1]

    pr = [0]

    def next_bank():
        b = pr[0] % 6
        pr[0] += 1
        return b

    for t in range(32):
        pb = next_bank()
        for kc in range(8):
            mm(ps[:, pb, :], hT[:, kc, t * 128:(t + 1) * 128], wb[:, kc, 512:1024], [buf("hT"), buf("wb")], [bank[pb]],
               start=(kc == 0), stop=(kc == 7))
        dve("tensor_copy", Vaug[:, t, :, 0:128], ps[:, pb, 0:256].rearrange("p (h e) -> p h e", h=2),
            reads=[bank[pb]], writes=[buf("Vaug")])
        zi = t % 2
        act(zt[zi], ps[:, pb, 256:512], AF.Silu, [bank[pb]], [buf("zt%d" % zi)])
        dve("scalar_tensor_tensor", gz[:, t, :], zt[zi], 1.0 - lam_init, na_t[:], ALU.mult, ALU.mult,
            reads=[buf("zt%d" % zi), buf("na")], writes=[buf("gz")])

    gcnt = [0]
    for hl in range(2):
        S.dma("sync", QT[0][64:68, :], aug[hl, 0], writes=[buf("QT0")])
        S.dma("sync", QT[1][0:4, :], aug[hl, 0], writes=[buf("QT1")])
        S.dma("sync", KT[0][64:68, :], aug[hl, 1], writes=[buf("KT0")])
        S.dma("sync", KT[1][0:4, :], aug[hl, 1], writes=[buf("KT1")])
        for blk in range(8):
            for qk in range(2):
                c0 = qk * 256 + hl * 128
                dst = QT if qk == 0 else KT
                nm = "QT" if qk == 0 else "KT"
                pb = next_bank()
                for kc in range(8):
                    mm(ps[:, pb, :], wb[:, kc, c0:c0 + 128], hT[:, kc, blk * 512:(blk + 1) * 512],
                       [buf("wb"), buf("hT")], [bank[pb]], start=(kc == 0), stop=(kc == 7))
                act(dst[0][0:64, blk * 512:(blk + 1) * 512], ps[0:64, pb, :], AF.Copy, [bank[pb]], [buf(nm + "0")])
                dve("tensor_copy", dst[1][64:128, blk * 512:(blk + 1) * 512], ps[64:128, pb, :],
                    reads=[bank[pb]], writes=[buf(nm + "1")])
        if hl == 1:
            load_w(1024, 1284)
        jobs = []
        for I in range(8):
            for c in range(2):
                groups = []
                for J in range(I):
                    for hf in range(2):
                        groups.append((False, I - J, [4 * J + 2 * hf, 4 * J + 2 * hf + 1]))
                for hf in range(2):
                    groups.append((True, 0, [4 * I + 2 * hf, 4 * I + 2 * hf + 1]))
                for gx, (diag, d, tiles) in enumerate(groups):
                    jobs.append((I, c, diag, d, tiles, gx == 0, gx == len(groups) - 1))
        NJ = len(jobs)
        pending_tp = []

        def rec_st(n):
            I, c, diag, d, tiles, first, last = jobs[n]
            gi = n % 3
            b0 = 2 * gi
            for jj, j in enumerate(tiles):
                n0 = 128 * (j - 4 * I) if diag else 0
                mm(ps[:, b0 + jj, n0:512], KT[c][:, j * 128:(j + 1) * 128], QT[c][:, I * 512 + n0:(I + 1) * 512],
                   [buf("KT%d" % c), buf("QT%d" % c)], [bank[b0 + jj]], start=True, stop=True)

        def rec_exp(n):
            I, c, diag, d, tiles, first, last = jobs[n]
            gi = n % 3
            b0 = 2 * gi
            ptbs = [buf("PT%d_%d" % (gi, jj)) for jj in range(2)]
            bcol = abias_t[:, hl * 8 + d:hl * 8 + d + 1]
            if not diag:
                act(PT[gi], ps[:, b0:b0 + 2, :], AF.Exp, [bank[b0], bank[b0 + 1], buf("abias")], ptbs,
                    scale=0.125, bias=bcol)
            else:
                for jj, j in enumerate(tiles):
                    n0 = 128 * (j - 4 * I)
                    act(PT[gi][:, jj, n0:512], ps[:, b0 + jj, n0:512], AF.Exp, [bank[b0 + jj], buf("abias")], [ptbs[jj]],
                        scale=0.125, bias=bcol)
                    dve("tensor_tensor", PT[gi][:, jj, n0:n0 + 128], PT[gi][:, jj, n0:n0 + 128], trib[:], ALU.mult,
                        reads=[ptbs[jj], buf("trib")], writes=[ptbs[jj]])

        def rec_pv(n, m):
            I, c, diag, d, tiles, first, last = jobs[n]
            gi = n % 3
            b0 = 2 * gi
            for jj, j in enumerate(tiles):
                ptb = buf("PT%d_%d" % (gi, jj))
                r = j - 4 * I
                us = range(r, 4) if diag else range(4)
                for u in us:
                    ob_ = 6 + u // 2
                    oc = (u % 2) * 129
                    mm(ps[:, ob_, oc:oc + 129], PT[gi][:, jj, u * 128:(u + 1) * 128], Vaug[:, j, hl, :],
                       [ptb, buf("Vaug")], [bank[ob_]],
                       start=(first and jj == 0 and u in (0, 2)), stop=(diag and r == u), skip_group_check=True)
            if not last:
                return
            for bb in range(2):
                dve("tensor_copy", Osb[bb], ps[:, 6 + bb, 0:258], reads=[bank[6 + bb]], writes=[buf("Osb%d" % bb)])
            for u in range(4):
                bb = u // 2
                oc = (u % 2) * 129
                O = Osb[bb][:, oc:oc + 128]
                Osum = Osb[bb][:, oc + 128:oc + 129]
                osb = buf("Osb%d" % bb)
                k = ri[0] % 4
                ri[0] += 1
                dve("reciprocal", rs[k][:, 0:1], Osum, reads=[osb], writes=[buf("rs%d" % k)])
                if c == 0:
                    dve("tensor_scalar", otmp[u], O, rs[k][:, 0:1], None, ALU.mult,
                        reads=[osb, buf("rs%d" % k)], writes=[buf("otmp%d" % u)])
                else:
                    dve("tensor_tensor", rs[k][:, 0:1], rs[k][:, 0:1], neglam, ALU.mult,
                        reads=[buf("rs%d" % k), buf("neglam")], writes=[buf("rs%d" % k)])
                    dve("scalar_tensor_tensor", ot[u], O, rs[k][:, 0:1], otmp[u], ALU.mult, ALU.add,
                        reads=[osb, buf("rs%d" % k), buf("otmp%d" % u)], writes=[buf("ot%d" % u)])
            if c == 1:
                for u in range(4):
                    k2 = u
                    dve("scalar_tensor_tensor", junkf, ot[u], 1.0, ot[u], ALU.mult, ALU.mult,
                        reads=[buf("ot%d" % u)], writes=[buf("junkf"), buf("ssn%d" % k2)], accum_out=ssn[k2][:, 0:1])
                for u in range(4):
                    k2 = u
                    act(rsn[k2][:, 0:1], ssn[k2][:, 0:1], AF.Ln, [buf("ssn%d" % k2)], [buf("rsn%d" % k2)], scale=1.0 / 128, bias=EPS)
                    act(rsn[k2][:, 0:1], rsn[k2][:, 0:1], AF.Exp, [buf("rsn%d" % k2)], [buf("rsn%d" % k2)], scale=-0.5)
                for u in range(4):
                    k2 = u
                    tt = 4 * I + u
                    dve("scalar_tensor_tensor", yab[u], ot[u], rsn[k2][:, 0:1], gz[:, tt, hl * 128:(hl + 1) * 128],
                        ALU.mult, ALU.mult, reads=[buf("ot%d" % u), buf("rsn%d" % k2), buf("gz")], writes=[buf("yab%d" % u)])
                pending_tp.append((m + 2, I, b0))

        def flush_tp(m, force=False):
            while pending_tp and (force or pending_tp[0][0] <= m):
                _, I_, b0 = pending_tp.pop(0)
                pT4 = ps[:, b0, 0:256].bitcast(BF16)
                for u in range(4):
                    tp(pT4[:, u * 128:(u + 1) * 128], yab[u], [buf("yab%d" % u)], [bank[b0]])
                act(yTs[:, 4 * I_ * 128:(4 * I_ + 4) * 128], pT4, AF.Copy, [bank[b0]], [buf("yTs")])

        for m in range(NJ + 2):
            if m < NJ:
                rec_st(m)
            if 1 <= m <= NJ:
                rec_exp(m - 1)
            if m >= 2:
                rec_pv(m - 2, m)
            flush_tp(m)
        flush_tp(0, force=True)
        S.dma("gpsimd", yTo[0][hl * 128:(hl + 1) * 128, :], yTs, reads=[buf("yTs")], writes=[buf("yA")])
    if T.get("after_A") is not None:
        T["after_A"]()
    S.barrier()

    mem.release(stage_mark)
    qbT = [ar.alloc([SEQ], BF16) for _ in range(2)]
    kbT = [ar.alloc([SEQ], BF16) for _ in range(2)]
    vbaug = ar.alloc([32, 2, 129], BF16)
    sgo = ar.alloc([32, 256], BF16)
    gzb = ar.alloc([32, 256], BF16)
    yTb = [ar.alloc([1024], BF16) for _ in range(2)]
    xpre = [ar.alloc([515], F32) for _ in range(2)]
    ct = [ar.alloc([512], F32) for _ in range(4)]
    zt = [ar.alloc([256], F32) for _ in range(2)]
    gts = ar.alloc([32, 4], F32)
    IG = ar.alloc([2, 32], F32)
    LF = ar.alloc([2, 32], F32)
    EB = ar.alloc([64], F32)
    DEC = ar.alloc([64], F32)
    EK = ar.alloc([64], F32)
    SmT = [ar.alloc([128], BF16) for _ in range(4)]
    Ktm = [ar.alloc([128], BF16) for _ in range(4)]
    G = [ar.alloc([129], F32) for _ in range(2)]
    Cbf = [ar.alloc([129], BF16) for _ in range(2)]
    hbt = [ar.alloc([128], F32) for _ in range(2)]
    ybb = [ar.alloc([128], BF16) for _ in range(4)]
    dn = [ar.alloc([4], F32) for _ in range(2)]
    junkf3 = [ar.alloc([128], BF16) for _ in range(2)]
    ytmp = [ar.alloc([128], F32) for _ in range(2)]
    pool("memset", vbaug, 1.0, writes=[buf("vbaug")])
    for t in range(32):
        pb = next_bank()
        for kc in range(8):
            mm(ps[:, pb, :], hT[:, kc, t * 128:(t + 1) * 128], wb[:, kc, 512:1024], [buf("hT"), buf("wb")], [bank[pb]],
               start=(kc == 0), stop=(kc == 7))
        dve("tensor_copy", vbaug[:, t, :, 0:128], ps[:, pb, 0:256].rearrange("p (h e) -> p h e", h=2),
            reads=[bank[pb]], writes=[buf("vbaug")])
        act(sgo[:, t, :], ps[:, pb, 256:512], AF.Sigmoid, [bank[pb]], [buf("sgo")])
    for t in range(32):
        pb = next_bank()
        for kc in range(8):
            mm(ps[:, pb, 0:260], hT[:, kc, t * 128:(t + 1) * 128], wb[:, kc, 1024:1284], [buf("hT"), buf("wb")], [bank[pb]],
               start=(kc == 0), stop=(kc == 7))
        zi = t % 2
        act(zt[zi], ps[:, pb, 0:256], AF.Silu, [bank[pb]], [buf("zt%d" % zi)])
        dve("tensor_tensor", gzb[:, t, :], zt[zi], nb_t[:], ALU.mult, reads=[buf("zt%d" % zi), buf("nb")], writes=[buf("gzb")])
        dve("tensor_copy", gts[:, t, :], ps[:, pb, 256:260], reads=[bank[pb]], writes=[buf("gts")])
    for h in range(2):
        dve("tensor_scalar", IG[:, h, :], gts[:, :, h], bif_t[:, h:h + 1], None, ALU.add,
            reads=[buf("gts"), buf("bif")], writes=[buf("IG")])
        dve("tensor_scalar", LF[:, h, :], gts[:, :, 2 + h], bif_t[:, 2 + h:3 + h], None, ALU.add,
            reads=[buf("gts"), buf("bif")], writes=[buf("LF")])
    LFf = LF.rearrange("p a b -> p (a b)")
    IGf = IG.rearrange("p a b -> p (a b)")
    act(LFf, LFf, AF.Exp, [buf("LF")], [buf("LF")], scale=-1.0)
    act(LFf, LFf, AF.Ln, [buf("LF")], [buf("LF")], bias=1.0)
    dve("tensor_scalar", LFf, LFf, -1.0, None, ALU.mult, reads=[buf("LF")], writes=[buf("LF")])
    mm(ps[:, 6, 0:64], tri_f, LFf, [buf("cst"), buf("LF")], [bank[6]], start=True, stop=True)
    mm(ps[:, 7, 0:64], ones_f, LFf, [buf("cst"), buf("LF")], [bank[7]], start=True, stop=True)
    act(EB, ps[:, 6, 0:64], AF.Exp, [bank[6]], [buf("EB")])
    act(DEC, ps[:, 7, 0:64], AF.Exp, [bank[7]], [buf("DEC")])
    dve("tensor_tensor", EK, IGf, ps[:, 6, 0:64], ALU.subtract, reads=[buf("IG"), bank[6]], writes=[buf("EK")])
    act(EK, EK, AF.Exp, [buf("EK")], [buf("EK")], bias=math.log(128.0 ** -0.5))
    blks = [(grp, blk) for grp in range(4) for blk in range(8)]

    def conv_a(n):
        grp, blk = blks[n]
        pb = next_bank()
        for kc in range(8):
            mm(ps[:, pb, :], wb[:, kc, grp * 128:(grp + 1) * 128], hT[:, kc, blk * 512:(blk + 1) * 512],
               [buf("wb"), buf("hT")], [bank[pb]], start=(kc == 0), stop=(kc == 7))
        i = n % 2
        xp = xpre[i]
        xpb = buf("xpre%d" % i)
        if blk == 0:
            pool("memset", xp[:, 0:3], 0.0, writes=[xpb])
        else:
            pool("tensor_copy", xp[:, 0:3], xpre[1 - i][:, 512:515], reads=[buf("xpre%d" % (1 - i))], writes=[xpb])
        act(xp[:, 3:515], ps[:, pb, :], AF.Copy, [bank[pb]], [xpb])

    def conv_b(n):
        grp, blk = blks[n]
        dst = (qbT if grp < 2 else kbT)[grp % 2]
        dnm = ("qbT" if grp < 2 else "kbT") + str(grp % 2)
        i = n % 2
        xp = xpre[i]
        xpb = buf("xpre%d" % i)
        c0_, c1_ = ct[2 * i], ct[2 * i + 1]
        b0_, b1_ = buf("ct%d" % (2 * i)), buf("ct%d" % (2 * i + 1))
        cw = convw_t[:, grp * 4:(grp + 1) * 4]
        dve("tensor_scalar", c0_, xp[:, 0:512], cw[:, 0:1], None, ALU.mult, reads=[xpb, buf("convw")], writes=[b0_])
        dve("scalar_tensor_tensor", c1_, xp[:, 1:513], cw[:, 1:2], c0_, ALU.mult, ALU.add,
            reads=[xpb, buf("convw"), b0_], writes=[b1_])
        dve("scalar_tensor_tensor", c0_, xp[:, 2:514], cw[:, 2:3], c1_, ALU.mult, ALU.add,
            reads=[xpb, buf("convw"), b1_], writes=[b0_])
        dve("scalar_tensor_tensor", c1_, xp[:, 3:515], cw[:, 3:4], c0_, ALU.mult, ALU.add,
            reads=[xpb, buf("convw"), b0_], writes=[b1_])
        act(dst[:, blk * 512:(blk + 1) * 512], c1_, AF.Silu, [b1_], [buf(dnm)])

    for n in range(len(blks) + 1):
        if n < len(blks):
            conv_a(n)
        if n >= 1:
            conv_b(n - 1)
    if T.get("prefetch") is not None:
        T["prefetch"]([buf("hT"), buf("wb")])
    def ph_x(c, h):
        col = h * 32 + c
        i4 = 2 * (c % 2) + h
        cs = slice(c * 128, (c + 1) * 128)
        qb_, kb_ = buf("qbT%d" % h), buf("kbT%d" % h)
        pTk = ps[:, h, 128:192].bitcast(BF16)
        return [
            lambda: mm(ps[:, h, 0:128], kbT[h][:, cs], qbT[h][:, cs], [qb_, kb_], [bank[h]], start=True, stop=True),
            lambda: tp(pTk, kbT[h][:, cs], [kb_], [bank[h]]),
            lambda: dve("scalar_tensor_tensor", SmT[i4], ps[:, h, 0:128], EK[:, col:col + 1], tri_f, ALU.mult, ALU.mult,
                        reads=[bank[h], buf("EK"), buf("cst")], writes=[buf("SmT%d" % i4)]),
            lambda: act(Ktm[i4], pTk, AF.Copy, [bank[h], buf("EK")], [buf("Ktm%d" % i4)], scale=EK[:, col:col + 1]),
        ]

    def ph_y(c, h):
        col = h * 32 + c
        i4 = 2 * (c % 2) + h
        hb_ = 2 + 2 * (c % 2) + h
        ub_ = 6 + h
        cs = slice(c * 128, (c + 1) * 128)
        qb_ = buf("qbT%d" % h)
        ops = [lambda: mm(ps[:, hb_, 0:129], SmT[i4], vbaug[:, c, h, :], [buf("SmT%d" % i4), buf("vbaug")], [bank[hb_]],
                          start=True, stop=(c == 0))]
        if c > 0:
            ops.append(lambda: mm(ps[:, hb_, 0:129], qbT[h][:, cs], Cbf[h], [qb_, buf("Cbf%d" % h)], [bank[hb_]],
                                  start=False, stop=True))
        ops.append(lambda: mm(ps[:, ub_, 0:129], Ktm[i4], vbaug[:, c, h, :], [buf("Ktm%d" % i4), buf("vbaug")], [bank[ub_]],
                              start=True, stop=True))
        if c == 0:
            ops.append(lambda: dve("tensor_copy", G[h], ps[:, ub_, 0:129], reads=[bank[ub_]], writes=[buf("G%d" % h)]))
        else:
            ops.append(lambda: dve("scalar_tensor_tensor", G[h], G[h], DEC[:, col - 1:col], ps[:, ub_, 0:129], ALU.mult, ALU.add,
                                   reads=[buf("G%d" % h), buf("DEC"), bank[ub_]], writes=[buf("G%d" % h)]))
        if c < 31:
            ops.append(lambda: act(Cbf[h], G[h], AF.Copy, [buf("G%d" % h), buf("DEC")], [buf("Cbf%d" % h)],
                                   scale=DEC[:, col:col + 1]))
        return ops

    def ph_z1(c, h):
        col = h * 32 + c
        i4 = 2 * (c % 2) + h
        hb_ = 2 + 2 * (c % 2) + h
        dnb = buf("dn%d" % h)
        d0, d1, d2 = dn[h][:, 0:1], dn[h][:, 1:2], dn[h][:, 2:3]
        k = ri[0] % 4
        ri[0] += 1
        ssb, rsb = buf("ss%d" % k), buf("rs%d" % k)
        return [
            lambda: dve("tensor_tensor", d0, ps[:, hb_, 128:129], EB[:, col:col + 1], ALU.mult,
                        reads=[bank[hb_], buf("EB")], writes=[dnb]),
            lambda: dve("tensor_scalar", d1, d0, -1.0, 1.0, ALU.mult, ALU.max, reads=[dnb], writes=[dnb]),
            lambda: dve("tensor_tensor", d0, d0, d1, ALU.max, reads=[dnb], writes=[dnb]),
            lambda: dve("reciprocal", d1, d0, reads=[dnb], writes=[dnb]),
            lambda: dve("tensor_tensor", d2, d1, EB[:, col:col + 1], ALU.mult, reads=[dnb, buf("EB")], writes=[dnb]),
            lambda: dve("scalar_tensor_tensor", hbt[h], ps[:, hb_, 0:128], d2, sgo[:, c, h * 128:(h + 1) * 128],
                        ALU.mult, ALU.mult, reads=[bank[hb_], dnb, buf("sgo")], writes=[buf("hbt%d" % h)]),
            lambda: act(junkf3[h], hbt[h], AF.Square, [buf("hbt%d" % h)], [buf("junkf3_%d" % h), ssb], accum_out=ss[k][:, 0:1]),
            lambda: act(rs[k][:, 0:1], ss[k][:, 0:1], AF.Ln, [ssb], [rsb], scale=1.0 / 128, bias=EPS),
            lambda: act(rs[k][:, 0:1], rs[k][:, 0:1], AF.Exp, [rsb], [rsb], scale=-0.5),
            lambda: pool("tensor_scalar", ytmp[h], hbt[h], rs[k][:, 0:1], 1.0, ALU.mult, ALU.mult,
                         reads=[buf("hbt%d" % h), rsb], writes=[buf("ytmp%d" % h)]),
            lambda: pool("tensor_tensor", ybb[i4], ytmp[h], gzb[:, c, h * 128:(h + 1) * 128], ALU.mult,
                         reads=[buf("ytmp%d" % h), buf("gzb")], writes=[buf("ybb%d" % i4)]),
        ]

    def ph_z2(c, h):
        i4 = 2 * (c % 2) + h
        pTy = ps[:, h, 192:256].bitcast(BF16)
        c8 = c % 8
        ops = [lambda: tp(pTy, ybb[i4], [buf("ybb%d" % i4)], [bank[h]]),
               lambda: act(yTb[h][:, c8 * 128:(c8 + 1) * 128], pTy, AF.Copy, [bank[h]], [buf("yTb%d" % h)])]
        if c8 == 7:
            ops.append(lambda: S.dma("gpsimd", yTo[1][h * 128:(h + 1) * 128, (c - 7) * 128:(c + 1) * 128], yTb[h],
                                     reads=[buf("yTb%d" % h)], writes=[buf("yB")]))
        return ops

    def interleave(fn, c):
        a_, b_ = fn(c, 0), fn(c, 1)
        for j in range(max(len(a_), len(b_))):
            if j < len(a_):
                a_[j]()
            if j < len(b_):
                b_[j]()

    for q in range(32 + 3):
        if q < 32:
            interleave(ph_x, q)
        if 1 <= q <= 32:
            interleave(ph_y, q - 1)
        if 2 <= q <= 33:
            interleave(ph_z1, q - 2)
        if q >= 3:
            interleave(ph_z2, q - 3)
    S.barrier()


_SLOPES = [2.0 ** (-8.0 * (h + 1) / 4) for h in range(4)]
_CST = None


def _p1_inputs(inp, l, g, x_full):
    w_in = inp["w_in"][l]
    sl = slice(g * 256, (g + 1) * 256)
    parts = [w_in[:, 0:512][:, sl], w_in[:, 512:1024][:, sl], w_in[:, 1024:1536][:, sl], w_in[:, 1536:2048][:, sl],
             w_in[:, 2048:2560][:, sl], w_in[:, 2560:3072][:, sl], w_in[:, 3072:3584][:, sl], w_in[:, 3584:4096][:, sl],
             w_in[:, 4096:4608][:, sl], w_in[:, 4608 + 2 * g:4610 + 2 * g], w_in[:, 4612 + 2 * g:4614 + 2 * g]]
    w1 = np.ascontiguousarray(np.concatenate(parts, axis=1), dtype=np.float32)
    assert w1.shape == (DM, W1_COLS)
    v = np.arange(SEQ) % 512
    hi = ((v // 16) * 16).astype(np.float32)
    lo = (v % 16).astype(np.float32)
    one = np.ones(SEQ, np.float32)
    aug = np.zeros((2, 2, 4, SEQ), np.float32)
    abias = np.zeros((128, 16), np.float32)
    for hl in range(2):
        s = _SLOPES[2 * g + hl]
        aug[hl, 0] = np.stack([-s * 8.0 * hi, -s * 8.0 * lo, one, one])
        aug[hl, 1] = np.stack([one, one, s * 8.0 * hi, s * 8.0 * lo])
        for d in range(8):
            abias[:, hl * 8 + d] = -s * 512.0 * d
    cq = inp["conv_qk"][l]
    convw = np.zeros((128, 16), np.float32)
    for grp in range(4):
        ch0 = (0 if grp < 2 else 512) + (2 * g + grp % 2) * 128
        convw[:, grp * 4:(grp + 1) * 4] = cq[:, ch0:ch0 + 128].T
    bi = inp["b_if"][l]
    bif = np.broadcast_to(np.array([bi[2 * g], bi[2 * g + 1], bi[4 + 2 * g], bi[5 + 2 * g]], np.float32), (128, 4))
    return {
        "w1": w1,
        "gpre": np.ascontiguousarray(np.broadcast_to(inp["norm_pre"][l], (128, DM))),
        "cst": _consts_f32(),
        "aug": aug.astype(ml_dtypes.bfloat16),
        "abias": abias,
        "na": np.ascontiguousarray(np.broadcast_to(inp["norm_a"][l][sl], (128, 256))),
        "nb": np.ascontiguousarray(np.broadcast_to(inp["norm_b"][l][sl], (128, 256))),
        "convw": convw,
        "bif": np.ascontiguousarray(bif),
        "lq": np.ascontiguousarray(np.broadcast_to(inp["lambda_qk"][l].reshape(256), (128, 256))),
    }


def _host_maps(inp):
    x = np.asarray(inp["x"], dtype=np.float32)
    maps = []
    for c in range(8):
        b, g = c // 2, c % 2
        m = {"x": np.ascontiguousarray(x[b]), "xown": np.ascontiguousarray(x[b, g * HALF:(g + 1) * HALF]),
             "cst": _consts_f32()}
        selv = np.zeros((128, 2), np.float32)
        selv[:, g] = 1.0
        m["sel"] = selv
        for l in range(2):
            p1 = _p1_inputs(inp, l, g, None)
            for k in ("w1", "gpre", "na", "nb", "convw", "bif", "lq"):
                m["%s_%d" % (k, l)] = p1[k]
            if l == 0:
                m["aug"] = p1["aug"]
                m["abias"] = p1["abias"]
            w_in = inp["w_in"][l]
            m["w2_%d" % l] = np.ascontiguousarray(w_in[:, 4616:6664], dtype=np.float32)
            m["wab_%d" % l] = np.ascontiguousarray(np.concatenate([inp["w_a"][l], inp["w_b"][l]], axis=0), dtype=np.float32)
            m["wout_%d" % l] = np.ascontiguousarray(inp["w_out"][l], dtype=np.float32)
            m["gpost_%d" % l] = np.ascontiguousarray(np.broadcast_to(inp["norm_post"][l], (128, DM)), dtype=np.float32)
        maps.append(m)
    return maps


PAIRS = [[0, 1], [2, 3], [4, 5], [6, 7]]


def build_fused():
    nc = bass.Bass("TRN2", target_bir_lowering=False)

    def ext(name, shape, dt=F32):
        return nc.dram_tensor(name, shape, dt, kind="ExternalInput").ap()

    x = ext("x", [SEQ, DM])
    xown = ext("xown", [HALF, DM])
    cst = ext("cst", [128, 384])
    sel = ext("sel", [128, 2])
    aug = ext("aug", [2, 2, 4, SEQ], BF16)
    abias = ext("abias", [128, 16])
    L = []
    for l in range(2):
        L.append({
            "w1": ext("w1_%d" % l, [DM, W1_COLS]), "gpre": ext("gpre_%d" % l, [128, DM]),
            "na": ext("na_%d" % l, [128, 256]), "nb": ext("nb_%d" % l, [128, 256]),
            "convw": ext("convw_%d" % l, [128, 16]), "bif": ext("bif_%d" % l, [128, 4]),
            "lq": ext("lq_%d" % l, [128, 256]), "w2": ext("w2_%d" % l, [DM, 2048]),
            "wab": ext("wab_%d" % l, [1024, DM]), "wout": ext("wout_%d" % l, [DM, DM]),
            "gpost": ext("gpost_%d" % l, [128, DM]), "aug": aug, "abias": abias, "sel": sel,
        })
    out = nc.dram_tensor("out", [HALF, DM], F32, kind="ExternalOutput").ap()
    ysend = [nc.dram_tensor("ysend%d" % k, [256, SEQ], BF16) for k in range(2)]
    yrecv = [nc.dram_tensor("yrecv%d" % k, [512, SEQ], BF16) for k in range(2)]
    xnew = [nc.dram_tensor("xnew%d" % k, [512, DM], F32) for k in range(4)]
    xg = [nc.dram_tensor("xg%d" % k, [1024, DM], F32) for k in range(4)]
    ctx = Ctx(nc, cst)
    S = ctx.S

    def gather(src, dst, sbuf, dbuf):
        S.op("gpsimd", lambda e: e.collective_compute("AllGather", ALU.bypass, replica_groups=PAIRS,
                                                       ins=[src.ap().opt()], outs=[dst.ap().opt()]),
             [ctx.pb(sbuf)], [ctx.pb(dbuf)], dma=True, cc=True)

    for l in range(2):
        T1 = dict(L[l])
        if l == 0:
            T1["x"] = lambda t: (x[t * 128:(t + 1) * 128, :], "xin")
        else:
            def xsrc1(t):
                hh, tt = t // 16, t % 16
                k, r = tt // 4, tt % 4
                return xg[k].ap()[hh * 512 + r * 128:hh * 512 + (r + 1) * 128, :], "xin%d" % k
            T1["x"] = xsrc1
            for k in range(4):
                T1["xin%d_buf" % k] = "xg%d" % k
        T1.update({"yTo": [ysend[0].ap(), ysend[1].ap()], "yA_buf": "ysend0", "yB_buf": "ysend1",
                   "after_A": lambda: gather(ysend[0], yrecv[0], "ysend0", "yrecv0")})
        def prefetch(dead_bufs, l=l):
            tmp = Arena(ctx.mem.t, ctx.mem.n)
            w2b = tmp.alloc([8, 2048], BF16)
            wabb = tmp.alloc([8, 1024], BF16)
            woutb = tmp.alloc([8, 1024], BF16)
            p2_weight_dmas(S, w2b, wabb, woutb, L[l]["w2"], L[l]["wab"], L[l]["wout"], dead_bufs)
        T1["prefetch"] = prefetch
        emit_p1(ctx, l, T1)
        gather(ysend[1], yrecv[1], "ysend1", "yrecv1")
        T2 = dict(L[l])
        T2.update({"ysrc": [yrecv[0].ap(), yrecv[1].ap()], "ysrcA_buf": "yrecv0", "ysrcB_buf": "yrecv1",
                   "w_prefetched": True})
        if l == 0:
            T2["x"] = lambda t: (xown[t * 128:(t + 1) * 128, :], "xin")
            T2["xo"] = lambda t: (xnew[t // 4].ap()[(t % 4) * 128:(t % 4 + 1) * 128, :], "xo%d" % (t // 4))
            for k in range(4):
                T2["xo%d_buf" % k] = "xnew%d" % k
            T2["after_blk"] = lambda blk: gather(xnew[blk], xg[blk], "xnew%d" % blk, "xg%d" % blk)
        else:
            T2["x"] = lambda t: (xnew[t // 4].ap()[(t % 4) * 128:(t % 4 + 1) * 128, :], "xin%d" % (t // 4))
            for k in range(4):
                T2["xin%d_buf" % k] = "xnew%d" % k
            T2["xo"] = lambda t: (out[t * 128:(t + 1) * 128, :], "xo0")
            T2["xo0_buf"] = "out"
        emit_p2(ctx, T2)
    S.emit()
    return nc


def kernel(**inputs):
    inp = {k: np.asarray(v) for k, v in inputs.items()}
    maps = _host_maps(inp)
    nc = build_fused()
    res = run_bass_kernel_spmd(nc, maps, core_ids=list(range(8))).results
    out = np.empty((4, SEQ, DM), np.float32)
    for c in range(8):
        b, g = c // 2, c % 2
        out[b, g * HALF:(g + 1) * HALF] = res[c]["out"]
    return out
```

```python
import math
import contextlib
import ml_dtypes
import numpy as np
import concourse.bass as bass
import concourse.mybir as mybir
from concourse.bass_utils import run_bass_kernel_spmd

F32 = mybir.dt.float32
BF16 = mybir.dt.bfloat16
AF = mybir.ActivationFunctionType
ALU = mybir.AluOpType
AX = mybir.AxisListType


class Buf:
    __slots__ = ("name", "w", "r", "psum")

    def __init__(self, name="", psum=False):
        self.name = name
        self.w = []
        self.r = []
        self.psum = psum


class Op:
    __slots__ = ("eng", "fn", "deps", "is_dma", "need_inc", "cnt", "slot", "val", "pos", "is_cc")


ENGS = ("sync", "scalar", "vector", "gpsimd", "tensor")
NSLOT = 12


class Sched:
    def __init__(self, nc):
        self.nc = nc
        self.q = {e: [] for e in ENGS}
        self.ndma = {e: 0 for e in ENGS}

    def op(self, eng, fn, reads=(), writes=(), dma=False, cc=False):
        o = Op()
        o.is_cc = cc
        o.eng = eng
        o.fn = fn
        o.is_dma = dma
        o.need_inc = False
        o.cnt = 0
        o.deps = set()
        for b in reads:
            for w in b.w:
                o.deps.add(w)
            if b.psum:
                for r in b.r:
                    if r.eng != eng:
                        o.deps.add(r)
        for b in writes:
            append = dma and (not cc) and len(b.w) > 0 and (not b.r) and all(w.is_dma and not w.is_cc for w in b.w)
            if not append:
                for w in b.w:
                    o.deps.add(w)
            else:
                for w in b.w:
                    o.deps.update(d for d in w.deps)
            for r in b.r:
                o.deps.add(r)
        o.deps.discard(o)
        for b in reads:
            if b not in writes:
                b.r.append(o)
        for b in writes:
            append = dma and (not cc) and len(b.w) > 0 and (not b.r) and all(w.is_dma and not w.is_cc for w in b.w)
            if append:
                b.w.append(o)
            else:
                b.w = [o]
            b.r = []
        if cc:
            assert dma and eng == "gpsimd"
            self.ncc = getattr(self, "ncc", 0) + 1
            o.slot = -1
            o.val = self.ncc
        elif dma:
            n = self.ndma[eng]
            self.ndma[eng] = n + 1
            o.slot = n % NSLOT
            o.val = 16 * (n // NSLOT + 1)
        o.pos = len(self.q[eng])
        self.q[eng].append(o)
        return o

    def dma(self, eng, out, in_, reads=(), writes=(), **kw):
        return self.op(eng, lambda e: e.dma_start(out=out, in_=in_, **kw), reads, writes, dma=True)

    def emit(self):
        nc = self.nc
        for e in ENGS:
            for o in self.q[e]:
                for d in o.deps:
                    if d.is_dma:
                        continue
                    if d.eng == "tensor" and o.eng == "tensor":
                        continue
                    d.need_inc = True
        for e in ENGS:
            c = 0
            for o in self.q[e]:
                if (not o.is_dma) and o.need_inc:
                    c += 1
                    o.cnt = c
        import contextlib
        with contextlib.ExitStack() as st:
            esem = {e: st.enter_context(nc.semaphore("s_" + e)) for e in ENGS}
            dsem = {e: [st.enter_context(nc.semaphore("d_%s_%d" % (e, i))) for i in range(NSLOT)]
                    for e in ENGS if self.ndma[e] > 0}
            ccsem = st.enter_context(nc.semaphore("s_cc"))
            block = st.enter_context(nc.Block())

            def run(ename, eng):
                waited = {}

                def wait(sem, val):
                    k = sem.name
                    if waited.get(k, 0) >= val:
                        return
                    waited[k] = val
                    eng.wait_ge(sem, val)

                for o in self.q[ename]:
                    need = {}
                    for d in o.deps:
                        if d.is_cc:
                            sem, val = ccsem, d.val
                        elif d.is_dma:
                            sem, val = dsem[d.eng][d.slot], d.val
                        else:
                            if d.eng == "tensor" and ename == "tensor":
                                continue
                            sem, val = esem[d.eng], d.cnt
                        if sem.name not in need or need[sem.name][1] < val:
                            need[sem.name] = (sem, val)
                    for nm in sorted(need):
                        wait(*need[nm])
                    if o.is_cc:
                        o.fn(eng).then_inc(ccsem, 1)
                    elif o.is_dma:
                        if o.val > 16:
                            wait(dsem[ename][o.slot], o.val - 16)
                        o.fn(eng).then_inc(dsem[ename][o.slot], 16)
                    else:
                        ins = o.fn(eng)
                        if o.need_inc:
                            ins.then_inc(esem[ename], 1)
                n = self.ndma[ename]
                for s in range(min(n, NSLOT)):
                    last = ((n - 1 - s) // NSLOT) * NSLOT + s
                    wait(dsem[ename][s], 16 * (last // NSLOT + 1))

            @block.sync
            def _(eng):
                run("sync", eng)

            @block.scalar
            def _(eng):
                run("scalar", eng)

            @block.vector
            def _(eng):
                run("vector", eng)

            @block.gpsimd
            def _(eng):
                run("gpsimd", eng)

            @block.tensor
            def _(eng):
                run("tensor", eng)

    def barrier(self):
        last = []
        for e in ENGS:
            seen = set()
            got_c = False
            got_cc = False
            for o in reversed(self.q[e]):
                if o.is_cc:
                    continue
                elif o.is_dma:
                    if o.slot not in seen:
                        seen.add(o.slot)
                        last.append(o)
                elif not got_c:
                    last.append(o)
                    got_c = True
                if got_c and len(seen) >= NSLOT:
                    break
        for e in ENGS:
            o = self.op(e, lambda eng: eng.nop(), (), ())
            o.deps = set(last)


EPS = 1e-6
SEQ = 4096
DM = 1024
HALF = 2048


def _consts_f32():
    ident = np.eye(128, dtype=np.float32)
    tri = np.triu(np.ones((128, 128), dtype=np.float32))
    ones = np.ones((128, 128), dtype=np.float32)
    return np.concatenate([ident, tri, ones], axis=1)


def p2_weight_dmas(S, w2b, wabb, woutb, w2, wab, wout, wbufs):
    for kc in range(8):
        for c0 in range(0, 2048, 1024):
            S.dma("gpsimd", w2b[:, kc, c0:c0 + 1024], w2[kc * 128:(kc + 1) * 128, c0:c0 + 1024], writes=wbufs)
    for kc in range(8):
        S.dma("gpsimd", wabb[:, kc, :], wab[kc * 128:(kc + 1) * 128, :], writes=wbufs)
    for kc in range(8):
        S.dma("gpsimd", woutb[:, kc, :], wout[kc * 128:(kc + 1) * 128, :], writes=wbufs)


def emit_p2(ctx, T):
    nc, S, ps, bank, mem = ctx.nc, ctx.S, ctx.ps, ctx.bank, ctx.mem
    x, ysrc, sel, w2, wab, wout, gpre, gpost, xo = (T[k] for k in
                                                     ("x", "ysrc", "sel", "w2", "wab", "wout", "gpre", "gpost", "xo"))
    cst_t, idb = ctx.cst_t, ctx.idb
    mem.reset()
    w2b = mem.alloc([8, 2048], BF16)
    wabb = mem.alloc([8, 1024], BF16)
    woutb = mem.alloc([8, 1024], BF16)
    gpre_t = mem.alloc([DM], F32)
    gpost_t = mem.alloc([DM], F32)
    sel_t = mem.alloc([8], F32)
    xt = [[mem.alloc([DM], F32) for t in range(4)] for i in range(2)]
    junk = mem.alloc([DM], BF16)
    hb = [mem.alloc([DM], BF16) for i in range(2)]
    hT = [mem.alloc([8, 512], BF16) for i in range(2)]
    yt = [mem.alloc([8, 512], BF16) for i in range(2)]
    yh = [mem.alloc([8, 512], BF16) for i in range(2)]
    sg = mem.alloc([16, 512], BF16)
    t1 = [mem.alloc([512], F32) for i in range(2)]
    t2 = [mem.alloc([512], F32) for i in range(2)]
    mT = mem.alloc([8, 512], BF16)
    xn = [mem.alloc([DM], F32) for i in range(2)]
    ss = [mem.alloc([8], F32)[:, 0:1] for i in range(4)]
    rs = [mem.alloc([8], F32)[:, 0:1] for i in range(4)]
    B = {}

    def buf(name):
        if name in PERSIST:
            return ctx.pb(T.get(name + "_buf", name))
        if name not in B:
            B[name] = Buf(name)
        return B[name]

    S.dma("sync", gpre_t[:], gpre, writes=[buf("gpre")])
    S.dma("sync", gpost_t[:], gpost, writes=[buf("gpost")])
    S.dma("sync", sel_t[:, 0:2], sel, writes=[buf("sel")])

    if not T.get("w_prefetched"):
        p2_weight_dmas(S, w2b, wabb, woutb, w2, wab, wout, [buf("w2b"), buf("wabb"), buf("woutb")])

    ri = [0]

    def prologue(blk):
        xs = xt[blk % 2]
        hTb = hT[blk % 2]
        for t in range(4):
            xb = buf("xt%d_%d" % (blk % 2, t))
            xsrc_ap, xsrc_buf = x(blk * 4 + t)
            S.dma("sync", xs[t][:], xsrc_ap, reads=[buf(xsrc_buf)], writes=[xb])
            k = ri[0] % 4
            ri[0] += 1
            S.op("scalar", lambda e, t=t, k=k, xs=xs: e.activation(junk[:], xs[t][:], AF.Square, accum_out=ss[k]),
                 [xb], [buf("junk"), buf("ss%d" % k)])
            S.op("scalar", lambda e, k=k: e.activation(rs[k], ss[k], AF.Sqrt, scale=1.0 / DM, bias=EPS),
                 [buf("ss%d" % k)], [buf("rs%d" % k)])
            S.op("vector", lambda e, k=k: e.reciprocal(rs[k], rs[k]), [buf("rs%d" % k)], [buf("rs%d" % k)])
            hbi = t % 2
            S.op("vector", lambda e, t=t, k=k, xs=xs, hbi=hbi: e.scalar_tensor_tensor(
                hb[hbi][:], xs[t][:], rs[k], gpre_t[:], ALU.mult, ALU.mult),
                [xb, buf("rs%d" % k), buf("gpre")], [buf("hb%d" % hbi)])
            pb = 6 + (t % 2)
            pT = ps[:, pb, 0:512].bitcast(BF16)
            for c in range(8):
                S.op("tensor", lambda e, c=c, hbi=hbi, pT=pT: e.transpose(
                    pT[:, c * 128:(c + 1) * 128], hb[hbi][:, c * 128:(c + 1) * 128], idb[:]),
                    [buf("hb%d" % hbi), buf("idb")], [bank[pb]])
            S.op("scalar", lambda e, t=t, pT=pT, hTb=hTb: e.activation(
                hTb[:, :, t * 128:(t + 1) * 128], pT.rearrange("p (c t) -> p c t", c=8), AF.Copy),
                [bank[pb]], [buf("hT%d" % (blk % 2))])
        yb = buf("yt%d" % (blk % 2))
        ytb = yt[blk % 2]
        for hh in range(2):
            for ab in range(2):
                for r in range(2):
                    src = ysrc[ab][r * 256:(r + 1) * 256, hh * HALF + blk * 512:hh * HALF + (blk + 1) * 512].rearrange(
                        "(i p) t -> p i t", p=128)
                    c0_ = ab * 4 + r * 2
                    S.dma("sync", yh[hh][:, c0_:c0_ + 2, :], src, reads=[buf("ysrcA" if ab == 0 else "ysrcB")],
                          writes=[buf("yh%d" % hh)])
        S.op("vector", lambda e, ytb=ytb: e.tensor_scalar(ytb[:], yh[0][:], sel_t[:, 0:1], None, ALU.mult),
             [buf("yh0"), buf("sel")], [yb])
        S.op("vector", lambda e, ytb=ytb: e.scalar_tensor_tensor(ytb[:], yh[1][:], sel_t[:, 1:2], ytb[:], ALU.mult, ALU.add),
             [buf("yh1"), buf("sel"), yb], [yb])

    def gates(blk):
        hTb = hT[blk % 2]
        for fg in range(16):
            pb = fg % 2
            for kc in range(8):
                S.op("tensor", lambda e, fg=fg, kc=kc, pb=pb, hTb=hTb: e.matmul(
                    ps[:, pb, :], w2b[:, kc, fg * 128:(fg + 1) * 128], hTb[:, kc, :], start=(kc == 0), stop=(kc == 7)),
                    [buf("w2b"), buf("hT%d" % (blk % 2))], [bank[pb]])
            S.op("scalar", lambda e, fg=fg, pb=pb: e.activation(sg[:, fg, :], ps[:, pb, :], AF.Sigmoid),
                 [bank[pb]], [buf("sg")])

    def merge_out(blk):
        xs = xt[blk % 2]
        yb = buf("yt%d" % (blk % 2))
        ytb = yt[blk % 2]
        for dg in range(8):
            mb = 2 if dg % 2 == 0 else 0
            for ab in range(2):
                pb = mb + ab
                for kc in range(4):
                    S.op("tensor", lambda e, dg=dg, ab=ab, kc=kc, pb=pb, ytb=ytb: e.matmul(
                        ps[:, pb, :], wabb[:, ab * 4 + kc, dg * 128:(dg + 1) * 128], ytb[:, ab * 4 + kc, :],
                        start=(kc == 0), stop=(kc == 3)),
                        [buf("wabb"), yb], [bank[pb]])
            i = dg % 2
            S.op("vector", lambda e, dg=dg, i=i, mb=mb: e.tensor_tensor(t1[i][:], ps[:, mb, :], sg[:, dg, :], ALU.mult),
                 [bank[mb], buf("sg")], [buf("t1_%d" % i)])
            S.op("vector", lambda e, dg=dg, i=i, mb=mb: e.tensor_tensor(t2[i][:], ps[:, mb + 1, :], sg[:, 8 + dg, :], ALU.mult),
                 [bank[mb + 1], buf("sg")], [buf("t2_%d" % i)])
            S.op("gpsimd", lambda e, dg=dg, i=i: e.tensor_tensor(mT[:, dg, :], t1[i][:], t2[i][:], ALU.add),
                 [buf("t1_%d" % i), buf("t2_%d" % i)], [buf("mT")])
        for t in range(4):
            xb = buf("xt%d_%d" % (blk % 2, t))
            ob = 4 if t % 2 == 0 else 2
            for nb in range(2):
                for kc in range(8):
                    S.op("tensor", lambda e, t=t, nb=nb, kc=kc, ob=ob: e.matmul(
                        ps[:, ob + nb, :], mT[:, kc, t * 128:(t + 1) * 128], woutb[:, kc, nb * 512:(nb + 1) * 512],
                        start=(kc == 0), stop=(kc == 7)),
                        [buf("mT"), buf("woutb")], [bank[ob + nb]])
            k = ri[0] % 4
            ri[0] += 1
            po = ps[:, ob:ob + 2, :].rearrange("p a b -> p (a b)")
            S.op("scalar", lambda e, k=k, po=po: e.activation(junk[:], po, AF.Square, accum_out=ss[k]),
                 [bank[ob], bank[ob + 1]], [buf("junk"), buf("ss%d" % k)])
            S.op("scalar", lambda e, k=k: e.activation(rs[k], ss[k], AF.Sqrt, scale=1.0 / DM, bias=EPS),
                 [buf("ss%d" % k)], [buf("rs%d" % k)])
            S.op("vector", lambda e, k=k: e.reciprocal(rs[k], rs[k]), [buf("rs%d" % k)], [buf("rs%d" % k)])
            xi = t % 2
            S.op("vector", lambda e, k=k, xi=xi, po=po: e.scalar_tensor_tensor(
                xn[xi][:], po, rs[k], gpost_t[:], ALU.mult, ALU.mult),
                [bank[ob], bank[ob + 1], buf("rs%d" % k), buf("gpost")], [buf("xn%d" % xi)])
            S.op("gpsimd", lambda e, xi=xi, t=t, xs=xs: e.tensor_tensor(xn[xi][:], xn[xi][:], xs[t][:], ALU.add),
                 [buf("xn%d" % xi), xb], [buf("xn%d" % xi)])
            xdst_ap, xdst_buf = xo(blk * 4 + t)
            S.dma("gpsimd", xdst_ap, xn[xi][:], reads=[buf("xn%d" % xi)], writes=[buf(xdst_buf)])
        if T.get("after_blk") is not None:
            T["after_blk"](blk)

    prologue(0)
    for blk in range(4):
        gates(blk)
        if blk + 1 < 4:
            prologue(blk + 1)
        merge_out(blk)
    S.barrier()


class Arena:
    def __init__(self, t, nbytes):
        self.t = t
        self.n = nbytes
        self.off = 0

    def reset(self):
        self.off = 0

    def mark(self):
        return self.off

    def release(self, m):
        self.off = m

    def alloc(self, shape, dtype):
        esz = 4 if dtype == F32 else 2
        n = int(np.prod(shape)) * esz
        self.off = (self.off + 31) // 32 * 32
        a = self.off
        assert a + n <= self.n, ("arena overflow", a, n, self.n)
        self.off = a + n
        v = self.t[:, a // 2:(a + n) // 2]
        if dtype == F32:
            v = v.bitcast(F32)
        if len(shape) == 2:
            v = v.rearrange("p (a b) -> p a b", a=shape[0])
        elif len(shape) == 3:
            v = v.rearrange("p (a b c) -> p a b c", a=shape[0], b=shape[1])
        return v


SBUF_ARENA_BYTES = 210432


class Ctx:
    def __init__(self, nc, cst):
        self.nc = nc
        self.S = Sched(nc)
        sb = nc.alloc_sbuf_tensor
        self.cst_t = sb("cst_t", [128, 384], F32)
        self.idb = sb("idb", [128, 128], BF16)
        self.trib = sb("trib", [128, 128], BF16)
        self.mem = Arena(sb("arena", [128, SBUF_ARENA_BYTES // 2], BF16), SBUF_ARENA_BYTES)
        self.ps = nc.alloc_psum_tensor("ps", [128, 8, 512], F32)
        self.bank = [Buf("bank%d" % i, psum=True) for i in range(8)]
        self.pbuf = {}
        S = self.S
        S.dma("sync", self.cst_t[:], cst, writes=[self.pb("cst")])
        S.op("vector", lambda e: e.tensor_copy(self.idb[:], self.cst_t[:, 0:128]), [self.pb("cst")], [self.pb("idb")])
        S.op("vector", lambda e: e.tensor_copy(self.trib[:], self.cst_t[:, 128:256]), [self.pb("cst")], [self.pb("trib")])

    def pb(self, name):
        if name not in self.pbuf:
            self.pbuf[name] = Buf(name)
        return self.pbuf[name]


PERSIST = ("cst", "idb", "trib", "yA", "yB", "ysrcA", "ysrcB", "xin", "xo0", "xo1", "xo2", "xo3", "xin0", "xin1", "xin2", "xin3")


W1_COLS = 2308
ARENA_BYTES = 102 * 1024


def emit_p1(ctx, layer, T):
    lam_init = 0.8 - 0.6 * math.exp(-0.3 * layer)
    nc, S, ps, bank, mem = ctx.nc, ctx.S, ctx.ps, ctx.bank, ctx.mem
    x, w1, gpre, aug, abias, na, nb, convw, bif, lq, yTo = (T[k] for k in
                                                             ("x", "w1", "gpre", "aug", "abias", "na", "nb", "convw", "bif", "lq", "yTo"))
    cst_t, idb, trib = ctx.cst_t, ctx.idb, ctx.trib
    mem.reset()
    hT = mem.alloc([8, SEQ], BF16)
    wb = mem.alloc([8, 1284], BF16)
    gpre_t = mem.alloc([DM], F32)
    abias_t = mem.alloc([16], F32)
    na_t = mem.alloc([256], F32)
    nb_t = mem.alloc([256], F32)
    convw_t = mem.alloc([16], F32)
    bif_t = mem.alloc([8], F32)
    lq_t = mem.alloc([256], F32)
    sm = mem.alloc([64], F32)
    ss = [mem.alloc([8], F32) for i in range(4)]
    rs = [mem.alloc([8], F32) for i in range(4)]
    ar = mem
    stage_mark = mem.mark()
    B = {}
    tri_f = cst_t[:, 128:256]
    ones_f = cst_t[:, 256:384]
    xin_name = T.get("xin_name", "xin")

    def buf(name):
        if name in PERSIST:
            return ctx.pb(T.get(name + "_buf", name))
        if name not in B:
            B[name] = Buf(name)
        return B[name]

    def act(out, in_, func, reads, writes, **kw):
        return S.op("scalar", lambda e: e.activation(out, in_, func, **kw), reads, writes)

    def dve(method, *args, reads=(), writes=(), **kw):
        return S.op("vector", lambda e: getattr(e, method)(*args, **kw), reads, writes)

    def pool(method, *args, reads=(), writes=(), **kw):
        return S.op("gpsimd", lambda e: getattr(e, method)(*args, **kw), reads, writes)

    def mm(out, lhsT, rhs, reads, writes, **kw):
        return S.op("tensor", lambda e: e.matmul(out, lhsT, rhs, **kw), reads, writes)

    def tp(out, in_, reads, writes):
        return S.op("tensor", lambda e: e.transpose(out, in_, idb[:]), list(reads) + [buf("idb")], writes)

    for name, t, src in (("gpre", gpre_t, gpre), ("abias", abias_t, abias), ("na", na_t, na),
                         ("nb", nb_t, nb), ("convw", convw_t, convw), ("lq", lq_t, lq)):
        S.dma("sync", t[:], src, writes=[buf(name)])
    S.dma("sync", bif_t[:, 0:4], bif, writes=[buf("bif")])

    si = [0]

    def load_w(c0, ncols):
        for kc in range(8):
            S.dma("gpsimd", wb[:, kc, 0:ncols], w1[kc * 128:(kc + 1) * 128, c0:c0 + ncols], writes=[buf("wb")])

    load_w(0, 1024)

    ri = [0]

    def rstd_from(ssk, k, n):
        act(rs[k][:, 0:1], ssk, AF.Sqrt, [buf("ss%d" % k)], [buf("rs%d" % k)], scale=1.0 / n, bias=EPS)
        dve("reciprocal", rs[k][:, 0:1], rs[k][:, 0:1], reads=[buf("rs%d" % k)], writes=[buf("rs%d" % k)])

    mem.release(stage_mark)
    xt = [ar.alloc([DM], F32) for _ in range(6)]
    junk = ar.alloc([DM], BF16)
    hb = [ar.alloc([DM], BF16) for _ in range(2)]
    kk = {}

    pos = {}

    def st0_l(t):
        pos[t] = len(pos)
        i = pos[t] % 6
        xb = buf("xt%d" % i)
        xsrc_ap, xsrc_buf = x(t)
        S.dma("sync", xt[i], xsrc_ap, reads=[buf(xsrc_buf)], writes=[xb])
        k = ri[0] % 4
        ri[0] += 1
        kk[t] = k
        act(junk, xt[i], AF.Square, [xb], [buf("junk"), buf("ss%d" % k)], accum_out=ss[k][:, 0:1])
        rstd_from(ss[k][:, 0:1], k, DM)

    def st0_m(t):
        i, j, k = pos[t] % 6, pos[t] % 2, kk[t]
        dve("scalar_tensor_tensor", hb[j], xt[i], rs[k][:, 0:1], gpre_t[:], ALU.mult, ALU.mult,
            reads=[buf("xt%d" % i), buf("rs%d" % k), buf("gpre")], writes=[buf("hb%d" % j)])

    def st0_n(t):
        j = pos[t] % 2
        pb = 6 + j
        pT = ps[:, pb, 0:512].bitcast(BF16)
        for c in range(8):
            tp(pT[:, c * 128:(c + 1) * 128], hb[j][:, c * 128:(c + 1) * 128], [buf("hb%d" % j)], [bank[pb]])
        if pos[t] % 2 == 0:
            act(hT[:, :, t * 128:(t + 1) * 128], pT.rearrange("p (c t) -> p c t", c=8), AF.Copy, [bank[pb]], [buf("hT")])
        else:
            dve("tensor_copy", hT[:, :, t * 128:(t + 1) * 128], pT.rearrange("p (c t) -> p c t", c=8),
                reads=[bank[pb]], writes=[buf("hT")])

    order = list(range(0, 12)) + list(range(16, 28)) + list(range(12, 16)) + list(range(28, 32))
    for m in range(32 + 2):
        if m < 32:
            st0_l(order[m])
        if 1 <= m <= 32:
            st0_m(order[m - 1])
        if m >= 2:
            st0_n(order[m - 2])
    S.barrier()

    mem.release(stage_mark)
    QT = [ar.alloc([SEQ], BF16) for _ in range(2)]
    KT = [ar.alloc([SEQ], BF16) for _ in range(2)]
    Vaug = ar.alloc([32, 2, 129], BF16)
    gz = ar.alloc([32, 256], BF16)
    PT = [ar.alloc([2, 512], BF16) for _ in range(3)]
    yTs = ar.alloc([SEQ], BF16)
    otmp = [ar.alloc([128], F32) for _ in range(4)]
    ot = [ar.alloc([128], F32) for _ in range(4)]
    Osb = [ar.alloc([258], F32) for _ in range(2)]
    ssn = [ar.alloc([8], F32) for _ in range(4)]
    rsn = [ar.alloc([8], F32) for _ in range(4)]
    yab = [ar.alloc([128], BF16) for _ in range(4)]
    zt = [ar.alloc([256], F32) for _ in range(2)]
    junkf = ar.alloc([128], F32)
    for i in range(2):
        pool("memset", QT[i], 0.0, writes=[buf("QT%d" % i)])
        pool("memset", KT[i], 0.0, writes=[buf("KT%d" % i)])
    pool("memset", Vaug, 1.0, writes=[buf("Vaug")])
    lp = zt[0]
    dve("tensor_tensor", lp[:, 0:64], lq_t[:, 0:64], lq_t[:, 64:128], ALU.mult, reads=[buf("lq")], writes=[buf("zt0")])
    dve("tensor_tensor", lp[:, 64:128], lq_t[:, 128:192], lq_t[:, 192:256], ALU.mult, reads=[buf("lq")], writes=[buf("zt0")])
    dve("reduce_sum", sm[:, 1:2], lp[:, 0:64], AX.X, reads=[buf("zt0")], writes=[buf("sm1")])
    dve("reduce_sum", sm[:, 2:3], lp[:, 64:128], AX.X, reads=[buf("zt0")], writes=[buf("sm2")])
    act(sm[:, 1:2], sm[:, 1:2], AF.Exp, [buf("sm1")], [buf("sm1")])
    act(sm[:, 2:3], sm[:, 2:3], AF.Exp, [buf("sm2")], [buf("sm2")])
    dve("tensor_tensor", sm[:, 0:1], sm[:, 2:3], sm[:, 1:2], ALU.subtract, reads=[buf("sm1"), buf("sm2")], writes=[buf("neglam")])
    dve("tensor_scalar", sm[:, 0:1], sm[:, 0:1], -lam_init, None, ALU.add, reads=[buf("neglam")], writes=[buf("neglam")])
    neglam = sm[:, 0:1]

    pr = [0]

    def next_bank():
        b = pr[0] % 6
        pr[0] += 1
        return b

    for t in range(32):
        pb = next_bank()
        for kc in range(8):
            mm(ps[:, pb, :], hT[:, kc, t * 128:(t + 1) * 128], wb[:, kc, 512:1024], [buf("hT"), buf("wb")], [bank[pb]],
               start=(kc == 0), stop=(kc == 7))
        dve("tensor_copy", Vaug[:, t, :, 0:128], ps[:, pb, 0:256].rearrange("p (h e) -> p h e", h=2),
            reads=[bank[pb]], writes=[buf("Vaug")])
        zi = t % 2
        act(zt[zi], ps[:, pb, 256:512], AF.Silu, [bank[pb]], [buf("zt%d" % zi)])
        dve("scalar_tensor_tensor", gz[:, t, :], zt[zi], 1.0 - lam_init, na_t[:], ALU.mult, ALU.mult,
            reads=[buf("zt%d" % zi), buf("na")], writes=[buf("gz")])

    gcnt = [0]
    for hl in range(2):
        S.dma("sync", QT[0][64:68, :], aug[hl, 0], writes=[buf("QT0")])
        S.dma("sync", QT[1][0:4, :], aug[hl, 0], writes=[buf("QT1")])
        S.dma("sync", KT[0][64:68, :], aug[hl, 1], writes=[buf("KT0")])
        S.dma("sync", KT[1][0:4, :], aug[hl, 1], writes=[buf("KT1")])
        for blk in range(8):
            for qk in range(2):
                c0 = qk * 256 + hl * 128
                dst = QT if qk == 0 else KT
                nm = "QT" if qk == 0 else "KT"
                pb = next_bank()
                for kc in range(8):
                    mm(ps[:, pb, :], wb[:, kc, c0:c0 + 128], hT[:, kc, blk * 512:(blk + 1) * 512],
                       [buf("wb"), buf("hT")], [bank[pb]], start=(kc == 0), stop=(kc == 7))
                act(dst[0][0:64, blk * 512:(blk + 1) * 512], ps[0:64, pb, :], AF.Copy, [bank[pb]], [buf(nm + "0")])
                dve("tensor_copy", dst[1][64:128, blk * 512:(blk + 1) * 512], ps[64:128, pb, :],
                    reads=[bank[pb]], writes=[buf(nm + "1")])
        if hl == 1:
            load_w(1024, 1284)
        jobs = []
        for I in range(8):
            for c in range(2):
                groups = []
                for J in range(I):
                    for hf in range(2):
                        groups.append((False, I - J, [4 * J + 2 * hf, 4 * J + 2 * hf + 1]))
                for hf in range(2):
                    groups.append((True, 0, [4 * I + 2 * hf, 4 * I + 2 * hf + 1]))
                for gx, (diag, d, tiles) in enumerate(groups):
                    jobs.append((I, c, diag, d, tiles, gx == 0, gx == len(groups) - 1))
        NJ = len(jobs)
        pending_tp = []

        def rec_st(n):
            I, c, diag, d, tiles, first, last = jobs[n]
            gi = n % 3
            b0 = 2 * gi
            for jj, j in enumerate(tiles):
                n0 = 128 * (j - 4 * I) if diag else 0
                mm(ps[:, b0 + jj, n0:512], KT[c][:, j * 128:(j + 1) * 128], QT[c][:, I * 512 + n0:(I + 1) * 512],
                   [buf("KT%d" % c), buf("QT%d" % c)], [bank[b0 + jj]], start=True, stop=True)

        def rec_exp(n):
            I, c, diag, d, tiles, first, last = jobs[n]
            gi = n % 3
            b0 = 2 * gi
            ptbs = [buf("PT%d_%d" % (gi, jj)) for jj in range(2)]
            bcol = abias_t[:, hl * 8 + d:hl * 8 + d + 1]
            if not diag:
                act(PT[gi], ps[:, b0:b0 + 2, :], AF.Exp, [bank[b0], bank[b0 + 1], buf("abias")], ptbs,
                    scale=0.125, bias=bcol)
            else:
                for jj, j in enumerate(tiles):
                    n0 = 128 * (j - 4 * I)
                    act(PT[gi][:, jj, n0:512], ps[:, b0 + jj, n0:512], AF.Exp, [bank[b0 + jj], buf("abias")], [ptbs[jj]],
                        scale=0.125, bias=bcol)
                    dve("tensor_tensor", PT[gi][:, jj, n0:n0 + 128], PT[gi][:, jj, n0:n0 + 128], trib[:], ALU.mult,
                        reads=[ptbs[jj], buf("trib")], writes=[ptbs[jj]])

        def rec_pv(n, m):
            I, c, diag, d, tiles, first, last = jobs[n]
            gi = n % 3
            b0 = 2 * gi
            for jj, j in enumerate(tiles):
                ptb = buf("PT%d_%d" % (gi, jj))
                r = j - 4 * I
                us = range(r, 4) if diag else range(4)
                for u in us:
                    ob_ = 6 + u // 2
                    oc = (u % 2) * 129
                    mm(ps[:, ob_, oc:oc + 129], PT[gi][:, jj, u * 128:(u + 1) * 128], Vaug[:, j, hl, :],
                       [ptb, buf("Vaug")], [bank[ob_]],
                       start=(first and jj == 0 and u in (0, 2)), stop=(diag and r == u), skip_group_check=True)
            if not last:
                return
            for bb in range(2):
                dve("tensor_copy", Osb[bb], ps[:, 6 + bb, 0:258], reads=[bank[6 + bb]], writes=[buf("Osb%d" % bb)])
            for u in range(4):
                bb = u // 2
                oc = (u % 2) * 129
                O = Osb[bb][:, oc:oc + 128]
                Osum = Osb[bb][:, oc + 128:oc + 129]
                osb = buf("Osb%d" % bb)
                k = ri[0] % 4
                ri[0] += 1
                dve("reciprocal", rs[k][:, 0:1], Osum, reads=[osb], writes=[buf("rs%d" % k)])
                if c == 0:
                    dve("tensor_scalar", otmp[u], O, rs[k][:, 0:1], None, ALU.mult,
                        reads=[osb, buf("rs%d" % k)], writes=[buf("otmp%d" % u)])
                else:
                    dve("tensor_tensor", rs[k][:, 0:1], rs[k][:, 0:1], neglam, ALU.mult,
                        reads=[buf("rs%d" % k), buf("neglam")], writes=[buf("rs%d" % k)])
                    dve("scalar_tensor_tensor", ot[u], O, rs[k][:, 0:1], otmp[u], ALU.mult, ALU.add,
                        reads=[osb, buf("rs%d" % k), buf("otmp%d" % u)], writes=[buf("ot%d" % u)])
            if c == 1:
                for u in range(4):
                    k2 = u
                    dve("scalar_tensor_tensor", junkf, ot[u], 1.0, ot[u], ALU.mult, ALU.mult,
                        reads=[buf("ot%d" % u)], writes=[buf("junkf"), buf("ssn%d" % k2)], accum_out=ssn[k2][:, 0:1])
                for u in range(4):
                    k2 = u
                    act(rsn[k2][:, 0:1], ssn[k2][:, 0:1], AF.Ln, [buf("ssn%d" % k2)], [buf("rsn%d" % k2)], scale=1.0 / 128, bias=EPS)
                    act(rsn[k2][:, 0:1], rsn[k2][:, 0:1], AF.Exp, [buf("rsn%d" % k2)], [buf("rsn%d" % k2)], scale=-0.5)
                for u in range(4):
                    k2 = u
                    tt = 4 * I + u
                    dve("scalar_tensor_tensor", yab[u], ot[u], rsn[k2][:, 0:1], gz[:, tt, hl * 128:(hl + 1) * 128],
                        ALU.mult, ALU.mult, reads=[buf("ot%d" % u), buf("rsn%d" % k2), buf("gz")], writes=[buf("yab%d" % u)])
                pending_tp.append((m + 2, I, b0))

        def flush_tp(m, force=False):
            while pending_tp and (force or pending_tp[0][0] <= m):
                _, I_, b0 = pending_tp.pop(0)
                pT4 = ps[:, b0, 0:256].bitcast(BF16)
                for u in range(4):
                    tp(pT4[:, u * 128:(u + 1) * 128], yab[u], [buf("yab%d" % u)], [bank[b0]])
                act(yTs[:, 4 * I_ * 128:(4 * I_ + 4) * 128], pT4, AF.Copy, [bank[b0]], [buf("yTs")])

        for m in range(NJ + 2):
            if m < NJ:
                rec_st(m)
            if 1 <= m <= NJ:
                rec_exp(m - 1)
            if m >= 2:
                rec_pv(m - 2, m)
            flush_tp(m)
        flush_tp(0, force=True)
        S.dma("gpsimd", yTo[0][hl * 128:(hl + 1) * 128, :], yTs, reads=[buf("yTs")], writes=[buf("yA")])
    if T.get("after_A") is not None:
        T["after_A"]()
    S.barrier()

    mem.release(stage_mark)
    qbT = [ar.alloc([SEQ], BF16) for _ in range(2)]
    kbT = [ar.alloc([SEQ], BF16) for _ in range(2)]
    vbaug = ar.alloc([32, 2, 129], BF16)
    sgo = ar.alloc([32, 256], BF16)
    gzb = ar.alloc([32, 256], BF16)
    yTb = [ar.alloc([1024], BF16) for _ in range(2)]
    xpre = [ar.alloc([515], F32) for _ in range(2)]
    ct = [ar.alloc([512], F32) for _ in range(4)]
    zt = [ar.alloc([256], F32) for _ in range(2)]
    gts = ar.alloc([32, 4], F32)
    IG = ar.alloc([2, 32], F32)
    LF = ar.alloc([2, 32], F32)
    EB = ar.alloc([64], F32)
    DEC = ar.alloc([64], F32)
    EK = ar.alloc([64], F32)
    SmT = [ar.alloc([128], BF16) for _ in range(4)]
    Ktm = [ar.alloc([128], BF16) for _ in range(4)]
    G = [ar.alloc([129], F32) for _ in range(2)]
    Cbf = [ar.alloc([129], BF16) for _ in range(2)]
    hbt = [ar.alloc([128], F32) for _ in range(2)]
    ybb = [ar.alloc([128], BF16) for _ in range(4)]
    dn = [ar.alloc([4], F32) for _ in range(2)]
    junkf3 = [ar.alloc([128], BF16) for _ in range(2)]
    ytmp = [ar.alloc([128], F32) for _ in range(2)]
    pool("memset", vbaug, 1.0, writes=[buf("vbaug")])
    for t in range(32):
        pb = next_bank()
        for kc in range(8):
            mm(ps[:, pb, :], hT[:, kc, t * 128:(t + 1) * 128], wb[:, kc, 512:1024], [buf("hT"), buf("wb")], [bank[pb]],
               start=(kc == 0), stop=(kc == 7))
        dve("tensor_copy", vbaug[:, t, :, 0:128], ps[:, pb, 0:256].rearrange("p (h e) -> p h e", h=2),
            reads=[bank[pb]], writes=[buf("vbaug")])
        act(sgo[:, t, :], ps[:, pb, 256:512], AF.Sigmoid, [bank[pb]], [buf("sgo")])
    for t in range(32):
        pb = next_bank()
        for kc in range(8):
            mm(ps[:, pb, 0:260], hT[:, kc, t * 128:(t + 1) * 128], wb[:, kc, 1024:1284], [buf("hT"), buf("wb")], [bank[pb]],
               start=(kc == 0), stop=(kc == 7))
        zi = t % 2
        act(zt[zi], ps[:, pb, 0:256], AF.Silu, [bank[pb]], [buf("zt%d" % zi)])
        dve("tensor_tensor", gzb[:, t, :], zt[zi], nb_t[:], ALU.mult, reads=[buf("zt%d" % zi), buf("nb")], writes=[buf("gzb")])
        dve("tensor_copy", gts[:, t, :], ps[:, pb, 256:260], reads=[bank[pb]], writes=[buf("gts")])
    for h in range(2):
        dve("tensor_scalar", IG[:, h, :], gts[:, :, h], bif_t[:, h:h + 1], None, ALU.add,
            reads=[buf("gts"), buf("bif")], writes=[buf("IG")])
        dve("tensor_scalar", LF[:, h, :], gts[:, :, 2 + h], bif_t[:, 2 + h:3 + h], None, ALU.add,
            reads=[buf("gts"), buf("bif")], writes=[buf("LF")])
    LFf = LF.rearrange("p a b -> p (a b)")
    IGf = IG.rearrange("p a b -> p (a b)")
    act(LFf, LFf, AF.Exp, [buf("LF")], [buf("LF")], scale=-1.0)
    act(LFf, LFf, AF.Ln, [buf("LF")], [buf("LF")], bias=1.0)
    dve("tensor_scalar", LFf, LFf, -1.0, None, ALU.mult, reads=[buf("LF")], writes=[buf("LF")])
    mm(ps[:, 6, 0:64], tri_f, LFf, [buf("cst"), buf("LF")], [bank[6]], start=True, stop=True)
    mm(ps[:, 7, 0:64], ones_f, LFf, [buf("cst"), buf("LF")], [bank[7]], start=True, stop=True)
    act(EB, ps[:, 6, 0:64], AF.Exp, [bank[6]], [buf("EB")])
    act(DEC, ps[:, 7, 0:64], AF.Exp, [bank[7]], [buf("DEC")])
    dve("tensor_tensor", EK, IGf, ps[:, 6, 0:64], ALU.subtract, reads=[buf("IG"), bank[6]], writes=[buf("EK")])
    act(EK, EK, AF.Exp, [buf("EK")], [buf("EK")], bias=math.log(128.0 ** -0.5))
    blks = [(grp, blk) for grp in range(4) for blk in range(8)]

    def conv_a(n):
        grp, blk = blks[n]
        pb = next_bank()
        for kc in range(8):
            mm(ps[:, pb, :], wb[:, kc, grp * 128:(grp + 1) * 128], hT[:, kc, blk * 512:(blk + 1) * 512],
               [buf("wb"), buf("hT")], [bank[pb]], start=(kc == 0), stop=(kc == 7))
        i = n % 2
        xp = xpre[i]
        xpb = buf("xpre%d" % i)
        if blk == 0:
            pool("memset", xp[:, 0:3], 0.0, writes=[xpb])
        else:
            pool("tensor_copy", xp[:, 0:3], xpre[1 - i][:, 512:515], reads=[buf("xpre%d" % (1 - i))], writes=[xpb])
        act(xp[:, 3:515], ps[:, pb, :], AF.Copy, [bank[pb]], [xpb])

    def conv_b(n):
        grp, blk = blks[n]
        dst = (qbT if grp < 2 else kbT)[grp % 2]
        dnm = ("qbT" if grp < 2 else "kbT") + str(grp % 2)
        i = n % 2
        xp = xpre[i]
        xpb = buf("xpre%d" % i)
        c0_, c1_ = ct[2 * i], ct[2 * i + 1]
        b0_, b1_ = buf("ct%d" % (2 * i)), buf("ct%d" % (2 * i + 1))
        cw = convw_t[:, grp * 4:(grp + 1) * 4]
        dve("tensor_scalar", c0_, xp[:, 0:512], cw[:, 0:1], None, ALU.mult, reads=[xpb, buf("convw")], writes=[b0_])
        dve("scalar_tensor_tensor", c1_, xp[:, 1:513], cw[:, 1:2], c0_, ALU.mult, ALU.add,
            reads=[xpb, buf("convw"), b0_], writes=[b1_])
        dve("scalar_tensor_tensor", c0_, xp[:, 2:514], cw[:, 2:3], c1_, ALU.mult, ALU.add,
            reads=[xpb, buf("convw"), b1_], writes=[b0_])
        dve("scalar_tensor_tensor", c1_, xp[:, 3:515], cw[:, 3:4], c0_, ALU.mult, ALU.add,
            reads=[xpb, buf("convw"), b0_], writes=[b1_])
        act(dst[:, blk * 512:(blk + 1) * 512], c1_, AF.Silu, [b1_], [buf(dnm)])

    for n in range(len(blks) + 1):
        if n < len(blks):
            conv_a(n)
        if n >= 1:
            conv_b(n - 1)
    if T.get("prefetch") is not None:
        T["prefetch"]([buf("hT"), buf("wb")])
    def ph_x(c, h):
        col = h * 32 + c
        i4 = 2 * (c % 2) + h
        cs = slice(c * 128, (c + 1) * 128)
        qb_, kb_ = buf("qbT%d" % h), buf("kbT%d" % h)
        pTk = ps[:, h, 128:192].bitcast(BF16)
        return [
            lambda: mm(ps[:, h, 0:128], kbT[h][:, cs], qbT[h][:, cs], [qb_, kb_], [bank[h]], start=True, stop=True),
            lambda: tp(pTk, kbT[h][:, cs], [kb_], [bank[h]]),
            lambda: dve("scalar_tensor_tensor", SmT[i4], ps[:, h, 0:128], EK[:, col:col + 1], tri_f, ALU.mult, ALU.mult,
                        reads=[bank[h], buf("EK"), buf("cst")], writes=[buf("SmT%d" % i4)]),
            lambda: act(Ktm[i4], pTk, AF.Copy, [bank[h], buf("EK")], [buf("Ktm%d" % i4)], scale=EK[:, col:col + 1]),
        ]

    def ph_y(c, h):
        col = h * 32 + c
        i4 = 2 * (c % 2) + h
        hb_ = 2 + 2 * (c % 2) + h
        ub_ = 6 + h
        cs = slice(c * 128, (c + 1) * 128)
        qb_ = buf("qbT%d" % h)
        ops = [lambda: mm(ps[:, hb_, 0:129], SmT[i4], vbaug[:, c, h, :], [buf("SmT%d" % i4), buf("vbaug")], [bank[hb_]],
                          start=True, stop=(c == 0))]
        if c > 0:
            ops.append(lambda: mm(ps[:, hb_, 0:129], qbT[h][:, cs], Cbf[h], [qb_, buf("Cbf%d" % h)], [bank[hb_]],
                                  start=False, stop=True))
        ops.append(lambda: mm(ps[:, ub_, 0:129], Ktm[i4], vbaug[:, c, h, :], [buf("Ktm%d" % i4), buf("vbaug")], [bank[ub_]],
                              start=True, stop=True))
        if c == 0:
            ops.append(lambda: dve("tensor_copy", G[h], ps[:, ub_, 0:129], reads=[bank[ub_]], writes=[buf("G%d" % h)]))
        else:
            ops.append(lambda: dve("scalar_tensor_tensor", G[h], G[h], DEC[:, col - 1:col], ps[:, ub_, 0:129], ALU.mult, ALU.add,
                                   reads=[buf("G%d" % h), buf("DEC"), bank[ub_]], writes=[buf("G%d" % h)]))
        if c < 31:
            ops.append(lambda: act(Cbf[h], G[h], AF.Copy, [buf("G%d" % h), buf("DEC")], [buf("Cbf%d" % h)],
                                   scale=DEC[:, col:col + 1]))
        return ops

    def ph_z1(c, h):
        col = h * 32 + c
        i4 = 2 * (c % 2) + h
        hb_ = 2 + 2 * (c % 2) + h
        dnb = buf("dn%d" % h)
        d0, d1, d2 = dn[h][:, 0:1], dn[h][:, 1:2], dn[h][:, 2:3]
        k = ri[0] % 4
        ri[0] += 1
        ssb, rsb = buf("ss%d" % k), buf("rs%d" % k)
        return [
            lambda: dve("tensor_tensor", d0, ps[:, hb_, 128:129], EB[:, col:col + 1], ALU.mult,
                        reads=[bank[hb_], buf("EB")], writes=[dnb]),
            lambda: dve("tensor_scalar", d1, d0, -1.0, 1.0, ALU.mult, ALU.max, reads=[dnb], writes=[dnb]),
            lambda: dve("tensor_tensor", d0, d0, d1, ALU.max, reads=[dnb], writes=[dnb]),
            lambda: dve("reciprocal", d1, d0, reads=[dnb], writes=[dnb]),
            lambda: dve("tensor_tensor", d2, d1, EB[:, col:col + 1], ALU.mult, reads=[dnb, buf("EB")], writes=[dnb]),
            lambda: dve("scalar_tensor_tensor", hbt[h], ps[:, hb_, 0:128], d2, sgo[:, c, h * 128:(h + 1) * 128],
                        ALU.mult, ALU.mult, reads=[bank[hb_], dnb, buf("sgo")], writes=[buf("hbt%d" % h)]),
            lambda: act(junkf3[h], hbt[h], AF.Square, [buf("hbt%d" % h)], [buf("junkf3_%d" % h), ssb], accum_out=ss[k][:, 0:1]),
            lambda: act(rs[k][:, 0:1], ss[k][:, 0:1], AF.Ln, [ssb], [rsb], scale=1.0 / 128, bias=EPS),
            lambda: act(rs[k][:, 0:1], rs[k][:, 0:1], AF.Exp, [rsb], [rsb], scale=-0.5),
            lambda: pool("tensor_scalar", ytmp[h], hbt[h], rs[k][:, 0:1], 1.0, ALU.mult, ALU.mult,
                         reads=[buf("hbt%d" % h), rsb], writes=[buf("ytmp%d" % h)]),
            lambda: pool("tensor_tensor", ybb[i4], ytmp[h], gzb[:, c, h * 128:(h + 1) * 128], ALU.mult,
                         reads=[buf("ytmp%d" % h), buf("gzb")], writes=[buf("ybb%d" % i4)]),
        ]

    def ph_z2(c, h):
        i4 = 2 * (c % 2) + h
        pTy = ps[:, h, 192:256].bitcast(BF16)
        c8 = c % 8
        ops = [lambda: tp(pTy, ybb[i4], [buf("ybb%d" % i4)], [bank[h]]),
               lambda: act(yTb[h][:, c8 * 128:(c8 + 1) * 128], pTy, AF.Copy, [bank[h]], [buf("yTb%d" % h)])]
        if c8 == 7:
            ops.append(lambda: S.dma("gpsimd", yTo[1][h * 128:(h + 1) * 128, (c - 7) * 128:(c + 1) * 128], yTb[h],
                                     reads=[buf("yTb%d" % h)], writes=[buf("yB")]))
        return ops

    def interleave(fn, c):
        a_, b_ = fn(c, 0), fn(c, 1)
        for j in range(max(len(a_), len(b_))):
            if j < len(a_):
                a_[j]()
            if j < len(b_):
                b_[j]()

    for q in range(32 + 3):
        if q < 32:
            interleave(ph_x, q)
        if 1 <= q <= 32:
            interleave(ph_y, q - 1)
        if q >= 3:
            interleave(ph_z2, q - 3)
        if 2 <= q <= 33:
            interleave(ph_z1, q - 2)
    S.barrier()


_SLOPES = [2.0 ** (-8.0 * (h + 1) / 4) for h in range(4)]
_CST = None


def _p1_inputs(inp, l, g, x_full):
    w_in = inp["w_in"][l]
    sl = slice(g * 256, (g + 1) * 256)
    parts = [w_in[:, 0:512][:, sl], w_in[:, 512:1024][:, sl], w_in[:, 1024:1536][:, sl], w_in[:, 1536:2048][:, sl],
             w_in[:, 2048:2560][:, sl], w_in[:, 2560:3072][:, sl], w_in[:, 3072:3584][:, sl], w_in[:, 3584:4096][:, sl],
             w_in[:, 4096:4608][:, sl], w_in[:, 4608 + 2 * g:4610 + 2 * g], w_in[:, 4612 + 2 * g:4614 + 2 * g]]
    w1 = np.ascontiguousarray(np.concatenate(parts, axis=1), dtype=np.float32)
    assert w1.shape == (DM, W1_COLS)
    v = np.arange(SEQ) % 512
    hi = ((v // 16) * 16).astype(np.float32)
    lo = (v % 16).astype(np.float32)
    one = np.ones(SEQ, np.float32)
    aug = np.zeros((2, 2, 4, SEQ), np.float32)
    abias = np.zeros((128, 16), np.float32)
    for hl in range(2):
        s = _SLOPES[2 * g + hl]
        aug[hl, 0] = np.stack([-s * 8.0 * hi, -s * 8.0 * lo, one, one])
        aug[hl, 1] = np.stack([one, one, s * 8.0 * hi, s * 8.0 * lo])
        for d in range(8):
            abias[:, hl * 8 + d] = -s * 512.0 * d
    cq = inp["conv_qk"][l]
    convw = np.zeros((128, 16), np.float32)
    for grp in range(4):
        ch0 = (0 if grp < 2 else 512) + (2 * g + grp % 2) * 128
        convw[:, grp * 4:(grp + 1) * 4] = cq[:, ch0:ch0 + 128].T
    bi = inp["b_if"][l]
    bif = np.broadcast_to(np.array([bi[2 * g], bi[2 * g + 1], bi[4 + 2 * g], bi[5 + 2 * g]], np.float32), (128, 4))
    return {
        "w1": w1,
        "gpre": np.ascontiguousarray(np.broadcast_to(inp["norm_pre"][l], (128, DM))),
        "cst": _consts_f32(),
        "aug": aug.astype(ml_dtypes.bfloat16),
        "abias": abias,
        "na": np.ascontiguousarray(np.broadcast_to(inp["norm_a"][l][sl], (128, 256))),
        "nb": np.ascontiguousarray(np.broadcast_to(inp["norm_b"][l][sl], (128, 256))),
        "convw": convw,
        "bif": np.ascontiguousarray(bif),
        "lq": np.ascontiguousarray(np.broadcast_to(inp["lambda_qk"][l].reshape(256), (128, 256))),
    }


def _host_maps(inp):
    x = np.asarray(inp["x"], dtype=np.float32)
    maps = []
    for c in range(8):
        b, g = c // 2, c % 2
        m = {"x": np.ascontiguousarray(x[b]), "xown": np.ascontiguousarray(x[b, g * HALF:(g + 1) * HALF]),
             "cst": _consts_f32()}
        selv = np.zeros((128, 2), np.float32)
        selv[:, g] = 1.0
        m["sel"] = selv
        for l in range(2):
            p1 = _p1_inputs(inp, l, g, None)
            for k in ("w1", "gpre", "na", "nb", "convw", "bif", "lq"):
                m["%s_%d" % (k, l)] = p1[k]
            if l == 0:
                m["aug"] = p1["aug"]
                m["abias"] = p1["abias"]
            w_in = inp["w_in"][l]
            m["w2_%d" % l] = np.ascontiguousarray(w_in[:, 4616:6664], dtype=np.float32)
            m["wab_%d" % l] = np.ascontiguousarray(np.concatenate([inp["w_a"][l], inp["w_b"][l]], axis=0), dtype=np.float32)
            m["wout_%d" % l] = np.ascontiguousarray(inp["w_out"][l], dtype=np.float32)
            m["gpost_%d" % l] = np.ascontiguousarray(np.broadcast_to(inp["norm_post"][l], (128, DM)), dtype=np.float32)
        maps.append(m)
    return maps


PAIRS = [[0, 1], [2, 3], [4, 5], [6, 7]]


def build_fused():
    nc = bass.Bass("TRN2", target_bir_lowering=False)

    def ext(name, shape, dt=F32):
        return nc.dram_tensor(name, shape, dt, kind="ExternalInput").ap()

    x = ext("x", [SEQ, DM])
    xown = ext("xown", [HALF, DM])
    cst = ext("cst", [128, 384])
    sel = ext("sel", [128, 2])
    aug = ext("aug", [2, 2, 4, SEQ], BF16)
    abias = ext("abias", [128, 16])
    L = []
    for l in range(2):
        L.append({
            "w1": ext("w1_%d" % l, [DM, W1_COLS]), "gpre": ext("gpre_%d" % l, [128, DM]),
            "na": ext("na_%d" % l, [128, 256]), "nb": ext("nb_%d" % l, [128, 256]),
            "convw": ext("convw_%d" % l, [128, 16]), "bif": ext("bif_%d" % l, [128, 4]),
            "lq": ext("lq_%d" % l, [128, 256]), "w2": ext("w2_%d" % l, [DM, 2048]),
            "wab": ext("wab_%d" % l, [1024, DM]), "wout": ext("wout_%d" % l, [DM, DM]),
            "gpost": ext("gpost_%d" % l, [128, DM]), "aug": aug, "abias": abias, "sel": sel,
        })
    out = nc.dram_tensor("out", [HALF, DM], F32, kind="ExternalOutput").ap()
    ysend = [nc.dram_tensor("ysend%d" % k, [256, SEQ], BF16) for k in range(2)]
    yrecv = [nc.dram_tensor("yrecv%d" % k, [512, SEQ], BF16) for k in range(2)]
    xnew = [nc.dram_tensor("xnew%d" % k, [512, DM], F32) for k in range(4)]
    xg = [nc.dram_tensor("xg%d" % k, [1024, DM], F32) for k in range(4)]
    ctx = Ctx(nc, cst)
    S = ctx.S

    def gather(src, dst, sbuf, dbuf):
        S.op("gpsimd", lambda e: e.collective_compute("AllGather", ALU.bypass, replica_groups=PAIRS,
                                                       ins=[src.ap().opt()], outs=[dst.ap().opt()]),
             [ctx.pb(sbuf)], [ctx.pb(dbuf)], dma=True, cc=True)

    for l in range(2):
        T1 = dict(L[l])
        if l == 0:
            T1["x"] = lambda t: (x[t * 128:(t + 1) * 128, :], "xin")
        else:
            def xsrc1(t):
                hh, tt = t // 16, t % 16
                k, r = tt // 4, tt % 4
                return xg[k].ap()[hh * 512 + r * 128:hh * 512 + (r + 1) * 128, :], "xin%d" % k
            T1["x"] = xsrc1
            for k in range(4):
                T1["xin%d_buf" % k] = "xg%d" % k
        T1.update({"yTo": [ysend[0].ap(), ysend[1].ap()], "yA_buf": "ysend0", "yB_buf": "ysend1",
                   "after_A": lambda: gather(ysend[0], yrecv[0], "ysend0", "yrecv0")})
        def prefetch(dead_bufs, l=l):
            tmp = Arena(ctx.mem.t, ctx.mem.n)
            w2b = tmp.alloc([8, 2048], BF16)
            wabb = tmp.alloc([8, 1024], BF16)
            woutb = tmp.alloc([8, 1024], BF16)
            p2_weight_dmas(S, w2b, wabb, woutb, L[l]["w2"], L[l]["wab"], L[l]["wout"], dead_bufs)
        T1["prefetch"] = prefetch
        emit_p1(ctx, l, T1)
        gather(ysend[1], yrecv[1], "ysend1", "yrecv1")
        T2 = dict(L[l])
        T2.update({"ysrc": [yrecv[0].ap(), yrecv[1].ap()], "ysrcA_buf": "yrecv0", "ysrcB_buf": "yrecv1",
                   "w_prefetched": True})
        if l == 0:
            T2["x"] = lambda t: (xown[t * 128:(t + 1) * 128, :], "xin")
            T2["xo"] = lambda t: (xnew[t // 4].ap()[(t % 4) * 128:(t % 4 + 1) * 128, :], "xo%d" % (t // 4))
            for k in range(4):
                T2["xo%d_buf" % k] = "xnew%d" % k
            T2["after_blk"] = lambda blk: gather(xnew[blk], xg[blk], "xnew%d" % blk, "xg%d" % blk)
        else:
            T2["x"] = lambda t: (xnew[t // 4].ap()[(t % 4) * 128:(t % 4 + 1) * 128, :], "xin%d" % (t // 4))
            for k in range(4):
                T2["xin%d_buf" % k] = "xnew%d" % k
            T2["xo"] = lambda t: (out[t * 128:(t + 1) * 128, :], "xo0")
            T2["xo0_buf"] = "out"
        emit_p2(ctx, T2)
    S.emit()
    return nc


def kernel(**inputs):
    inp = {k: np.asarray(v) for k, v in inputs.items()}
    maps = _host_maps(inp)
    nc = build_fused()
    res = run_bass_kernel_spmd(nc, maps, core_ids=list(range(8))).results
    out = np.empty((4, SEQ, DM), np.float32)
    for c in range(8):
        b, g = c // 2, c % 2
        out[b, g * HALF:(g + 1) * HALF] = res[c]["out"]
    return out
```
